# Optimizing a Trainium2 kernel written in Bass

```python
import math
import jax
import jax.numpy as jnp
from jax import lax
import numpy as np

D_MODEL = 1024
BATCH = 8
SEQ = 4096
DEPTH = 4

CTX_LEN = 256
GRID_W = 64
N_MIXERS = 3
N_MOD = 9
MACARON_WEIGHT = 0.5
D_FF = 256 * ((8 * D_MODEL // 3 + 255) // 256)
NORM_EPS = 1e-6
ROPE_THETA = 10000.0
Q_BLOCK = 128

DA_HEAD_DIM = 64
DA_HEADS = D_MODEL // (2 * DA_HEAD_DIM)
DA_WIDTH = DA_HEADS * 2 * DA_HEAD_DIM

HG_KEY = 128
HG_VAL = 128
HG_HEADS = D_MODEL // HG_VAL
HG_KW = HG_HEADS * HG_KEY
HG_VW = HG_HEADS * HG_VAL
HG_CHUNK = 64

GQA_HEAD_DIM = 128
GQA_Q_HEADS = D_MODEL // GQA_HEAD_DIM
GQA_KV_HEADS = GQA_Q_HEADS // 4
GQA_GROUP = GQA_Q_HEADS // GQA_KV_HEADS
GQA_Q_W = GQA_Q_HEADS * GQA_HEAD_DIM
GQA_KV_W = GQA_KV_HEADS * GQA_HEAD_DIM

N_DA = (DEPTH + N_MIXERS - 1) // N_MIXERS
N_HG = (DEPTH + N_MIXERS - 2) // N_MIXERS
N_GQA = (DEPTH + N_MIXERS - 3) // N_MIXERS

kernel_name = "hybrid_interleaved_dit_block"


def rmsnorm(x, g):
    xf = x.astype(jnp.float32)
    y = xf * lax.rsqrt(jnp.mean(xf * xf, axis=-1, keepdims=True) + NORM_EPS)
    return (y * g.astype(jnp.float32)).astype(x.dtype)


def modulate(x, g_pre, shift, scale):
    return rmsnorm(x, g_pre) * (1 + scale) + shift


def gated_residual(x, y, g_post, gate, weight):
    return x + weight * gate * rmsnorm(y, g_post)


def swiglu(h, w_in, w_out):
    gate, up = jnp.split(h @ w_in, 2, axis=-1)
    return (jax.nn.silu(gate) * up) @ w_out


def ffn_sublayer(x, shift, scale, gate, g_pre, g_post, w_in, w_out):
    y = swiglu(modulate(x, g_pre, shift, scale), w_in, w_out)
    return gated_residual(x, y, g_post, gate, MACARON_WEIGHT)


def axial_rope_tables(rows, head_dim):
    pairs = head_dim // 4
    inv_freq = jnp.power(ROPE_THETA, -jnp.arange(pairs, dtype=jnp.float32) / pairs)
    r = jnp.repeat(jnp.arange(rows, dtype=jnp.float32), GRID_W)
    col = jnp.tile(jnp.arange(GRID_W, dtype=jnp.float32), rows)
    ang = jnp.concatenate([r[:, None] * inv_freq, col[:, None] * inv_freq], axis=-1)
    return jnp.cos(ang), jnp.sin(ang)


def apply_rope(x, cos, sin):
    n, half = cos.shape
    shape = (1, n) + (1,) * (x.ndim - 3) + (half,)
    cs = cos.reshape(shape).astype(x.dtype)
    sn = sin.reshape(shape).astype(x.dtype)
    x1, x2 = x[..., 0::2], x[..., 1::2]
    return jnp.stack([x1 * cs - x2 * sn, x1 * sn + x2 * cs], axis=-1).reshape(x.shape)


def sweep_query_blocks(fn, q):
    bsz, n = q.shape[:2]
    qb = jnp.moveaxis(q.reshape((bsz, n // Q_BLOCK, Q_BLOCK) + q.shape[2:]), 1, 0)
    ob = lax.map(fn, qb)
    return jnp.moveaxis(ob, 0, 1).reshape((bsz, n) + ob.shape[3:])


def differential_attention(h_l, h_c, w_qkv, lam_p, subln, w_o, lam_init, cos, sin, need_ctx):
    bsz = h_l.shape[0]

    def project(h):
        n = h.shape[1]
        q, k, v = jnp.split(h @ w_qkv, 3, axis=-1)
        return (q.reshape(bsz, n, DA_HEADS, 2, DA_HEAD_DIM),
                k.reshape(bsz, n, DA_HEADS, 2, DA_HEAD_DIM),
                v.reshape(bsz, n, DA_HEADS, 2 * DA_HEAD_DIM))

    q_l, k_l, v_l = project(h_l)
    q_c, k_c, v_c = project(h_c)
    q_l = apply_rope(q_l, cos, sin)
    k_l = apply_rope(k_l, cos, sin)
    lp = lam_p.astype(jnp.float32)
    lam = jnp.exp(jnp.sum(lp[0] * lp[1])) - jnp.exp(jnp.sum(lp[2] * lp[3])) + lam_init
    scale = DA_HEAD_DIM ** -0.5

    def attend(q, k, v):
        s = jnp.einsum('bqhmd,bkhmd->bhmqk', q, k).astype(jnp.float32) * scale
        p = jax.nn.softmax(s, axis=-1)
        a = (p[:, :, 0] - lam * p[:, :, 1]).astype(v.dtype)
        return jnp.einsum('bhqk,bkhe->bqhe', a, v)

    def finish(o):
        o = rmsnorm(o, subln) * (1 - lam_init)
        return o.reshape(bsz, o.shape[1], DA_WIDTH) @ w_o

    k_all = jnp.concatenate([k_l, k_c], axis=1)
    v_all = jnp.concatenate([v_l, v_c], axis=1)
    y_l = finish(sweep_query_blocks(lambda qb: attend(qb, k_all, v_all), q_l))
    y_c = finish(attend(q_c, k_c, v_c)) if need_ctx else None
    return y_l, y_c


def gqa_attention(h_l, h_c, w_qkv, q_norm, k_norm, w_o, cos, sin, need_ctx):
    bsz = h_l.shape[0]

    def project(h):
        n = h.shape[1]
        q, k, v = jnp.split(h @ w_qkv, [GQA_Q_W, GQA_Q_W + GQA_KV_W], axis=-1)
        q = rmsnorm(q.reshape(bsz, n, GQA_KV_HEADS, GQA_GROUP, GQA_HEAD_DIM), q_norm)
        k = rmsnorm(k.reshape(bsz, n, GQA_KV_HEADS, GQA_HEAD_DIM), k_norm)
        return q, k, v.reshape(bsz, n, GQA_KV_HEADS, GQA_HEAD_DIM)

    q_l, k_l, v_l = project(h_l)
    q_c, k_c, v_c = project(h_c)
    q_l = apply_rope(q_l, cos, sin)
    k_l = apply_rope(k_l, cos, sin)
    scale = GQA_HEAD_DIM ** -0.5

    def attend(q, k, v):
        s = jnp.einsum('bqhgd,bkhd->bhgqk', q, k).astype(jnp.float32) * scale
        p = jax.nn.softmax(s, axis=-1).astype(v.dtype)
        return jnp.einsum('bhgqk,bkhd->bqhgd', p, v)

    def finish(o):
        return o.reshape(bsz, o.shape[1], GQA_Q_W) @ w_o

    k_all = jnp.concatenate([k_l, k_c], axis=1)
    v_all = jnp.concatenate([v_l, v_c], axis=1)
    y_l = finish(sweep_query_blocks(lambda qb: attend(qb, k_all, v_all), q_l))
    y_c = finish(attend(q_c, k_c, v_c)) if need_ctx else None
    return y_l, y_c


def chunkwise_gated_scan(q, k, v, log_f, s0):
    bsz, n, heads, _ = q.shape
    nc = n // HG_CHUNK

    def to_chunks(a):
        return jnp.moveaxis(a.reshape(bsz, nc, HG_CHUNK, heads, a.shape[-1]), 1, 0)

    lower_tri = jnp.tril(jnp.ones((HG_CHUNK, HG_CHUNK), dtype=bool))[None, :, :, None, None]

    def step(state, chunk):
        qc, kc, vc, gc = chunk
        G = jnp.cumsum(gc, axis=1)
        g_last = G[:, -1]
        o_inter = jnp.einsum('bthk,bhkv->bthv', qc * jnp.exp(G), state)
        rel = jnp.exp(jnp.where(lower_tri, G[:, :, None] - G[:, None, :], -jnp.inf))
        att = jnp.einsum('bthk,btshk,bshk->bhts', qc, rel, kc)
        o_intra = jnp.einsum('bhts,bshv->bthv', att, vc)
        state = (jnp.exp(g_last)[..., None] * state
                 + jnp.einsum('bshk,bshv->bhkv', kc * jnp.exp(g_last[:, None] - G), vc))
        return state, o_inter + o_intra

    s_fin, o = lax.scan(step, s0, (to_chunks(q), to_chunks(k), to_chunks(v), to_chunks(log_f)))
    return jnp.moveaxis(o, 0, 1).reshape(bsz, n, heads, v.shape[-1]), s_fin


def hgrn2_bidirectional(h_l, h_c, w_in, lb_fwd, lb_bwd, norm_g, w_o, need_ctx):
    bsz = h_l.shape[0]
    lbs = (lb_fwd.reshape(HG_HEADS, HG_KEY), lb_bwd.reshape(HG_HEADS, HG_KEY))

    def project(h):
        n = h.shape[1]
        q, f_fwd, f_bwd, i_in, gate = jnp.split(
            h @ w_in, [HG_KW, 2 * HG_KW, 3 * HG_KW, 3 * HG_KW + HG_VW], axis=-1)

        def heads(a, d):
            return a.reshape(bsz, n, HG_HEADS, d).astype(jnp.float32)

        q = jax.nn.silu(heads(q, HG_KEY))
        gates = []
        for f_logit, lb in zip((f_fwd, f_bwd), lbs):
            f = lb + (1 - lb) * jax.nn.sigmoid(heads(f_logit, HG_KEY))
            gates.append((1 - f, jnp.log(f)))
        return q, gates, heads(i_in, HG_VAL), gate

    q_l, gates_l, v_l, gate_l = project(h_l)
    q_c, gates_c, v_c, gate_c = project(h_c)
    (k_cf, lf_cf), (k_cb, lf_cb) = gates_c
    (k_lf, lf_lf), (k_lb, lf_lb) = gates_l

    def flip(a):
        return a[:, ::-1]

    s0 = jnp.zeros((bsz, HG_HEADS, HG_KEY, HG_VAL), jnp.float32)
    o_cf, s_cf = chunkwise_gated_scan(q_c, k_cf, v_c, lf_cf, s0)
    o_cb, s_cb = chunkwise_gated_scan(flip(q_c), flip(k_cb), flip(v_c), flip(lf_cb), s0)
    o_lf, _ = chunkwise_gated_scan(q_l, k_lf, v_l, lf_lf, s_cf)
    o_lb, _ = chunkwise_gated_scan(flip(q_l), flip(k_lb), flip(v_l), flip(lf_lb), s_cb)

    def finish(o, gate):
        n = o.shape[1]
        o = rmsnorm(o, norm_g).reshape(bsz, n, HG_VW).astype(gate.dtype) * jax.nn.silu(gate)
        return o @ w_o

    y_l = finish(o_lf + flip(o_lb), gate_l)
    y_c = finish(o_cf + flip(o_cb), gate_c) if need_ctx else None
    return y_l, y_c


def setup_inputs(seed: int = 0) -> dict:
    key = jax.random.key(seed)
    ks = jax.random.split(key, 24)

    def nrm(k, shape, scale):
        return scale * jax.random.normal(k, shape, jnp.float32)

    def gain(k, shape):
        return 1.0 + 0.02 * jax.random.normal(k, shape, jnp.float32)

    return {
        "x": nrm(ks[0], (BATCH, SEQ, D_MODEL), 1.0),
        "c": nrm(ks[1], (BATCH, D_MODEL), 1.0),
        "ctx": nrm(ks[2], (BATCH, CTX_LEN, D_MODEL), 1.0),
        "c_ctx": nrm(ks[3], (D_MODEL,), 1.0),
        "w_mod": nrm(ks[4], (DEPTH, D_MODEL, N_MOD * D_MODEL), 0.5 * D_MODEL ** -0.5),
        "b_mod": nrm(ks[5], (DEPTH, N_MOD * D_MODEL), 0.01),
        "norm_g": gain(ks[6], (DEPTH, 6, D_MODEL)),
        "ffn_w_in": nrm(ks[7], (DEPTH, 2, D_MODEL, 2 * D_FF), D_MODEL ** -0.5),
        "ffn_w_out": nrm(ks[8], (DEPTH, 2, D_FF, D_MODEL), D_FF ** -0.5),
        "da_w_qkv": nrm(ks[9], (N_DA, D_MODEL, 3 * DA_WIDTH), D_MODEL ** -0.5),
        "da_lambda": nrm(ks[10], (N_DA, 4, DA_HEAD_DIM), 0.1),
        "da_subln": gain(ks[11], (N_DA, 2 * DA_HEAD_DIM)),
        "da_w_o": nrm(ks[12], (N_DA, DA_WIDTH, D_MODEL), DA_WIDTH ** -0.5),
        "hg_w_in": nrm(ks[13], (N_HG, D_MODEL, 3 * HG_KW + 2 * HG_VW), D_MODEL ** -0.5),
        "hg_lower_bound": nrm(ks[14], (2, DEPTH, HG_KW), 0.5),
        "hg_norm": gain(ks[15], (N_HG, HG_VAL)),
        "hg_w_o": nrm(ks[16], (N_HG, HG_VW, D_MODEL), HG_VW ** -0.5),
        "gqa_w_qkv": nrm(ks[17], (N_GQA, D_MODEL, GQA_Q_W + 2 * GQA_KV_W), D_MODEL ** -0.5),
        "gqa_q_norm": gain(ks[18], (N_GQA, GQA_HEAD_DIM)),
        "gqa_k_norm": gain(ks[19], (N_GQA, GQA_HEAD_DIM)),
        "gqa_w_o": nrm(ks[20], (N_GQA, GQA_Q_W, D_MODEL), GQA_Q_W ** -0.5),
    }


def reference(x, c, ctx, c_ctx, w_mod, b_mod, norm_g, ffn_w_in, ffn_w_out,
              da_w_qkv, da_lambda, da_subln, da_w_o,
              hg_w_in, hg_lower_bound, hg_norm, hg_w_o,
              gqa_w_qkv, gqa_q_norm, gqa_k_norm, gqa_w_o):
    bsz, n_lat, _ = x.shape
    rows = n_lat // GRID_W
    da_cos, da_sin = axial_rope_tables(rows, DA_HEAD_DIM)
    gqa_cos, gqa_sin = axial_rope_tables(rows, GQA_HEAD_DIM)
    lb_table = jnp.cumsum(jax.nn.softmax(hg_lower_bound.astype(jnp.float32), axis=1), axis=1)
    lb_table = lb_table - lb_table[:, :1]
    silu_c = jax.nn.silu(c)
    silu_cc = jax.nn.silu(c_ctx)
    xl, xc = x, ctx
    for i in range(DEPTH):
        kind, j = i % N_MIXERS, i // N_MIXERS
        need_ctx = i < DEPTH - 1
        mod_l = (silu_c @ w_mod[i] + b_mod[i]).reshape(bsz, N_MOD, 1, D_MODEL)
        mod_c = (silu_cc @ w_mod[i] + b_mod[i]).reshape(N_MOD, D_MODEL)
        g = norm_g[i]
        xl = ffn_sublayer(xl, mod_l[:, 0], mod_l[:, 1], mod_l[:, 2], g[0], g[1],
                          ffn_w_in[i, 0], ffn_w_out[i, 0])
        xc = ffn_sublayer(xc, mod_c[0], mod_c[1], mod_c[2], g[0], g[1],
                          ffn_w_in[i, 0], ffn_w_out[i, 0])
        hl = modulate(xl, g[2], mod_l[:, 3], mod_l[:, 4])
        hc = modulate(xc, g[2], mod_c[3], mod_c[4])
        if kind == 0:
            lam_init = 0.8 - 0.6 * math.exp(-0.3 * i)
            yl, yc = differential_attention(hl, hc, da_w_qkv[j], da_lambda[j], da_subln[j],
                                            da_w_o[j], lam_init, da_cos, da_sin, need_ctx)
        elif kind == 1:
            yl, yc = hgrn2_bidirectional(hl, hc, hg_w_in[j], lb_table[0, i], lb_table[1, i],
                                         hg_norm[j], hg_w_o[j], need_ctx)
        else:
            yl, yc = gqa_attention(hl, hc, gqa_w_qkv[j], gqa_q_norm[j], gqa_k_norm[j],
                                   gqa_w_o[j], gqa_cos, gqa_sin, need_ctx)
        xl = gated_residual(xl, yl, g[3], mod_l[:, 5], 1.0)
        xl = ffn_sublayer(xl, mod_l[:, 6], mod_l[:, 7], mod_l[:, 8], g[4], g[5],
                          ffn_w_in[i, 1], ffn_w_out[i, 1])
        if need_ctx:
            xc = gated_residual(xc, yc, g[3], mod_c[5], 1.0)
            xc = ffn_sublayer(xc, mod_c[6], mod_c[7], mod_c[8], g[4], g[5],
                              ffn_w_in[i, 1], ffn_w_out[i, 1])
    return xl
```

```python
import contextlib
import math
import numpy as np
import concourse.bass as bass
import concourse.mybir as mybir
from concourse.bass_utils import run_bass_kernel_spmd

F32 = mybir.dt.float32
BF16 = mybir.dt.bfloat16
AF = mybir.ActivationFunctionType
ALU = mybir.AluOpType

D = 1024
TL = 4096
TC = 256
T = TL + TC
DEPTH = 4
DFF = 2816
NJ = DFF // 128
EPS = 1e-6
GRID_W = 64
import os
DBG = int(os.environ.get('KDBG', '0'))


class Buf:
    __slots__ = ("name", "w", "r", "excl")

    def __init__(self, name="", excl=False):
        self.name = name
        self.w = None
        self.r = []
        self.excl = excl


class Sched:
    ENGS = ("pe", "act", "dve", "pool", "sp")

    def __init__(self, nc):
        self.nc = nc
        self.prog = {e: [] for e in self.ENGS}
        self.cnt = {}
        self.known = {e: {} for e in self.ENGS}
        for e in self.ENGS:
            self.cnt["E" + e] = 0
        self.all_dma = [[], []]
        self.free_dma = [[], []]
        self.persist = set()

    def _need(self, eng, deps):
        kn = self.known[eng]
        best = {}
        for d in deps:
            if d is None:
                continue
            k, v = d
            if kn.get(k, 0) >= v:
                continue
            if best.get(k, 0) < v:
                best[k] = v
        out = []
        for k, v in best.items():
            kn[k] = v
            out.append((k, v))
        return out

    def _deps(self, eng, reads, writes):
        deps = []
        me = "E" + eng
        pe = eng == "pe"
        for b in reads:
            if b.w is not None and not (pe and b.w[0] == me):
                deps.append(b.w)
        for b in writes:
            if b.w is not None and not (pe and b.w[0] == me):
                deps.append(b.w)
            for r in b.r:
                if r[0] != me:
                    deps.append(r)
        return deps

    def _commit(self, tok, reads, writes):
        for b in reads:
            b.r.append(tok)
            if len(b.r) > 64:
                best = {}
                for k, v in b.r:
                    if best.get(k, 0) < v:
                        best[k] = v
                b.r = list(best.items())
        for b in writes:
            b.w = tok
            b.r = []

    def op(self, eng, fn, reads=(), writes=()):
        if any(b.excl for b in reads):
            writes = list(writes) + [b for b in reads if b.excl]
            reads = [b for b in reads if not b.excl]
        waits = self._need(eng, self._deps(eng, reads, writes))
        key = "E" + eng
        self.cnt[key] += 1
        tok = (key, self.cnt[key])
        self.prog[eng].append((waits, fn, key, 1))
        self._commit(tok, reads, writes)
        return tok

    def new_dma_sem(self, name):
        return [None, name.startswith("cv")]

    def _bind(self, sem, eng):
        if sem[0] is None:
            q = 1 if eng == "pool" else 0
            if self.free_dma[q]:
                sem[0] = self.free_dma[q].pop()
            else:
                key = "%s%d" % ("PQ"[q], len(self.all_dma[q]))
                self.all_dma[q].append(key)
                self.cnt[key] = 0
                sem[0] = key
            if sem[1]:
                self.persist.add(sem[0])
        return sem[0]

    def dma(self, eng, sem, out, in_, reads=(), writes=()):
        sem = self._bind(sem, eng)
        waits = self._need(eng, [d for d in self._deps(eng, reads, writes) if d[0] != sem])
        self.cnt[sem] += 16
        tok = (sem, self.cnt[sem])
        self.prog[eng].append((waits, lambda e: e.dma_start(out=out, in_=in_), sem, 16))
        self._commit(tok, reads, writes)
        return tok

    def wait_all(self, eng, toks):
        waits = self._need(eng, toks)
        if waits:
            self.prog[eng].append((waits, None, None, 0))

    def barrier(self):
        toks = [(k, v) for k, v in self.cnt.items() if v > 0]
        for e in self.ENGS:
            self.wait_all(e, [t for t in toks if t[0] != "E" + e])
        self.free_dma = [[k for k in reversed(self.all_dma[q]) if k not in self.persist] for q in range(2)]

    def emit(self):
        nc = self.nc
        with contextlib.ExitStack() as st:
            sems = {k: st.enter_context(nc.semaphore(k)) for k in self.cnt}
            block = st.enter_context(nc.Block())
            hmap = {"pe": block.tensor, "act": block.scalar, "dve": block.vector,
                    "pool": block.gpsimd, "sp": block.sync}
            for e in self.ENGS:
                prog = self.prog[e]
                if not prog:
                    continue

                def body(h, prog=prog):
                    for waits, fn, key, inc in prog:
                        for (k, v) in waits:
                            h.wait_ge(sems[k], v)
                        if fn is not None:
                            fn(h).then_inc(sems[key], inc)
                hmap[e](body)


class Arena:
    def __init__(self, ap):
        self.ap = ap
        self.n = ap.shape[1]
        self.base = 0
        self.off = 0

    def freeze(self):
        self.base = self.off

    def reset(self):
        self.off = self.base

    def f32(self, n):
        o = self.off
        self.off += n
        assert self.off <= self.n, "arena overflow %d > %d" % (self.off, self.n)
        return self.ap[:, o:o + n]

    def bf16(self, n):
        w = (n + 1) // 2
        return self.f32(w).bitcast(BF16)[:, 0:n]


TILES = [(i * 512, 512, False) for i in range(8)] + [(TL, TC, True)]


def build_program(stop=None, start_layer=0):
    nc = bass.Bass("TRN2", target_bir_lowering=False)

    def din(name, shape, dt=F32):
        return nc.dram_tensor(name, list(shape), dt, kind="ExternalInput").ap()

    def dscr(name, shape, dt):
        return nc.dram_tensor(name, list(shape), dt).ap()

    xT_in = din("xT", [D, T])
    ccT = din("ccT", [128, 16])
    w_mod = din("w_mod", [DEPTH, D, 9 * D])
    bmodT = din("bmodT", [128, DEPTH * 72])
    normT = din("normT", [128, DEPTH * 48])
    ffn_w_in = din("ffn_w_in", [DEPTH, 2, D, 2 * DFF])
    ffn_w_out = din("ffn_w_out", [DEPTH, 2, DFF, D])
    consts = din("consts", [128, 3 * 128])
    da_w_qkv = din("da_w_qkv", [2, D, 3 * D])
    da_w_o = din("da_w_o", [2, D, D])
    dal = din("dal", [128, 2 * 256])
    dasub = din("dasub", [128, 2])
    ropeD = din("ropeD", [2, 128, TL])
    gqa_w_qkv = din("gqa_w_qkv", [1, D, 1536])
    gqa_w_o = din("gqa_w_o", [1, D, D])
    gqn = din("gqn", [128, 2])
    ropeG = din("ropeG", [2, 128, TL])
    hg_w_in = din("hg_w_in", [1, D, 5 * D])
    hg_w_o = din("hg_w_o", [1, D, D])
    hlbT = din("hlbT", [128, 64])
    hgn = din("hgn", [128, 1])
    hmask = din("hmask", [128, 256])
    yT = nc.dram_tensor("yT", [D, TL], F32, kind="ExternalOutput").ap()

    xs = dscr("xs", [D, T], F32)
    wi_b = dscr("wi_b", [DEPTH, 2, D, 2 * DFF], BF16)
    wo_b = dscr("wo_b", [DEPTH, 2, DFF, D], BF16)
    da_qkv_b = dscr("da_qkv_b", [2, D, 3 * D], BF16)
    da_o_b = dscr("da_o_b", [2, D, D], BF16)
    gqa_qkv_b = dscr("gqa_qkv_b", [1, D, 1536], BF16)
    gqa_o_b = dscr("gqa_o_b", [1, D, D], BF16)
    hg_in_b = dscr("hg_in_b", [1, D, 5 * D], BF16)
    hg_o_b = dscr("hg_o_b", [1, D, D], BF16)
    hq_T = dscr("hq_T", [D, T], F32)
    hlf_T = [dscr("hlf_T%d" % d_, [D, T], F32) for d_ in range(2)]
    hkk_T = [dscr("hkk_T%d" % d_, [D, T], F32) for d_ in range(2)]
    qkT = dscr("qkT", [16 * 128, T], BF16)
    vtok = dscr("vtok", [T, D], BF16)
    aT = dscr("aT", [D, T], BF16)

    with contextlib.ExitStack() as st:
        arena_t = st.enter_context(nc.sbuf_tensor("arena", [128, 50 * 1024], F32))
        PS = [st.enter_context(nc.psum_tensor("ps%d" % i, [128, 512], F32)) for i in range(8)]
        AR = Arena(arena_t[:, :])
        S = Sched(nc)
        PSB = [Buf("ps%d" % i, excl=True) for i in range(8)]

        ones_bf = AR.bf16(128)
        ones_f = AR.f32(128)
        ident_bf = AR.bf16(128)
        rotm_bf = AR.bf16(128)
        cst_f = AR.f32(384)
        eps_t = AR.f32(1)
        ccs = AR.f32(16)
        modL = AR.f32(DEPTH * 72)
        modC = AR.f32(DEPTH * 72)
        bmod_s = AR.f32(DEPTH * 72)
        norm_s = AR.f32(DEPTH * 48)
        SA = [AR.f32(DEPTH * 24), AR.f32(DEPTH * 24)]
        SG = [AR.f32(DEPTH * 24), AR.f32(DEPTH * 24)]
        MOD = [modL, modC]
        dal_s = AR.f32(512)
        dasub_s = AR.f32(2)
        gqn_s = AR.f32(2)
        da_neglam = AR.f32(2)
        da_subs = AR.f32(2)
        da_tmp = AR.f32(64)
        da_e = AR.f32(4)
        hlb_s = AR.f32(64)
        hgn_s = AR.f32(1)
        hg_lb = AR.f32(16)
        hg_oml = AR.f32(16)
        hg_sum = AR.f32(16)
        AR.freeze()
        b_const = Buf("const")
        b_mod = Buf("mod")

        conv_bufs = {}

        cv_inflight = [None, None]

        conv_jobs = []
        conv_pending = {}

        def convert(name, src, dst, rows):
            sems = [S.new_dma_sem("cv%d_%s" % (q, name)) for q in range(2)]
            bs = [Buf("cv%d_%s" % (q, name)) for q in range(2)]
            r0 = 0
            n = 0
            while r0 < rows:
                r1 = min(rows, r0 + 128)
                conv_jobs.append((name, n % 2, sems[n % 2], bs[n % 2], dst[r0:r1, :], src[r0:r1, :]))
                r0 = r1
                n += 1
            conv_pending[name] = n
            conv_bufs[name] = bs

        def pump(n):
            while n > 0 and conv_jobs:
                name, q, sem, b, dst, src = conv_jobs.pop(0)
                if cv_inflight[q] is not None:
                    S.wait_all("pool", [cv_inflight[q]])
                cv_inflight[q] = S.dma("pool", sem, dst, src, writes=[b])
                conv_pending[name] -= 1
                n -= 1

        def pump_until(*names):
            while any(conv_pending.get(nm, 0) > 0 for nm in names):
                pump(1)

        def convert_ffn(i, f):
            convert("wi%d%d" % (i, f), ffn_w_in[i, f], wi_b[i, f], D)
            convert("wo%d%d" % (i, f), ffn_w_out[i, f], wo_b[i, f], DFF)

        def convert_mixer(i):
            kind, j = i % 3, i // 3
            if kind == 0:
                convert("daqkv%d" % j, da_w_qkv[j], da_qkv_b[j], D)
                convert("dao%d" % j, da_w_o[j], da_o_b[j], D)
            elif kind == 2:
                convert("gqaqkv%d" % j, gqa_w_qkv[j], gqa_qkv_b[j], D)
                convert("gqao%d" % j, gqa_w_o[j], gqa_o_b[j], D)
            else:
                convert("hgin%d" % j, hg_w_in[j], hg_in_b[j], D)
                convert("hgo%d" % j, hg_w_o[j], hg_o_b[j], D)

        def mixer_names(i):
            kind, j = i % 3, i // 3
            return {0: ("daqkv%d" % j, "dao%d" % j), 1: ("hgin%d" % j, "hgo%d" % j), 2: ("gqaqkv%d" % j, "gqao%d" % j)}[kind]

        for li in range(start_layer, DEPTH):
            convert_ffn(li, 0)
            convert_mixer(li)
            convert_ffn(li, 1)
        pump_until("wi%d0" % start_layer, "wo%d0" % start_layer)
        pump(16)

        s_c0 = S.new_dma_sem("c0")
        S.dma("sp", s_c0, cst_f, consts[:, :], writes=[b_const])
        S.dma("sp", s_c0, ccs, ccT[:, :], writes=[b_const])
        S.dma("sp", s_c0, bmod_s, bmodT[:, :], writes=[b_const])
        S.dma("sp", s_c0, norm_s, normT[:, :], writes=[b_const])
        b_c2 = Buf("const2")
        S.op("dve", lambda e: e.memset(ones_bf, 1.0), writes=[b_c2])
        S.op("dve", lambda e: e.memset(ones_f, 1.0), writes=[b_c2])
        S.op("dve", lambda e: e.memset(eps_t, EPS), writes=[b_c2])
        S.op("dve", lambda e: e.tensor_copy(ident_bf, cst_f[:, 0:128]), reads=[b_const], writes=[b_c2])
        S.op("dve", lambda e: e.tensor_copy(rotm_bf, cst_f[:, 128:256]), reads=[b_const], writes=[b_c2])
        S.op("act", lambda e: e.activation(out=ccs, in_=ccs, func=AF.Silu), reads=[b_const], writes=[b_const])

        S.dma("sp", s_c0, dal_s, dal[:, :], writes=[b_const])
        S.dma("sp", s_c0, dasub_s, dasub[:, :], writes=[b_const])
        S.dma("sp", s_c0, gqn_s, gqn[:, :], writes=[b_const])
        b_da = Buf("da")
        for jj in range(2):
            li = 3 * jj
            lam_init = 0.8 - 0.6 * math.exp(-0.3 * li)
            for pair in range(2):
                o0 = jj * 256 + pair * 128
                S.op("dve", lambda e, o0=o0: e.tensor_tensor(da_tmp, dal_s[:, o0:o0 + 64], dal_s[:, o0 + 64:o0 + 128], ALU.mult),
                     reads=[b_const], writes=[b_da])
                S.op("dve", lambda e, jj=jj, pair=pair: e.tensor_reduce(
                    da_e[:, jj * 2 + pair:jj * 2 + pair + 1], da_tmp, mybir.AxisListType.X, ALU.add),
                    reads=[b_da], writes=[b_da])
            S.op("act", lambda e, jj=jj: e.activation(out=da_e[:, jj * 2:jj * 2 + 2], in_=da_e[:, jj * 2:jj * 2 + 2], func=AF.Exp),
                 reads=[b_da], writes=[b_da])
            S.op("dve", lambda e, jj=jj, lam_init=lam_init: e.scalar_tensor_tensor(
                da_neglam[:, jj:jj + 1], da_e[:, jj * 2 + 1:jj * 2 + 2], -lam_init, da_e[:, jj * 2:jj * 2 + 1], ALU.add, ALU.subtract),
                reads=[b_da], writes=[b_da])
            S.op("dve", lambda e, jj=jj, lam_init=lam_init: e.tensor_scalar(
                da_subs[:, jj:jj + 1], dasub_s[:, jj:jj + 1], 1.0 - lam_init, None, ALU.mult),
                reads=[b_const, b_da], writes=[b_da])

        S.dma("sp", s_c0, hlb_s, hlbT[:, :], writes=[b_const])
        S.dma("sp", s_c0, hgn_s, hgn[:, :], writes=[b_const])
        b_hg = Buf("hg")
        HL = 1
        S.op("act", lambda e: e.activation(out=hlb_s, in_=hlb_s, func=AF.Exp), reads=[b_const], writes=[b_const])
        hl4 = hlb_s.rearrange("p (d l k) -> p d l k", d=2, l=4)
        for dd in range(2):
            S.op("dve", lambda e, dd=dd: e.tensor_tensor(hg_sum[:, dd * 8:dd * 8 + 8], hl4[:, dd, 0, :], hl4[:, dd, 1, :], ALU.add),
                 reads=[b_const], writes=[b_hg])
            for l in (2, 3):
                S.op("dve", lambda e, dd=dd, l=l: e.tensor_tensor(hg_sum[:, dd * 8:dd * 8 + 8], hg_sum[:, dd * 8:dd * 8 + 8], hl4[:, dd, l, :], ALU.add),
                     reads=[b_const, b_hg], writes=[b_hg])
            S.op("dve", lambda e, dd=dd: e.tensor_copy(hg_lb[:, dd * 8:dd * 8 + 8], hl4[:, dd, 1, :]), reads=[b_const], writes=[b_hg])
            for l in range(2, HL + 1):
                S.op("dve", lambda e, dd=dd, l=l: e.tensor_tensor(hg_lb[:, dd * 8:dd * 8 + 8], hg_lb[:, dd * 8:dd * 8 + 8], hl4[:, dd, l, :], ALU.add),
                     reads=[b_const, b_hg], writes=[b_hg])
        S.op("dve", lambda e: e.reciprocal(hg_sum, hg_sum), reads=[b_hg], writes=[b_hg])
        S.op("dve", lambda e: e.tensor_tensor(hg_lb, hg_lb, hg_sum, ALU.mult), reads=[b_hg], writes=[b_hg])
        S.op("dve", lambda e: e.tensor_scalar(hg_oml, hg_lb, -1.0, 1.0, ALU.mult, ALU.add), reads=[b_hg], writes=[b_hg])

        wm_slots = [AR.f32(8 * 1024), AR.f32(8 * 1024)]
        wm_bufs = [Buf("wm0"), Buf("wm1")]
        wm_sems = [S.new_dma_sem("wm0"), S.new_dma_sem("wm1")]
        blk = 0
        for i in range(DEPTH):
            wv = w_mod[i].rearrange("(k p) n -> p k n", p=128)
            for m in range(9):
                sl = blk % 2
                wt = wm_slots[sl].rearrange("p (k n) -> p k n", k=8)
                S.dma("sp", wm_sems[sl], wt, wv[:, :, m * 1024:(m + 1) * 1024], writes=[wm_bufs[sl]])
                pb = blk % 2
                ps = PS[pb]
                for n in range(8):
                    for k in range(8):
                        S.op("pe", lambda e, ps=ps, wt=wt, n=n, k=k: e.matmul(
                            ps[:, n * 2:n * 2 + 2], wt[:, k, n * 128:(n + 1) * 128], ccs[:, 2 * k:2 * k + 2],
                            start=(k == 0), stop=(k == 7)),
                            reads=[wm_bufs[sl], b_const], writes=[PSB[pb]])
                ps3 = ps[:, 0:16].rearrange("p (n j) -> p n j", j=2)
                c0 = i * 72 + m * 8
                for j in range(2):
                    S.op("dve", lambda e, ps3=ps3, j=j, c0=c0: e.tensor_tensor(
                        MOD[j][:, c0:c0 + 8], ps3[:, :, j], bmod_s[:, c0:c0 + 8], ALU.add),
                        reads=[PSB[pb], b_const], writes=[b_mod])
                blk += 1
        for i in range(DEPTH):
            for s in range(3):
                wgt = 1.0 if s == 1 else 0.5
                c = (i * 3 + s) * 8
                gpre = norm_s[:, i * 48 + (2 * s) * 8: i * 48 + (2 * s) * 8 + 8]
                gpost = norm_s[:, i * 48 + (2 * s + 1) * 8: i * 48 + (2 * s + 1) * 8 + 8]
                for j in range(2):
                    scale = MOD[j][:, i * 72 + (3 * s + 1) * 8: i * 72 + (3 * s + 1) * 8 + 8]
                    gate = MOD[j][:, i * 72 + (3 * s + 2) * 8: i * 72 + (3 * s + 2) * 8 + 8]
                    S.op("dve", lambda e, j=j, c=c, scale=scale, gpre=gpre: e.scalar_tensor_tensor(
                        SA[j][:, c:c + 8], scale, 1.0, gpre, ALU.add, ALU.mult),
                        reads=[b_mod, b_const], writes=[b_mod])
                    S.op("dve", lambda e, j=j, c=c, gate=gate, gpost=gpost, wgt=wgt: e.scalar_tensor_tensor(
                        SG[j][:, c:c + 8], gate, wgt, gpost, ALU.mult, ALU.mult),
                        reads=[b_mod, b_const], writes=[b_mod])

        def scal(i, s, j):
            c = (i * 3 + s) * 8
            shift0 = i * 72 + (3 * s) * 8
            return (lambda k: SA[j][:, c + k:c + k + 1],
                    lambda k: MOD[j][:, shift0 + k:shift0 + k + 1],
                    lambda k: SG[j][:, c + k:c + k + 1])

        S.barrier()
        AR.reset()
        if stop in ("setup", "setup0"):
            S.emit()
            return nc

        xs_v = xs.rearrange("(k p) t -> p k t", p=128)
        xin_v = xT_in.rearrange("(k p) t -> p k t", p=128)
        y_v = yT.rearrange("(k p) t -> p k t", p=128)
        XB = [Buf("xs%d" % t) for t in range(len(TILES))]
        state = {"src_is_input": True}

        def rstd_from_sq(sq3, W, nk, ps, psb, rstd, b_rstd, b_sq, denom):
            for k in range(nk):
                S.op("pe", lambda e, k=k: e.matmul(ps[:, 0:W], ones_bf, sq3[:, k, :], start=(k == 0), stop=(k == nk - 1)),
                     reads=[b_sq, b_c2], writes=[psb])
            S.op("act", lambda e: e.activation(out=rstd[:, 0:W], in_=ps[:, 0:W], func=AF.Sqrt, scale=1.0 / denom, bias=eps_t),
                 reads=[psb, b_c2], writes=[b_rstd])
            S.op("dve", lambda e: e.reciprocal(rstd[:, 0:W], rstd[:, 0:W]), reads=[b_rstd], writes=[b_rstd])

        def ffn_phase(i, f, last=False):
            s = 0 if f == 0 else 2
            need_ctx = not (i == DEPTH - 1 and f == 1)
            tiles = [t for t in range(len(TILES)) if need_ctx or not TILES[t][2]]
            AR.reset()
            xt_s = [AR.f32(8 * 512) for _ in range(2)]
            ysb = AR.f32(8 * 512)
            sq_s = AR.bf16(8 * 512)
            rstd_s = [AR.f32(512) for _ in range(2)]
            t1_s = [AR.f32(512) for _ in range(2)]
            h_s = [AR.bf16(8 * 512) for _ in range(2)]
            act_s = AR.bf16(NJ * 512)
            wo_s = AR.bf16(NJ * 1024)
            wi_s = [AR.bf16(8 * 1024) for _ in range(2)]
            sg_s = [AR.f32(512) for _ in range(2)]
            b_xt = [Buf(), Buf()]
            b_ysb, b_sq, b_act, b_wo = Buf(), Buf(), Buf(), Buf()
            b_rstd = [Buf(), Buf()]
            b_t1 = [Buf(), Buf()]
            b_h = [Buf(), Buf()]
            b_wi = [Buf(), Buf()]
            b_sg = [Buf(), Buf()]
            tag = "%d%d" % (i, f)
            sem_x = [S.new_dma_sem("fx0_" + tag), S.new_dma_sem("fx1_" + tag)]
            sem_wi = [S.new_dma_sem("fwi0_" + tag), S.new_dma_sem("fwi1_" + tag)]
            sem_wo = S.new_dma_sem("fwo_" + tag)
            sem_st = S.new_dma_sem("fst_" + tag)
            pump_until("wi" + tag, "wo" + tag)
            cv_wi = conv_bufs["wi" + tag]
            cv_wo = conv_bufs["wo" + tag]
            wi_v = wi_b[i, f].rearrange("(k p) n -> p k n", p=128)
            wo_v = wo_b[i, f].rearrange("(j p) n -> p j n", p=128)
            src_v = xin_v if state["src_is_input"] else xs_v

            wo3 = wo_s.rearrange("p (j n) -> p j n", j=NJ)
            for a in range(0, NJ, 6):
                b = min(NJ, a + 6)
                S.dma("sp", sem_wo, wo3[:, a:b, :], wo_v[:, a:b, :], reads=cv_wo, writes=[b_wo])

            groups = [(a, min(NJ, a + 4)) for a in range(0, NJ, 4)]
            wi_ctr = [0]

            def load_x(ti):
                t = tiles[ti]
                t0, W, isc = TILES[t]
                sl = ti % 2
                x3 = xt_s[sl].rearrange("p (k t) -> p k t", k=8)
                S.dma("sp", sem_x[sl], x3[:, :, 0:W], src_v[:, :, t0:t0 + W], reads=[XB[t]], writes=[b_xt[sl]])

            def prenorm(ti):
                t = tiles[ti]
                t0, W, isc = TILES[t]
                sl = ti % 2
                A, Bs, G = scal(i, s, 1 if isc else 0)
                x3 = xt_s[sl].rearrange("p (k t) -> p k t", k=8)
                sq3 = sq_s.rearrange("p (k t) -> p k t", k=8)
                h3 = h_s[sl].rearrange("p (k t) -> p k t", k=8)
                S.op("act", lambda e: e.activation(out=sq3[:, :, 0:W], in_=x3[:, :, 0:W], func=AF.Square),
                     reads=[b_xt[sl]], writes=[b_sq])
                rstd_from_sq(sq3[:, :, 0:W], W, 8, PS[4], PSB[4], rstd_s[sl], b_rstd[sl], b_sq, float(D))
                for k in range(8):
                    q = k % 2
                    S.op("dve", lambda e, k=k, q=q: e.scalar_tensor_tensor(
                        t1_s[q][:, 0:W], x3[:, k, 0:W], A(k), rstd_s[sl][:, 0:W], ALU.mult, ALU.mult),
                        reads=[b_xt[sl], b_rstd[sl], b_mod], writes=[b_t1[q]])
                    S.op("act", lambda e, k=k, q=q: e.activation(
                        out=h3[:, k, 0:W], in_=t1_s[q][:, 0:W], func=AF.Identity, bias=Bs(k), scale=1.0),
                        reads=[b_t1[q], b_mod], writes=[b_h[sl]])

            def gateup(ti, gi):
                t = tiles[ti]
                t0, W, isc = TILES[t]
                sl = ti % 2
                h3 = h_s[sl].rearrange("p (k t) -> p k t", k=8)
                a, b = groups[gi]
                ng = b - a
                ws = wi_ctr[0] % 2
                wi_ctr[0] += 1
                w3 = wi_s[ws].rearrange("p (k n) -> p k n", k=8)
                S.dma("sp", sem_wi[ws], w3[:, :, 0:ng * 128], wi_v[:, :, a * 128:b * 128], reads=cv_wi, writes=[b_wi[ws]])
                S.dma("sp", sem_wi[ws], w3[:, :, 512:512 + ng * 128], wi_v[:, :, DFF + a * 128:DFF + b * 128],
                      reads=cv_wi, writes=[b_wi[ws]])
                act3 = act_s.rearrange("p (j t) -> p j t", j=NJ)
                pump(1)
                for jj in range(ng):
                    j = a + jj
                    pp = j % 2
                    pg, pu = PS[2 * pp], PS[2 * pp + 1]
                    for k in range(8):
                        S.op("pe", lambda e, k=k, jj=jj, pg=pg: e.matmul(
                            pg[:, 0:W], w3[:, k, jj * 128:(jj + 1) * 128], h3[:, k, 0:W], start=(k == 0), stop=(k == 7)),
                            reads=[b_wi[ws], b_h[sl]], writes=[PSB[2 * pp]])
                    for k in range(8):
                        S.op("pe", lambda e, k=k, jj=jj, pu=pu: e.matmul(
                            pu[:, 0:W], w3[:, k, 512 + jj * 128:512 + (jj + 1) * 128], h3[:, k, 0:W], start=(k == 0), stop=(k == 7)),
                            reads=[b_wi[ws], b_h[sl]], writes=[PSB[2 * pp + 1]])
                    S.op("act", lambda e, pp=pp, pg=pg: e.activation(out=sg_s[pp][:, 0:W], in_=pg[:, 0:W], func=AF.Silu),
                         reads=[PSB[2 * pp]], writes=[b_sg[pp]])
                    S.op("dve", lambda e, pp=pp, pu=pu, j=j: e.tensor_tensor(
                        act3[:, j, 0:W], sg_s[pp][:, 0:W], pu[:, 0:W], ALU.mult),
                        reads=[b_sg[pp], PSB[2 * pp + 1]], writes=[b_act])

            def wout(ti):
                t = tiles[ti]
                t0, W, isc = TILES[t]
                act3 = act_s.rearrange("p (j t) -> p j t", j=NJ)
                y3 = ysb.rearrange("p (k t) -> p k t", k=8)
                sq3 = sq_s.rearrange("p (k t) -> p k t", k=8)
                for dk in range(8):
                    pb = 5 + dk % 2
                    py = PS[pb]
                    for j in range(NJ):
                        S.op("pe", lambda e, j=j, dk=dk, py=py: e.matmul(
                            py[:, 0:W], wo3[:, j, dk * 128:(dk + 1) * 128], act3[:, j, 0:W], start=(j == 0), stop=(j == NJ - 1)),
                            reads=[b_wo, b_act], writes=[PSB[pb]])
                    S.op("act", lambda e, dk=dk, py=py: e.activation(out=sq3[:, dk, 0:W], in_=py[:, 0:W], func=AF.Square),
                         reads=[PSB[pb]], writes=[b_sq])
                    S.op("dve", lambda e, dk=dk, py=py: e.tensor_copy(y3[:, dk, 0:W], py[:, 0:W]),
                         reads=[PSB[pb]], writes=[b_ysb])

            def postnorm(ti):
                t = tiles[ti]
                t0, W, isc = TILES[t]
                sl = ti % 2
                A, Bs, G = scal(i, s, 1 if isc else 0)
                x3 = xt_s[sl].rearrange("p (k t) -> p k t", k=8)
                y3 = ysb.rearrange("p (k t) -> p k t", k=8)
                sq3 = sq_s.rearrange("p (k t) -> p k t", k=8)
                rs = rstd_s[sl]
                rstd_from_sq(sq3[:, :, 0:W], W, 8, PS[4], PSB[4], rs, b_rstd[sl], b_sq, float(D))
                for k in range(8):
                    S.op("dve", lambda e, k=k: e.scalar_tensor_tensor(
                        y3[:, k, 0:W], y3[:, k, 0:W], G(k), rs[:, 0:W], ALU.mult, ALU.mult),
                        reads=[b_rstd[sl], b_mod], writes=[b_ysb])
                S.op("pool", lambda e: e.tensor_tensor(y3[:, :, 0:W], y3[:, :, 0:W], x3[:, :, 0:W], ALU.add),
                     reads=[b_xt[sl], b_ysb], writes=[b_ysb])
                if last and not isc:
                    S.dma("pool", sem_st, y_v[:, :, t0:t0 + W], y3[:, :, 0:W], reads=[b_ysb], writes=[XB[t]])
                else:
                    S.dma("pool", sem_st, xs_v[:, :, t0:t0 + W], y3[:, :, 0:W], reads=[b_ysb], writes=[XB[t]])

            n = len(tiles)
            if DBG:
                load_x(0)
                prenorm(0)
                if DBG >= 2:
                    for gi in range(len(groups)):
                        gateup(0, gi)
                if DBG >= 3:
                    wout(0)
                if DBG >= 4:
                    postnorm(0)
                S.barrier()
                return
            load_x(0)
            prenorm(0)
            for ti in range(n):
                gateup(ti, 0)
                gateup(ti, 1)
                if ti + 1 < n:
                    load_x(ti + 1)
                for gi in range(2, len(groups)):
                    gateup(ti, gi)
                if ti + 1 < n:
                    prenorm(ti + 1)
                wout(ti)
                postnorm(ti)
            state["src_is_input"] = False
            S.barrier()

        def dump_and_finish():
            AR.reset()
            tb = AR.f32(8 * 512)
            t3 = tb.rearrange("p (k t) -> p k t", k=8)
            bb = Buf()
            s1 = S.new_dma_sem("dbg_l")
            s2 = S.new_dma_sem("dbg_s")
            for t in range(8):
                t0, W, _ = TILES[t]
                S.dma("sp", s1, t3, xs_v[:, :, t0:t0 + W], reads=[XB[t]], writes=[bb])
                S.dma("sp", s2, y_v[:, :, t0:t0 + W], t3, reads=[bb])
            S.barrier()

        def attn_qkv_phase(i, kind, j):
            is_da = kind == 0
            NQ = 8
            NKC = 8 if is_da else 2
            NV = NKC
            ncols = (NQ + 2 * NKC) * 128
            wsrc = (da_qkv_b if is_da else gqa_qkv_b)[j]
            rope = ropeD if is_da else ropeG
            cvb = conv_bufs[("daqkv%d" if is_da else "gqaqkv%d") % j]
            AR.reset()
            xt_s = [AR.f32(8 * 512) for _ in range(2)]
            sq_s = AR.bf16(8 * 512)
            rstd_s = [AR.f32(512) for _ in range(2)]
            t1_s = [AR.f32(512) for _ in range(2)]
            h_s = [AR.bf16(8 * 512) for _ in range(2)]
            w_s = AR.bf16(8 * ncols)
            cs_s = [AR.f32(2 * 512) for _ in range(2)]
            qk_s = [AR.bf16((NQ + NKC) * 512) for _ in range(2)]
            v_s = [AR.bf16(4 * NV * 128) for _ in range(2)]
            qb_s = [AR.bf16(512) for _ in range(2)]
            ta_s = [AR.f32(512) for _ in range(2)]
            tb_s = [AR.f32(512) for _ in range(2)]
            qn_s = [AR.f32(512) for _ in range(2)]
            rs_s = [AR.f32(512) for _ in range(2)]
            b_xt = [Buf(), Buf()]
            b_sq = Buf()
            b_rstd = [Buf(), Buf()]
            b_t1 = [Buf(), Buf()]
            b_h = [Buf(), Buf()]
            b_w = Buf()
            b_cs = [Buf(), Buf()]
            b_qk = [Buf(), Buf()]
            b_v = [Buf(), Buf()]
            b_qb = [Buf(), Buf()]
            b_ta = [Buf(), Buf()]
            b_tb = [Buf(), Buf()]
            b_qn = [Buf(), Buf()]
            b_rs = [Buf(), Buf()]
            tag = "q%d" % i
            sem_x = [S.new_dma_sem("ax0_" + tag), S.new_dma_sem("ax1_" + tag)]
            sem_cs = [S.new_dma_sem("acs0_" + tag), S.new_dma_sem("acs1_" + tag)]
            sem_w = S.new_dma_sem("aw_" + tag)
            sem_st = [S.new_dma_sem("ast0_" + tag), S.new_dma_sem("ast1_" + tag)]
            sem_sv = [S.new_dma_sem("asv0_" + tag), S.new_dma_sem("asv1_" + tag)]
            w3 = w_s.rearrange("p (k n) -> p k n", k=8)
            wv = wsrc.rearrange("(k p) n -> p k n", p=128)
            for a in range(0, ncols, 768):
                S.dma("sp", sem_w, w3[:, :, a:a + 768], wv[:, :, a:a + 768], reads=cvb, writes=[b_w])
            qk_v = qkT.rearrange("(c p) t -> p c t", p=128)
            v_v = vtok.rearrange("(tb p) e -> p tb e", p=128)
            gq = gqn_s[:, 0:1]
            gk = gqn_s[:, 1:2]
            n = len(TILES)
            cnt = [0]

            def load_x(ti):
                t0, W, isc = TILES[ti]
                sl = ti % 2
                x3 = xt_s[sl].rearrange("p (k t) -> p k t", k=8)
                S.dma("sp", sem_x[sl], x3[:, :, 0:W], xs_v[:, :, t0:t0 + W], reads=[XB[ti]], writes=[b_xt[sl]])
                if not isc:
                    c3 = cs_s[sl].rearrange("p (a t) -> p a t", a=2)
                    S.dma("sp", sem_cs[sl], c3[:, :, 0:W], rope.rearrange("a p t -> p a t")[:, :, t0:t0 + W], writes=[b_cs[sl]])

            def prenorm(ti):
                t0, W, isc = TILES[ti]
                sl = ti % 2
                A, Bs, G = scal(i, 1, 1 if isc else 0)
                x3 = xt_s[sl].rearrange("p (k t) -> p k t", k=8)
                sq3 = sq_s.rearrange("p (k t) -> p k t", k=8)
                h3 = h_s[sl].rearrange("p (k t) -> p k t", k=8)
                S.op("act", lambda e: e.activation(out=sq3[:, :, 0:W], in_=x3[:, :, 0:W], func=AF.Square),
                     reads=[b_xt[sl]], writes=[b_sq])
                rstd_from_sq(sq3[:, :, 0:W], W, 8, PS[7], PSB[7], rstd_s[sl], b_rstd[sl], b_sq, float(D))
                for k in range(8):
                    q = k % 2
                    S.op("dve", lambda e, k=k, q=q: e.scalar_tensor_tensor(
                        t1_s[q][:, 0:W], x3[:, k, 0:W], A(k), rstd_s[sl][:, 0:W], ALU.mult, ALU.mult),
                        reads=[b_xt[sl], b_rstd[sl], b_mod], writes=[b_t1[q]])
                    S.op("act", lambda e, k=k, q=q: e.activation(
                        out=h3[:, k, 0:W], in_=t1_s[q][:, 0:W], func=AF.Identity, bias=Bs(k), scale=1.0),
                        reads=[b_t1[q], b_mod], writes=[b_h[sl]])

            def project(ti):
                t0, W, isc = TILES[ti]
                sl = ti % 2
                h3 = h_s[sl].rearrange("p (k t) -> p k t", k=8)
                qk3 = qk_s[sl].rearrange("p (c t) -> p c t", c=NQ + NKC)
                cos_t = cs_s[sl][:, 0:W]
                sin_t = cs_s[sl][:, 512:512 + W]
                for c in range(NQ + NKC):
                    u = cnt[0] % 2
                    cnt[0] += 1
                    pb = u
                    pr = 2 + u
                    ps = PS[pb]
                    for k in range(8):
                        S.op("pe", lambda e, k=k, c=c, ps=ps: e.matmul(
                            ps[:, 0:W], w3[:, k, c * 128:(c + 1) * 128], h3[:, k, 0:W], start=(k == 0), stop=(k == 7)),
                            reads=[b_w, b_h[sl]], writes=[PSB[pb]])
                    if is_da:
                        src = ps[:, 0:W]
                        srcb = [PSB[pb]]
                    else:
                        gain = gq if c < NQ else gk
                        S.op("act", lambda e, u=u, ps=ps: e.activation(out=qb_s[u][:, 0:W], in_=ps[:, 0:W], func=AF.Square),
                             reads=[PSB[pb]], writes=[b_qb[u]])
                        S.op("pe", lambda e, u=u: e.matmul(PS[4 + u][:, 0:W], ones_bf, qb_s[u][:, 0:W], start=True, stop=True),
                             reads=[b_qb[u], b_c2], writes=[PSB[4 + u]])
                        S.op("act", lambda e, u=u: e.activation(out=rs_s[u][:, 0:W], in_=PS[4 + u][:, 0:W], func=AF.Sqrt,
                                                                scale=1.0 / 128.0, bias=eps_t),
                             reads=[PSB[4 + u], b_c2], writes=[b_rs[u]])
                        S.op("dve", lambda e, u=u: e.reciprocal(rs_s[u][:, 0:W], rs_s[u][:, 0:W]), reads=[b_rs[u]], writes=[b_rs[u]])
                        S.op("dve", lambda e, u=u, ps=ps, gain=gain: e.scalar_tensor_tensor(
                            qn_s[u][:, 0:W], ps[:, 0:W], gain, rs_s[u][:, 0:W], ALU.mult, ALU.mult),
                            reads=[PSB[pb], b_rs[u], b_const], writes=[b_qn[u]])
                        src = qn_s[u][:, 0:W]
                        srcb = [b_qn[u]]
                    if isc:
                        S.op("act", lambda e, c=c, src=src: e.activation(out=qk3[:, c, 0:W], in_=src, func=AF.Copy),
                             reads=srcb, writes=[b_qk[sl]])
                    else:
                        S.op("act", lambda e, u=u, src=src: e.activation(out=qb_s[u][:, 0:W], in_=src, func=AF.Copy),
                             reads=srcb, writes=[b_qb[u]])
                        S.op("pe", lambda e, u=u, pr=pr: e.matmul(PS[pr][:, 0:W], rotm_bf, qb_s[u][:, 0:W], start=True, stop=True),
                             reads=[b_qb[u], b_c2], writes=[PSB[pr]])
                        S.op("dve", lambda e, u=u, src=src: e.tensor_tensor(ta_s[u][:, 0:W], src, cos_t, ALU.mult),
                             reads=srcb + [b_cs[sl]], writes=[b_ta[u]])
                        S.op("dve", lambda e, u=u, pr=pr: e.tensor_tensor(tb_s[u][:, 0:W], PS[pr][:, 0:W], sin_t, ALU.mult),
                             reads=[PSB[pr], b_cs[sl]], writes=[b_tb[u]])
                        S.op("pool", lambda e, u=u, c=c: e.tensor_tensor(qk3[:, c, 0:W], ta_s[u][:, 0:W], tb_s[u][:, 0:W], ALU.add),
                             reads=[b_ta[u], b_tb[u]], writes=[b_qk[sl]])
                ntb = W // 128
                vcols = NV * 128
                v3 = v_s[sl].rearrange("p (tb e) -> p tb e", tb=4)
                voff = (NQ + NKC) * 128
                for tb in range(ntb):
                    for c0 in range(0, vcols, 512):
                        cw = min(512, vcols - c0)
                        u = cnt[0] % 2
                        cnt[0] += 1
                        pb = u
                        ps = PS[pb]
                        for k in range(8):
                            S.op("pe", lambda e, k=k, tb=tb, c0=c0, cw=cw, ps=ps: e.matmul(
                                ps[:, 0:cw], h3[:, k, tb * 128:(tb + 1) * 128], w3[:, k, voff + c0:voff + c0 + cw],
                                start=(k == 0), stop=(k == 7)),
                                reads=[b_w, b_h[sl]], writes=[PSB[pb]])
                        if u == 0:
                            S.op("act", lambda e, tb=tb, c0=c0, cw=cw, ps=ps: e.activation(
                                out=v3[:, tb, c0:c0 + cw], in_=ps[:, 0:cw], func=AF.Copy),
                                reads=[PSB[pb]], writes=[b_v[sl]])
                        else:
                            S.op("dve", lambda e, tb=tb, c0=c0, cw=cw, ps=ps: e.tensor_copy(
                                v3[:, tb, c0:c0 + cw], ps[:, 0:cw]),
                                reads=[PSB[pb]], writes=[b_v[sl]])
                S.dma("pool", sem_st[sl], qk_v[:, 0:NQ + NKC, t0:t0 + W], qk3[:, :, 0:W], reads=[b_qk[sl]])
                S.dma("pool", sem_sv[sl], v_v[:, t0 // 128:t0 // 128 + ntb, 0:vcols], v3[:, 0:ntb, 0:vcols],
                      reads=[b_v[sl]])

            load_x(0)
            prenorm(0)
            for ti in range(n):
                if ti + 1 < n:
                    load_x(ti + 1)
                project(ti)
                if ti + 1 < n:
                    prenorm(ti + 1)
            S.barrier()

        def attn_core_phase(i, kind, j):
            is_da = kind == 0
            need_ctx = i < DEPTH - 1
            NQ = 8
            NKC = 8 if is_da else 2
            jj = j
            AR.reset()
            q_s = [AR.bf16(T) for _ in range(2)]
            k_s = [AR.bf16(T) for _ in range(2)]
            k2_s = [AR.bf16(T) for _ in range(2)] if is_da else None
            v_s = [AR.bf16(34 * 128) for _ in range(2)]
            pT_s = [AR.bf16(512) for _ in range(3)]
            r_s = [AR.f32(512) for _ in range(2)]
            dacc_s = [AR.f32(512) for _ in range(2)]
            b_dacc = [Buf(), Buf()]
            o_s = [AR.f32(512) for _ in range(2)]
            osq_s = AR.bf16(512)
            rs_s = AR.f32(512)
            on_s = [AR.bf16(512) for _ in range(2)]
            b_hd = [Buf(), Buf()]
            b_pT = [Buf(), Buf(), Buf()]
            b_r = [Buf(), Buf()]
            b_o = [Buf(), Buf()]
            b_osq, b_rs = Buf(), Buf()
            b_on = [Buf(), Buf()]
            tag = "c%d" % i
            sem_h = [S.new_dma_sem("ch0_" + tag), S.new_dma_sem("ch1_" + tag)]
            sem_st = [S.new_dma_sem("cst0_" + tag), S.new_dma_sem("cst1_" + tag)]
            v_v = vtok.rearrange("(c p) e -> p c e", p=128)
            qtiles = [t for t in range(len(TILES)) if need_ctx or not TILES[t][2]]
            maps = [(0, 64), (64, 64)] if is_da else [(0, 128)]
            nm = len(maps)
            sc = 0.125 if is_da else 128.0 ** -0.5
            neglam = da_neglam[:, jj:jj + 1]
            subs = da_subs[:, jj:jj + 1]
            octr = [0]
            pctr = [0]

            def load_head(hq):
                sl = hq % 2
                kc = NQ + (hq if is_da else hq // 4)
                vc = hq if is_da else hq // 4
                S.dma("sp", sem_h[sl], q_s[sl], qkT[hq * 128:(hq + 1) * 128, :], writes=[b_hd[sl]])
                if is_da:
                    S.dma("sp", sem_h[sl], k_s[sl][0:64, :], qkT[kc * 128:kc * 128 + 64, :], reads=[b_kz], writes=[b_hd[sl]])
                    S.dma("sp", sem_h[sl], k2_s[sl][64:128, :], qkT[kc * 128 + 64:(kc + 1) * 128, :], reads=[b_kz], writes=[b_hd[sl]])
                else:
                    S.dma("sp", sem_h[sl], k_s[sl], qkT[kc * 128:(kc + 1) * 128, :], writes=[b_hd[sl]])
                vd = v_s[sl].rearrange("p (c e) -> p c e", c=34)
                for c0 in range(0, 34, 8):
                    c1 = min(34, c0 + 8)
                    S.dma("sp", sem_h[sl], vd[:, c0:c1, :], v_v[:, c0:c1, vc * 128:(vc + 1) * 128],
                          writes=[b_hd[sl]])

            def head(hq):
                sl = hq % 2
                v3 = v_s[sl].rearrange("p (c e) -> p c e", c=34)
                for t in qtiles:
                    qtile(hq, sl, v3, t)

            def qtile(hq, sl, v3, t):
                if True:
                    t0, W, isc = TILES[t]
                    kcs = [32, 33] if isc else list(range(34))
                    steps = [(kc, m) for kc in kcs for m in range(nm)]

                    def score(n):
                        kc, m = steps[n]
                        r0, rn = maps[m]
                        pb = pctr[0] % 2
                        kk_ = k2_s[sl] if (is_da and m == 1) else k_s[sl]
                        S.op("pe", lambda e, kc=kc, pb=pb, kk_=kk_: e.matmul(
                            PS[pb][:, 0:W], kk_[:, kc * 128:(kc + 1) * 128], q_s[sl][:, t0:t0 + W],
                            start=True, stop=True),
                            reads=[b_hd[sl]], writes=[PSB[pb]])
                        pctr[0] += 1
                        return pb

                    pend = score(0)
                    for n in range(len(steps)):
                        kc, m = steps[n]
                        pb = pend
                        if n + 1 < len(steps):
                            pend = score(n + 1)
                        u = n % 3
                        S.op("act", lambda e, pb=pb, u=u: e.activation(out=pT_s[u][:, 0:W], in_=PS[pb][:, 0:W], func=AF.Exp, scale=sc),
                             reads=[PSB[pb]], writes=[b_pT[u]])
                        first = (n < nm)
                        lastk = (n >= len(steps) - nm)
                        S.op("pe", lambda e, kc=kc, m=m, u=u, first=first, lastk=lastk: e.matmul(
                            PS[2 + m][:, 0:W], v3[:, kc, :], pT_s[u][:, 0:W], start=first, stop=lastk),
                            reads=[b_hd[sl], b_pT[u]], writes=[PSB[2 + m]])
                        ik = n // nm
                        if ik % 2 == 0:
                            if ik == 0:
                                S.op("dve", lambda e, m=m, u=u: e.tensor_copy(dacc_s[m][:, 0:W], pT_s[u][:, 0:W]),
                                     reads=[b_pT[u]], writes=[b_dacc[m]])
                            else:
                                S.op("dve", lambda e, m=m, u=u: e.tensor_tensor(dacc_s[m][:, 0:W], dacc_s[m][:, 0:W], pT_s[u][:, 0:W], ALU.add),
                                     reads=[b_pT[u]], writes=[b_dacc[m]])
                        else:
                            S.op("pe", lambda e, m=m, u=u, st_=(ik == 1): e.matmul(
                                PS[4 + m][:, 0:W], ones_bf, pT_s[u][:, 0:W], start=st_, stop=False),
                                reads=[b_c2, b_pT[u]], writes=[PSB[4 + m]])
                    for m in range(nm):
                        S.op("pe", lambda e, m=m: e.matmul(PS[4 + m][:, 0:W], ones_f, dacc_s[m][:, 0:W], start=False, stop=True),
                             reads=[b_c2, b_dacc[m]], writes=[PSB[4 + m]])
                    oc = octr[0] % 2
                    octr[0] += 1
                    for m in range(nm):
                        S.op("dve", lambda e, m=m: e.reciprocal(r_s[m][:, 0:W], PS[4 + m][:, 0:W]),
                             reads=[PSB[4 + m]], writes=[b_r[m]])
                    if is_da:
                        S.op("dve", lambda e: e.tensor_tensor(o_s[0][:, 0:W], PS[2][:, 0:W], r_s[0][:, 0:W], ALU.mult),
                             reads=[PSB[2], b_r[0]], writes=[b_o[0]])
                        S.op("dve", lambda e: e.scalar_tensor_tensor(o_s[1][:, 0:W], PS[3][:, 0:W], neglam, r_s[1][:, 0:W], ALU.mult, ALU.mult),
                             reads=[PSB[3], b_r[1], b_da], writes=[b_o[1]])
                        S.op("pool", lambda e: e.tensor_tensor(o_s[0][:, 0:W], o_s[0][:, 0:W], o_s[1][:, 0:W], ALU.add),
                             reads=[b_o[1]], writes=[b_o[0]])
                        S.op("act", lambda e: e.activation(out=osq_s[:, 0:W], in_=o_s[0][:, 0:W], func=AF.Square),
                             reads=[b_o[0]], writes=[b_osq])
                        S.op("pe", lambda e: e.matmul(PS[6][:, 0:W], ones_bf, osq_s[:, 0:W], start=True, stop=True),
                             reads=[b_osq, b_c2], writes=[PSB[6]])
                        S.op("act", lambda e: e.activation(out=rs_s[:, 0:W], in_=PS[6][:, 0:W], func=AF.Sqrt, scale=1.0 / 128.0, bias=eps_t),
                             reads=[PSB[6], b_c2], writes=[b_rs])
                        S.op("dve", lambda e: e.reciprocal(rs_s[:, 0:W], rs_s[:, 0:W]), reads=[b_rs], writes=[b_rs])
                        S.op("dve", lambda e, oc=oc: e.scalar_tensor_tensor(on_s[oc][:, 0:W], o_s[0][:, 0:W], subs, rs_s[:, 0:W], ALU.mult, ALU.mult),
                             reads=[b_o[0], b_rs, b_da], writes=[b_on[oc]])
                    else:
                        S.op("dve", lambda e, oc=oc: e.tensor_tensor(on_s[oc][:, 0:W], PS[2][:, 0:W], r_s[0][:, 0:W], ALU.mult),
                             reads=[PSB[2], b_r[0]], writes=[b_on[oc]])
                    S.dma("pool", sem_st[oc], aT[hq * 128:(hq + 1) * 128, t0:t0 + W], on_s[oc][:, 0:W], reads=[b_on[oc]])

            b_kz = Buf()
            if is_da:
                for sl_ in range(2):
                    S.op("pool", lambda e, sl_=sl_: e.memset(k_s[sl_][64:128, :], 0.0), writes=[b_kz])
                    S.op("pool", lambda e, sl_=sl_: e.memset(k2_s[sl_][0:64, :], 0.0), writes=[b_kz])
            load_head(0)
            for hq in range(NQ):
                if hq + 1 < NQ:
                    load_head(hq + 1)
                head(hq)
            S.barrier()

        def mixer_out_phase(i, wsrc, cvb):
            need_ctx = i < DEPTH - 1
            tiles = [t for t in range(len(TILES)) if need_ctx or not TILES[t][2]]
            AR.reset()
            xt_s = [AR.f32(8 * 512) for _ in range(2)]
            a_s = [AR.bf16(8 * 512) for _ in range(2)]
            ysb = AR.f32(8 * 512)
            sq_s = AR.bf16(8 * 512)
            rstd_s = AR.f32(512)
            w_s = AR.bf16(8 * 1024)
            b_xt = [Buf(), Buf()]
            b_a = [Buf(), Buf()]
            b_ysb, b_sq, b_rstd, b_w = Buf(), Buf(), Buf(), Buf()
            tag = "o%d" % i
            sem_x = [S.new_dma_sem("ox0_" + tag), S.new_dma_sem("ox1_" + tag)]
            sem_a = [S.new_dma_sem("oa0_" + tag), S.new_dma_sem("oa1_" + tag)]
            sem_w = S.new_dma_sem("ow_" + tag)
            sem_st = S.new_dma_sem("ost_" + tag)
            w3 = w_s.rearrange("p (k n) -> p k n", k=8)
            S.dma("sp", sem_w, w3, wsrc.rearrange("(k p) n -> p k n", p=128), reads=cvb, writes=[b_w])
            a_v = aT.rearrange("(k p) t -> p k t", p=128)

            def load(ti):
                t = tiles[ti]
                t0, W, isc = TILES[t]
                sl = ti % 2
                x3 = xt_s[sl].rearrange("p (k t) -> p k t", k=8)
                a3 = a_s[sl].rearrange("p (k t) -> p k t", k=8)
                S.dma("sp", sem_x[sl], x3[:, :, 0:W], xs_v[:, :, t0:t0 + W], reads=[XB[t]], writes=[b_xt[sl]])
                S.dma("sp", sem_a[sl], a3[:, :, 0:W], a_v[:, :, t0:t0 + W], writes=[b_a[sl]])

            def compute(ti):
                t = tiles[ti]
                t0, W, isc = TILES[t]
                sl = ti % 2
                A, Bs, G = scal(i, 1, 1 if isc else 0)
                x3 = xt_s[sl].rearrange("p (k t) -> p k t", k=8)
                a3 = a_s[sl].rearrange("p (k t) -> p k t", k=8)
                y3 = ysb.rearrange("p (k t) -> p k t", k=8)
                sq3 = sq_s.rearrange("p (k t) -> p k t", k=8)
                for dk in range(8):
                    pb = dk % 2
                    py = PS[pb]
                    for k in range(8):
                        S.op("pe", lambda e, k=k, dk=dk, py=py: e.matmul(
                            py[:, 0:W], w3[:, k, dk * 128:(dk + 1) * 128], a3[:, k, 0:W], start=(k == 0), stop=(k == 7)),
                            reads=[b_w, b_a[sl]], writes=[PSB[pb]])
                    S.op("act", lambda e, dk=dk, py=py: e.activation(out=sq3[:, dk, 0:W], in_=py[:, 0:W], func=AF.Square),
                         reads=[PSB[pb]], writes=[b_sq])
                    S.op("dve", lambda e, dk=dk, py=py: e.tensor_copy(y3[:, dk, 0:W], py[:, 0:W]),
                         reads=[PSB[pb]], writes=[b_ysb])
                rstd_from_sq(sq3[:, :, 0:W], W, 8, PS[4], PSB[4], rstd_s, b_rstd, b_sq, float(D))
                for k in range(8):
                    S.op("dve", lambda e, k=k: e.scalar_tensor_tensor(
                        y3[:, k, 0:W], y3[:, k, 0:W], G(k), rstd_s[:, 0:W], ALU.mult, ALU.mult),
                        reads=[b_rstd, b_mod], writes=[b_ysb])
                S.op("pool", lambda e: e.tensor_tensor(y3[:, :, 0:W], y3[:, :, 0:W], x3[:, :, 0:W], ALU.add),
                     reads=[b_xt[sl], b_ysb], writes=[b_ysb])
                S.dma("pool", sem_st, xs_v[:, :, t0:t0 + W], y3[:, :, 0:W], reads=[b_ysb], writes=[XB[t]])

            n = len(tiles)
            load(0)
            for ti in range(n):
                if ti + 1 < n:
                    load(ti + 1)
                compute(ti)
            S.barrier()

        NCH = T // 64
        NBL = T // 128

        def hgrn_proj_phase(i, j):
            cvb = conv_bufs["hgin%d" % j]
            AR.reset()
            xt_s = AR.f32(8 * 512)
            sq_s = AR.bf16(8 * 512)
            rstd_s = AR.f32(512)
            t1_s = [AR.f32(512) for _ in range(2)]
            h_s = [AR.bf16(8 * 512) for _ in range(2)]
            w_s = AR.bf16(8 * 5120)
            stg = [AR.f32(8 * 512) for _ in range(2)]
            v_s = AR.bf16(4 * 1024)
            sg_s = [AR.f32(512) for _ in range(2)]
            b_xt, b_sq, b_rstd, b_w, b_v = Buf(), Buf(), Buf(), Buf(), Buf()
            b_t1 = [Buf(), Buf()]
            b_h = [Buf(), Buf()]
            b_stg = [Buf(), Buf()]
            b_sg = [Buf(), Buf()]
            tag = "hp%d" % i
            sem_x = S.new_dma_sem("hx_" + tag)
            sem_w = S.new_dma_sem("hw_" + tag)
            sem_st = [S.new_dma_sem("hst0_" + tag), S.new_dma_sem("hst1_" + tag)]
            sem_sv = S.new_dma_sem("hsv_" + tag)
            w3 = w_s.rearrange("p (k n) -> p k n", k=8)
            wv = hg_in_b[j].rearrange("(k p) n -> p k n", p=128)
            for a in range(0, 5120, 1024):
                S.dma("sp", sem_w, w3[:, :, a:a + 1024], wv[:, :, a:a + 1024], reads=cvb, writes=[b_w])
            v_v = vtok.rearrange("(tb p) e -> p tb e", p=128)
            dst_f32 = [hq_T, hlf_T[0], hkk_T[0], hlf_T[1], hkk_T[1]]
            n = len(TILES)
            cnt = [0]
            scnt = [0]

            def load_x(ti):
                t0, W, isc = TILES[ti]
                x3 = xt_s.rearrange("p (k t) -> p k t", k=8)
                S.dma("sp", sem_x, x3[:, :, 0:W], xs_v[:, :, t0:t0 + W], reads=[XB[ti]], writes=[b_xt])

            def prenorm(ti):
                t0, W, isc = TILES[ti]
                sl = ti % 2
                A, Bs, G = scal(i, 1, 1 if isc else 0)
                x3 = xt_s.rearrange("p (k t) -> p k t", k=8)
                sq3 = sq_s.rearrange("p (k t) -> p k t", k=8)
                h3 = h_s[sl].rearrange("p (k t) -> p k t", k=8)
                S.op("act", lambda e: e.activation(out=sq3[:, :, 0:W], in_=x3[:, :, 0:W], func=AF.Square),
                     reads=[b_xt], writes=[b_sq])
                rstd_from_sq(sq3[:, :, 0:W], W, 8, PS[7], PSB[7], rstd_s, b_rstd, b_sq, float(D))
                for k in range(8):
                    q = k % 2
                    S.op("dve", lambda e, k=k, q=q: e.scalar_tensor_tensor(
                        t1_s[q][:, 0:W], x3[:, k, 0:W], A(k), rstd_s[:, 0:W], ALU.mult, ALU.mult),
                        reads=[b_xt, b_rstd, b_mod], writes=[b_t1[q]])
                    S.op("act", lambda e, k=k, q=q: e.activation(
                        out=h3[:, k, 0:W], in_=t1_s[q][:, 0:W], func=AF.Identity, bias=Bs(k), scale=1.0),
                        reads=[b_t1[q], b_mod], writes=[b_h[sl]])

            def proj_chunk(ti, col0):
                t0, W, isc = TILES[ti]
                sl = ti % 2
                h3 = h_s[sl].rearrange("p (k t) -> p k t", k=8)
                u = cnt[0] % 4
                cnt[0] += 1
                ps = PS[u]
                for k in range(8):
                    S.op("pe", lambda e, k=k, ps=ps: e.matmul(
                        ps[:, 0:W], w3[:, k, col0:col0 + 128], h3[:, k, 0:W], start=(k == 0), stop=(k == 7)),
                        reads=[b_w, b_h[sl]], writes=[PSB[u]])
                return u

            def project(ti):
                t0, W, isc = TILES[ti]
                sl = ti % 2
                h3 = h_s[sl].rearrange("p (k t) -> p k t", k=8)
                g = scnt[0] % 2
                scnt[0] += 1
                s3 = stg[g].rearrange("p (k t) -> p k t", k=8)
                for hh in range(8):
                    u = proj_chunk(ti, hh * 128)
                    S.op("act", lambda e, hh=hh, u=u, s3=s3: e.activation(out=s3[:, hh, 0:W], in_=PS[u][:, 0:W], func=AF.Silu),
                         reads=[PSB[u]], writes=[b_stg[g]])
                S.dma("pool", sem_st[g], hq_T.rearrange("(k p) t -> p k t", p=128)[:, :, t0:t0 + W], s3[:, :, 0:W], reads=[b_stg[g]])
                for d in range(2):
                    g1 = scnt[0] % 2
                    g2 = (scnt[0] + 1) % 2
                    scnt[0] += 2
                    l3 = stg[g1].rearrange("p (k t) -> p k t", k=8)
                    k3 = stg[g2].rearrange("p (k t) -> p k t", k=8)
                    for hh in range(8):
                        u = proj_chunk(ti, (1 + d) * 1024 + hh * 128)
                        q = hh % 2
                        S.op("act", lambda e, u=u, q=q: e.activation(out=sg_s[q][:, 0:W], in_=PS[u][:, 0:W], func=AF.Sigmoid),
                             reads=[PSB[u]], writes=[b_sg[q]])
                        S.op("dve", lambda e, q=q, d=d, hh=hh: e.tensor_scalar(
                            sg_s[q][:, 0:W], sg_s[q][:, 0:W], hg_oml[:, d * 8 + hh:d * 8 + hh + 1], hg_lb[:, d * 8 + hh:d * 8 + hh + 1],
                            ALU.mult, ALU.add),
                            reads=[b_hg], writes=[b_sg[q]])
                        S.op("act", lambda e, q=q, hh=hh, l3=l3: e.activation(out=l3[:, hh, 0:W], in_=sg_s[q][:, 0:W], func=AF.Ln),
                             reads=[b_sg[q]], writes=[b_stg[g1]])
                        S.op("dve", lambda e, q=q, hh=hh, k3=k3: e.tensor_scalar(
                            k3[:, hh, 0:W], sg_s[q][:, 0:W], -1.0, 1.0, ALU.mult, ALU.add),
                            reads=[b_sg[q]], writes=[b_stg[g2]])
                    S.dma("pool", sem_st[g1], hlf_T[d].rearrange("(k p) t -> p k t", p=128)[:, :, t0:t0 + W], l3[:, :, 0:W], reads=[b_stg[g1]])
                    S.dma("pool", sem_st[g2], hkk_T[d].rearrange("(k p) t -> p k t", p=128)[:, :, t0:t0 + W], k3[:, :, 0:W], reads=[b_stg[g2]])
                g = scnt[0] % 2
                scnt[0] += 1
                gb3 = stg[g].bitcast(BF16)[:, 0:8 * 512].rearrange("p (k t) -> p k t", k=8)
                for hh in range(8):
                    u = proj_chunk(ti, 4096 + hh * 128)
                    S.op("act", lambda e, hh=hh, u=u, gb3=gb3: e.activation(out=gb3[:, hh, 0:W], in_=PS[u][:, 0:W], func=AF.Silu),
                         reads=[PSB[u]], writes=[b_stg[g]])
                S.dma("pool", sem_st[g], qkT.rearrange("(c p) t -> p c t", p=128)[:, 0:8, t0:t0 + W], gb3[:, :, 0:W], reads=[b_stg[g]])
                ntb = W // 128
                v3 = v_s.rearrange("p (tb e) -> p tb e", tb=4)
                for tb in range(ntb):
                    for c0 in range(0, 1024, 512):
                        u = cnt[0] % 4
                        cnt[0] += 1
                        ps = PS[u]
                        for k in range(8):
                            S.op("pe", lambda e, k=k, tb=tb, c0=c0, ps=ps: e.matmul(
                                ps[:, 0:512], h3[:, k, tb * 128:(tb + 1) * 128], w3[:, k, 3072 + c0:3072 + c0 + 512],
                                start=(k == 0), stop=(k == 7)),
                                reads=[b_w, b_h[sl]], writes=[PSB[u]])
                        S.op("dve", lambda e, tb=tb, c0=c0, ps=ps: e.tensor_copy(v3[:, tb, c0:c0 + 512], ps[:, 0:512]),
                             reads=[PSB[u]], writes=[b_v])
                S.dma("pool", sem_sv, v_v[:, t0 // 128:t0 // 128 + ntb, :], v3[:, 0:ntb, :], reads=[b_v])

            load_x(0)
            prenorm(0)
            for ti in range(n):
                if ti + 1 < n:
                    load_x(ti + 1)
                project(ti)
                if ti + 1 < n:
                    prenorm(ti + 1)
            S.barrier()

        def hgrn_scan_phase(i, j):
            AR.reset()
            bufA = AR.f32(T)
            bufB = AR.f32(T)
            bufC = AR.f32(T)
            bufQ = AR.f32(T)
            v_s = AR.bf16(NBL * 128)
            gs_s = AR.bf16(T)
            qt_s = AR.bf16(T)
            qh_s = AR.bf16(T)
            kt_s = AR.bf16(T)
            kh_s = AR.bf16(T)
            gm_s = AR.f32(NCH)
            ktok_s = AR.bf16(NBL * 128)
            sbef_s = AR.bf16(NCH * 128)
            st_s = [AR.f32(128) for _ in range(2)]
            att_s = [AR.bf16(512) for _ in range(2)]
            attf_s = AR.f32(512)
            b_attf = Buf()
            oacc = AR.f32(T)
            gl_s = AR.f32(NCH)
            dec_s = AR.f32(NCH)
            mask_s = AR.f32(512)
            msk_s = [AR.bf16(512), AR.bf16(512)]
            osq_s = AR.bf16(512)
            rs_s = AR.f32(512)
            on_s = [AR.bf16(512) for _ in range(2)]
            bA, bB, bC, bQ, bV, bG = Buf(), Buf(), Buf(), Buf(), Buf(), Buf()
            b_qt, b_qh, b_kt, b_ktok, b_sbef = Buf(), Buf(), Buf(), Buf(), Buf()
            b_kh, b_gm = Buf(), Buf()
            b_st = [Buf(), Buf()]
            b_att = [Buf(), Buf()]
            b_oacc, b_gl, b_dec, b_mask = Buf(), Buf(), Buf(), Buf()
            b_osq, b_rs = Buf(), Buf()
            b_on = [Buf(), Buf()]
            tag = "hs%d" % i
            semA, semB, semQ, semV, semG = (S.new_dma_sem(n_ + tag) for n_ in ("hA", "hB", "hQ", "hV", "hG"))
            sem_o = [S.new_dma_sem("ho0" + tag), S.new_dma_sem("ho1" + tag)]
            semM = S.new_dma_sem("hM" + tag)
            v_v = vtok.rearrange("(c p) e -> p c e", p=128)
            S.op("dve", lambda e: e.memset(mask_s, 1.0), writes=[b_mask])
            S.op("dve", lambda e: e.memset(mask_s.rearrange("p (c j) -> p c j", j=64)[:, :, 0:1], 0.0), writes=[b_mask])
            mtmp = AR.f32(256)
            b_mt = Buf()
            S.dma("sp", semM, mtmp, hmask[:, :], writes=[b_mt])
            for dd in range(2):
                for r in range(4):
                    S.op("dve", lambda e, dd=dd, r=r: e.tensor_copy(msk_s[dd][:, r * 128:(r + 1) * 128], mtmp[:, dd * 128:(dd + 1) * 128]),
                         reads=[b_mt], writes=[b_mask])
            ord_f = [64, 65, 66, 67] + list(range(64))
            ord_b = [67, 66, 65, 64] + list(range(63, -1, -1))
            A3 = bufA.rearrange("p (c j) -> p c j", j=64)
            C3 = bufC.rearrange("p (c j) -> p c j", j=64)
            glb = gl_s.unsqueeze(2).to_broadcast([128, NCH, 64])
            pctr = [0]
            octr = [0]

            def direction(hh, d):
                S.dma("sp", semA, bufA, hlf_T[d][hh * 128:(hh + 1) * 128, :], writes=[bA])
                S.dma("sp", semB, bufB, hkk_T[d][hh * 128:(hh + 1) * 128, :], writes=[bB])
                for (t0, W, isc) in TILES:
                    S.op("dve", lambda e, t0=t0, W=W: e.tensor_tensor_scan(
                        bufC[:, t0:t0 + W], mask_s[:, 0:W], bufA[:, t0:t0 + W], 0.0, ALU.mult, ALU.add),
                        reads=[bA, b_mask], writes=[bC])
                S.op("dve", lambda e: e.tensor_copy(gl_s.unsqueeze(2), C3[:, :, 63:64]), reads=[bC], writes=[b_gl])
                S.op("act", lambda e: e.activation(out=dec_s, in_=gl_s, func=AF.Exp), reads=[b_gl], writes=[b_dec])
                if d == 0:
                    S.op("dve", lambda e: e.tensor_tensor(A3, C3, glb, ALU.subtract), reads=[bC, b_gl], writes=[bA])
                else:
                    S.op("dve", lambda e: e.tensor_tensor(bufA, bufA, bufC, ALU.subtract), reads=[bC], writes=[bA])
                    S.op("dve", lambda e: e.tensor_tensor(C3, A3, glb, ALU.add), reads=[bA, b_gl], writes=[bC])
                S.op("act", lambda e: e.activation(out=bufC, in_=bufC, func=AF.Exp), reads=[], writes=[bC])
                S.op("dve", lambda e: e.tensor_tensor(qt_s, bufQ, bufC, ALU.mult), reads=[bQ, bC], writes=[b_qt])
                S.op("act", lambda e: e.activation(out=bufC, in_=bufA, func=AF.Exp, scale=-1.0), reads=[bA], writes=[bC])
                S.op("dve", lambda e: e.tensor_tensor(kt_s, bufB, bufC, ALU.mult), reads=[bB, bC], writes=[b_kt])
                S.op("dve", lambda e: e.tensor_copy(gm_s.unsqueeze(2), A3[:, :, 32:33]), reads=[bA], writes=[b_gm])
                S.op("dve", lambda e: e.tensor_tensor(A3, A3, gm_s.unsqueeze(2).to_broadcast([128, NCH, 64]), ALU.subtract),
                     reads=[b_gm], writes=[bA])
                S.op("dve", lambda e: e.tensor_scalar(bufA, bufA, -80.0, 80.0, ALU.max, ALU.min), reads=[], writes=[bA])
                S.op("act", lambda e: e.activation(out=bufC, in_=bufA, func=AF.Exp), reads=[bA], writes=[bC])
                S.op("dve", lambda e: e.tensor_tensor(qh_s, bufQ, bufC, ALU.mult), reads=[bQ, bC], writes=[b_qh])
                S.op("act", lambda e: e.activation(out=bufC, in_=bufA, func=AF.Exp, scale=-1.0), reads=[bA], writes=[bC])
                S.op("dve", lambda e: e.tensor_tensor(kh_s, bufB, bufC, ALU.mult), reads=[bB, bC], writes=[b_kh])
                kt3 = ktok_s.rearrange("p (b k) -> p b k", b=NBL)
                for b0 in range(0, NBL, 8):
                    b1 = min(NBL, b0 + 8)
                    pb = 6 + (pctr[0] % 2)
                    pctr[0] += 1
                    pbf = PS[pb][:, :].bitcast(BF16)
                    for b in range(b0, b1):
                        S.op("pe", lambda e, b=b, b0=b0, pbf=pbf: e.transpose(
                            pbf[:, (b - b0) * 128:(b - b0 + 1) * 128], kt_s[:, b * 128:(b + 1) * 128], ident_bf),
                            reads=[b_kt, b_c2], writes=[PSB[pb]])
                    S.op("act", lambda e, b0=b0, b1=b1, pbf=pbf: e.activation(
                        out=ktok_s[:, b0 * 128:b1 * 128], in_=pbf[:, 0:(b1 - b0) * 128], func=AF.Copy),
                        reads=[PSB[pb]], writes=[b_ktok])
                order = ord_f if d == 0 else ord_b
                v3 = v_s.rearrange("p (b e) -> p b e", b=NBL)
                sb3 = sbef_s.rearrange("p (c v) -> p c v", c=NCH)
                cur = None
                for g0 in range(0, NCH, 8):
                    grp = order[g0:g0 + 8]
                    slot = {}
                    used = [0, 0]
                    for c in grp:
                        half = c % 2
                        slot[c] = (4 + half, used[half])
                        used[half] += 1
                    for c in grp:
                        blk, half = c // 2, c % 2
                        pb, gi = slot[c]
                        S.op("pe", lambda e, gi=gi, blk=blk, half=half, pb=pb: e.matmul(
                            PS[pb][:, gi * 128:(gi + 1) * 128], kt3[half * 64:(half + 1) * 64, blk, :], v3[half * 64:(half + 1) * 64, blk, :],
                            start=True, stop=True),
                            reads=[b_ktok, bV], writes=[PSB[pb]])
                    for gj, c in enumerate(grp):
                        pos = g0 + gj
                        nxt = pos % 2
                        pb, gi = slot[c]
                        if pos == 0:
                            S.op("dve", lambda e, gi=gi, pb=pb, nxt=nxt: e.tensor_copy(st_s[nxt], PS[pb][:, gi * 128:(gi + 1) * 128]),
                                 reads=[PSB[pb]], writes=[b_st[nxt]])
                        else:
                            prv = 1 - nxt
                            S.op("act", lambda e, c=c, prv=prv: e.activation(out=sb3[:, c, :], in_=st_s[prv], func=AF.Copy),
                                 reads=[b_st[prv]], writes=[b_sbef])
                            if pos < NCH - 1:
                                S.op("dve", lambda e, gi=gi, pb=pb, c=c, nxt=nxt, prv=prv: e.scalar_tensor_tensor(
                                    st_s[nxt], st_s[prv], dec_s[:, c:c + 1], PS[pb][:, gi * 128:(gi + 1) * 128], ALU.mult, ALU.add),
                                    reads=[b_st[prv], PSB[pb], b_dec], writes=[b_st[nxt]])
                first_c = order[0]
                for q0 in range(0, NBL, 4):
                    q1 = min(NBL, q0 + 4)
                    nb = q1 - q0
                    pa = pctr[0] % 2
                    po = 2 + (pctr[0] % 2)
                    pctr[0] += 1
                    au = pa
                    for b in range(q0, q1):
                        S.op("pe", lambda e, b=b, q0=q0, pa=pa: e.matmul(
                            PS[pa][:, (b - q0) * 128:(b - q0 + 1) * 128], kh_s[:, b * 128:(b + 1) * 128], qh_s[:, b * 128:(b + 1) * 128],
                            start=True, stop=True),
                            reads=[b_kh, b_qh], writes=[PSB[pa]])
                    S.op("dve", lambda e, nb=nb, pa=pa: e.tensor_scalar(
                        attf_s[:, 0:nb * 128], PS[pa][:, 0:nb * 128], -3.0e38, 3.0e38, ALU.max, ALU.min),
                        reads=[PSB[pa]], writes=[b_attf])
                    S.op("dve", lambda e, nb=nb, au=au, d=d: e.tensor_tensor(
                        att_s[au][:, 0:nb * 128], attf_s[:, 0:nb * 128], msk_s[d][:, 0:nb * 128], ALU.mult),
                        reads=[b_attf, b_mask], writes=[b_att[au]])
                    for b in range(q0, q1):
                        o0 = (b - q0) * 128
                        halves = [hf for hf in range(2) if 2 * b + hf != first_c]
                        S.op("pe", lambda e, b=b, o0=o0, po=po, au=au, halves=halves: e.matmul(
                            PS[po][:, o0:o0 + 128], v3[:, b, :], att_s[au][:, o0:o0 + 128], start=True, stop=(len(halves) == 0)),
                            reads=[bV, b_att[au]], writes=[PSB[po]])
                        for half in halves:
                            c = 2 * b + half
                            S.op("pe", lambda e, c=c, o0=o0, half=half, po=po, lasth=(half == halves[-1]): e.matmul(
                                PS[po][:, o0 + half * 64:o0 + half * 64 + 64], sb3[:, c, :], qt_s[:, c * 64:(c + 1) * 64],
                                start=False, stop=lasth),
                                reads=[b_sbef, b_qt], writes=[PSB[po]])
                    tt0 = q0 * 128
                    if d == 0:
                        S.op("act", lambda e, tt0=tt0, nb=nb, po=po: e.activation(
                            out=oacc[:, tt0:tt0 + nb * 128], in_=PS[po][:, 0:nb * 128], func=AF.Copy),
                            reads=[PSB[po]], writes=[b_oacc])
                    else:
                        S.op("dve", lambda e, tt0=tt0, nb=nb, po=po: e.tensor_tensor(
                            oacc[:, tt0:tt0 + nb * 128], oacc[:, tt0:tt0 + nb * 128], PS[po][:, 0:nb * 128], ALU.add),
                            reads=[PSB[po]], writes=[b_oacc])

            def finish_head(hh):
                gnorm = hgn_s[:, 0:1]
                for (t0, W, isc) in TILES:
                    oc = octr[0] % 2
                    octr[0] += 1
                    S.op("act", lambda e, t0=t0, W=W: e.activation(out=osq_s[:, 0:W], in_=oacc[:, t0:t0 + W], func=AF.Square),
                         reads=[b_oacc], writes=[b_osq])
                    S.op("pe", lambda e, W=W: e.matmul(PS[6][:, 0:W], ones_bf, osq_s[:, 0:W], start=True, stop=True),
                         reads=[b_osq, b_c2], writes=[PSB[6]])
                    S.op("act", lambda e, W=W: e.activation(out=rs_s[:, 0:W], in_=PS[6][:, 0:W], func=AF.Sqrt, scale=1.0 / 128.0, bias=eps_t),
                         reads=[PSB[6], b_c2], writes=[b_rs])
                    S.op("dve", lambda e, W=W: e.reciprocal(rs_s[:, 0:W], rs_s[:, 0:W]), reads=[b_rs], writes=[b_rs])
                    S.op("dve", lambda e, t0=t0, W=W: e.scalar_tensor_tensor(
                        rs_s[:, 0:W], oacc[:, t0:t0 + W], gnorm, rs_s[:, 0:W], ALU.mult, ALU.mult),
                        reads=[b_oacc, b_hg], writes=[b_rs])
                    S.op("dve", lambda e, t0=t0, W=W, oc=oc: e.tensor_tensor(on_s[oc][:, 0:W], rs_s[:, 0:W], gs_s[:, t0:t0 + W], ALU.mult),
                         reads=[b_rs, bG], writes=[b_on[oc]])
                    S.dma("pool", sem_o[oc], aT[hh * 128:(hh + 1) * 128, t0:t0 + W], on_s[oc][:, 0:W], reads=[b_on[oc]])

            for hh in range(8):
                S.dma("sp", semQ, bufQ, hq_T[hh * 128:(hh + 1) * 128, :], writes=[bQ])
                vd = v_s.rearrange("p (c e) -> p c e", c=NBL)
                for c0 in range(0, NBL, 8):
                    c1 = min(NBL, c0 + 8)
                    S.dma("sp", semV, vd[:, c0:c1, :], v_v[:, c0:c1, hh * 128:(hh + 1) * 128], writes=[bV])
                S.dma("sp", semG, gs_s, qkT[hh * 128:(hh + 1) * 128, :], writes=[bG])
                direction(hh, 0)
                direction(hh, 1)
                finish_head(hh)
            S.barrier()

        b_qkT, b_vtok, b_aT = Buf("qkT"), Buf("vtok"), Buf("aT")

        def mixer(i):
            kind, j = i % 3, i // 3
            pump_until(*mixer_names(i))
            if kind == 0:
                attn_qkv_phase(i, 0, j)
                attn_core_phase(i, 0, j)
                mixer_out_phase(i, da_o_b[j], conv_bufs["dao%d" % j])
            elif kind == 2:
                attn_qkv_phase(i, 2, j)
                attn_core_phase(i, 2, j)
                mixer_out_phase(i, gqa_o_b[j], conv_bufs["gqao%d" % j])
            else:
                hgrn_proj_phase(i, j)
                if stop == "hp":
                    return
                hgrn_scan_phase(i, j)
                if stop == "hs":
                    return
                mixer_out_phase(i, hg_o_b[j], conv_bufs["hgo%d" % j])

        done = False
        start_layer_mixer = 0
        for i in range(start_layer, DEPTH):
            ffn_phase(i, 0)
            if stop == "f%d0" % i:
                dump_and_finish()
                done = True
                break
            if i >= start_layer_mixer:
                mixer(i)
            if stop == "m%d" % i or stop in ("hp", "hs"):
                dump_and_finish()
                done = True
                break
            ffn_phase(i, 1, last=(i == DEPTH - 1))
            if stop == "f%d1" % i and i < DEPTH - 1:
                dump_and_finish()
                done = True
                break
        S.barrier()
        S.emit()
    return nc


def _feat_major(v):
    v = np.asarray(v, dtype=np.float32)
    lead = v.shape[:-1]
    r = v.reshape(lead + (8, 128))
    r = np.moveaxis(r, -1, 0)
    return np.ascontiguousarray(r)


def _consts():
    ident = np.eye(128, dtype=np.float32)
    rot = np.zeros((128, 128), np.float32)
    for j in range(64):
        rot[2 * j + 1, 2 * j] = -1.0
        rot[2 * j, 2 * j + 1] = 1.0
    tril = np.zeros((128, 128), np.float32)
    return np.ascontiguousarray(np.concatenate([ident, rot, tril], axis=1))


def _rope_table(head_dim):
    pairs = head_dim // 4
    inv_freq = np.power(np.float32(10000.0), -np.arange(pairs, dtype=np.float32) / np.float32(pairs)).astype(np.float32)
    rows = TL // GRID_W
    r = np.repeat(np.arange(rows, dtype=np.float32), GRID_W)
    col = np.tile(np.arange(GRID_W, dtype=np.float32), rows)
    ang = np.concatenate([r[:, None] * inv_freq, col[:, None] * inv_freq], axis=-1).astype(np.float32)
    p = np.arange(128)
    a = (p % head_dim) // 2
    tab = ang[:, a].T
    return np.ascontiguousarray(np.stack([np.cos(tab), np.sin(tab)], axis=0).astype(np.float32))


def _hmask():
    m = np.zeros((2, 128, 128), np.float32)
    for blk in range(2):
        for s_ in range(64):
            for t_ in range(64):
                if s_ <= t_:
                    m[0, blk * 64 + s_, blk * 64 + t_] = 1.0
                if s_ >= t_:
                    m[1, blk * 64 + s_, blk * 64 + t_] = 1.0
    return np.ascontiguousarray(np.concatenate([m[0], m[1]], axis=1))


_PROG = {}


def prepare_inputs(inputs):
    x = np.asarray(inputs["x"], np.float32)
    ctx = np.asarray(inputs["ctx"], np.float32)
    c = np.asarray(inputs["c"], np.float32)
    c_ctx = np.asarray(inputs["c_ctx"], np.float32)
    B = x.shape[0]
    shared = {
        "w_mod": np.ascontiguousarray(inputs["w_mod"], dtype=np.float32),
        "bmodT": _feat_major(np.asarray(inputs["b_mod"]).reshape(DEPTH, 9, D)).reshape(128, DEPTH * 72),
        "normT": _feat_major(np.asarray(inputs["norm_g"])).reshape(128, DEPTH * 48),
        "ffn_w_in": np.ascontiguousarray(inputs["ffn_w_in"], dtype=np.float32),
        "ffn_w_out": np.ascontiguousarray(inputs["ffn_w_out"], dtype=np.float32),
        "consts": _consts(),
        "da_w_qkv": np.ascontiguousarray(inputs["da_w_qkv"], dtype=np.float32),
        "da_w_o": np.ascontiguousarray(inputs["da_w_o"], dtype=np.float32),
        "dal": np.ascontiguousarray(np.broadcast_to(np.asarray(inputs["da_lambda"], np.float32).reshape(1, 512), (128, 512))),
        "dasub": np.ascontiguousarray(np.asarray(inputs["da_subln"], np.float32).T),
        "ropeD": _rope_table(64),
        "gqa_w_qkv": np.ascontiguousarray(inputs["gqa_w_qkv"], dtype=np.float32),
        "gqa_w_o": np.ascontiguousarray(inputs["gqa_w_o"], dtype=np.float32),
        "gqn": np.ascontiguousarray(np.stack([np.asarray(inputs["gqa_q_norm"], np.float32)[0],
                                              np.asarray(inputs["gqa_k_norm"], np.float32)[0]], axis=1)),
        "ropeG": _rope_table(128),
        "hg_w_in": np.ascontiguousarray(inputs["hg_w_in"], dtype=np.float32),
        "hg_w_o": np.ascontiguousarray(inputs["hg_w_o"], dtype=np.float32),
        "hlbT": _feat_major(np.asarray(inputs["hg_lower_bound"], np.float32)).reshape(128, 64),
        "hgn": np.ascontiguousarray(np.asarray(inputs["hg_norm"], np.float32).reshape(128, 1)),
        "hmask": _hmask(),
    }
    in_maps = []
    for b in range(B):
        xT = np.ascontiguousarray(np.concatenate([x[b].T, ctx[b].T], axis=1))
        cc = np.stack([c[b], c_ctx], axis=0)
        ccT = _feat_major(cc)
        ccT = np.ascontiguousarray(np.transpose(ccT, (0, 2, 1))).reshape(128, 16)
        m = dict(shared)
        m["xT"] = xT
        m["ccT"] = ccT
        in_maps.append(m)
    return in_maps


def kernel(**inputs):
    stop = inputs.pop("_stop", None)
    cores = inputs.pop("_cores", None)
    start = inputs.pop("_start", 0)
    bsel = inputs.pop("_batch", None)
    in_maps = prepare_inputs(inputs)
    if bsel is not None:
        in_maps = [in_maps[bsel]]
    if cores is not None:
        in_maps = in_maps[:cores]
    key = (stop, start)
    if key not in _PROG:
        _PROG[key] = build_program(stop, start)
    nc = _PROG[key]
    res = run_bass_kernel_spmd(nc, in_maps, core_ids=list(range(len(in_maps))))
    out = np.stack([np.ascontiguousarray(np.asarray(r["yT"]).T) for r in res.results], axis=0)
    return out.astype(np.float32)
```

```python
import contextlib
import math
import numpy as np
import concourse.bass as bass
import concourse.mybir as mybir
from concourse.bass_utils import run_bass_kernel_spmd

F32 = mybir.dt.float32
BF16 = mybir.dt.bfloat16
AF = mybir.ActivationFunctionType
ALU = mybir.AluOpType

D = 1024
TL = 4096
TC = 256
T = TL + TC
DEPTH = 4
DFF = 2816
NJ = DFF // 128
EPS = 1e-6
GRID_W = 64
import os
DBG = int(os.environ.get('KDBG', '0'))


class Buf:
    __slots__ = ("name", "w", "r", "excl")

    def __init__(self, name="", excl=False):
        self.name = name
        self.w = None
        self.r = []
        self.excl = excl


class Sched:
    ENGS = ("pe", "act", "dve", "pool", "sp")

    def __init__(self, nc):
        self.nc = nc
        self.prog = {e: [] for e in self.ENGS}
        self.cnt = {}
        self.known = {e: {} for e in self.ENGS}
        for e in self.ENGS:
            self.cnt["E" + e] = 0
        self.all_dma = [[], []]
        self.free_dma = [[], []]
        self.persist = set()

    def _need(self, eng, deps):
        kn = self.known[eng]
        best = {}
        for d in deps:
            if d is None:
                continue
            k, v = d
            if kn.get(k, 0) >= v:
                continue
            if best.get(k, 0) < v:
                best[k] = v
        out = []
        for k, v in best.items():
            kn[k] = v
            out.append((k, v))
        return out

    def _deps(self, eng, reads, writes):
        deps = []
        me = "E" + eng
        pe = eng == "pe"
        for b in reads:
            if b.w is not None and not (pe and b.w[0] == me):
                deps.append(b.w)
        for b in writes:
            if b.w is not None and not (pe and b.w[0] == me):
                deps.append(b.w)
            for r in b.r:
                if r[0] != me:
                    deps.append(r)
        return deps

    def _commit(self, tok, reads, writes):
        for b in reads:
            b.r.append(tok)
            if len(b.r) > 64:
                best = {}
                for k, v in b.r:
                    if best.get(k, 0) < v:
                        best[k] = v
                b.r = list(best.items())
        for b in writes:
            b.w = tok
            b.r = []

    def op(self, eng, fn, reads=(), writes=()):
        if any(b.excl for b in reads):
            writes = list(writes) + [b for b in reads if b.excl]
            reads = [b for b in reads if not b.excl]
        waits = self._need(eng, self._deps(eng, reads, writes))
        key = "E" + eng
        self.cnt[key] += 1
        tok = (key, self.cnt[key])
        self.prog[eng].append((waits, fn, key, 1))
        self._commit(tok, reads, writes)
        return tok

    def new_dma_sem(self, name):
        return [None, name.startswith("cv")]

    def _bind(self, sem, eng):
        if sem[0] is None:
            q = 1 if eng == "pool" else 0
            if self.free_dma[q]:
                sem[0] = self.free_dma[q].pop()
            else:
                key = "%s%d" % ("PQ"[q], len(self.all_dma[q]))
                self.all_dma[q].append(key)
                self.cnt[key] = 0
                sem[0] = key
            if sem[1]:
                self.persist.add(sem[0])
        return sem[0]

    def dma(self, eng, sem, out, in_, reads=(), writes=()):
        sem = self._bind(sem, eng)
        waits = self._need(eng, [d for d in self._deps(eng, reads, writes) if d[0] != sem])
        self.cnt[sem] += 16
        tok = (sem, self.cnt[sem])
        self.prog[eng].append((waits, lambda e: e.dma_start(out=out, in_=in_), sem, 16))
        self._commit(tok, reads, writes)
        return tok

    def wait_all(self, eng, toks):
        waits = self._need(eng, toks)
        if waits:
            self.prog[eng].append((waits, None, None, 0))

    def barrier(self):
        toks = [(k, v) for k, v in self.cnt.items() if v > 0]
        for e in self.ENGS:
            self.wait_all(e, [t for t in toks if t[0] != "E" + e])
        self.free_dma = [[k for k in reversed(self.all_dma[q]) if k not in self.persist] for q in range(2)]

    def emit(self):
        nc = self.nc
        with contextlib.ExitStack() as st:
            sems = {k: st.enter_context(nc.semaphore(k)) for k in self.cnt}
            block = st.enter_context(nc.Block())
            hmap = {"pe": block.tensor, "act": block.scalar, "dve": block.vector,
                    "pool": block.gpsimd, "sp": block.sync}
            for e in self.ENGS:
                prog = self.prog[e]
                if not prog:
                    continue

                def body(h, prog=prog):
                    for waits, fn, key, inc in prog:
                        for (k, v) in waits:
                            h.wait_ge(sems[k], v)
                        if fn is not None:
                            fn(h).then_inc(sems[key], inc)
                hmap[e](body)


class Arena:
    def __init__(self, ap):
        self.ap = ap
        self.n = ap.shape[1]
        self.base = 0
        self.off = 0

    def freeze(self):
        self.base = self.off

    def reset(self):
        self.off = self.base

    def f32(self, n):
        o = self.off
        self.off += n
        assert self.off <= self.n, "arena overflow %d > %d" % (self.off, self.n)
        return self.ap[:, o:o + n]

    def bf16(self, n):
        w = (n + 1) // 2
        return self.f32(w).bitcast(BF16)[:, 0:n]


TILES = [(i * 512, 512, False) for i in range(8)] + [(TL, TC, True)]


def build_program(stop=None, start_layer=0):
    nc = bass.Bass("TRN2", target_bir_lowering=False)

    def din(name, shape, dt=F32):
        return nc.dram_tensor(name, list(shape), dt, kind="ExternalInput").ap()

    def dscr(name, shape, dt):
        return nc.dram_tensor(name, list(shape), dt).ap()

    xT_in = din("xT", [D, T])
    ccT = din("ccT", [128, 16])
    w_mod = din("w_mod", [DEPTH, D, 9 * D])
    bmodT = din("bmodT", [128, DEPTH * 72])
    normT = din("normT", [128, DEPTH * 48])
    ffn_w_in = din("ffn_w_in", [DEPTH, 2, D, 2 * DFF])
    ffn_w_out = din("ffn_w_out", [DEPTH, 2, DFF, D])
    consts = din("consts", [128, 3 * 128])
    da_w_qkv = din("da_w_qkv", [2, D, 3 * D])
    da_w_o = din("da_w_o", [2, D, D])
    dal = din("dal", [128, 2 * 256])
    dasub = din("dasub", [128, 2])
    ropeD = din("ropeD", [2, 128, TL])
    gqa_w_qkv = din("gqa_w_qkv", [1, D, 1536])
    gqa_w_o = din("gqa_w_o", [1, D, D])
    gqn = din("gqn", [128, 2])
    ropeG = din("ropeG", [2, 128, TL])
    hg_w_in = din("hg_w_in", [1, D, 5 * D])
    hg_w_o = din("hg_w_o", [1, D, D])
    hlbT = din("hlbT", [128, 64])
    hgn = din("hgn", [128, 1])
    hmask = din("hmask", [128, 256])
    yT = nc.dram_tensor("yT", [D, TL], F32, kind="ExternalOutput").ap()

    xs = dscr("xs", [D, T], F32)
    wi_b = dscr("wi_b", [DEPTH, 2, D, 2 * DFF], BF16)
    wo_b = dscr("wo_b", [DEPTH, 2, DFF, D], BF16)
    da_qkv_b = dscr("da_qkv_b", [2, D, 3 * D], BF16)
    da_o_b = dscr("da_o_b", [2, D, D], BF16)
    gqa_qkv_b = dscr("gqa_qkv_b", [1, D, 1536], BF16)
    gqa_o_b = dscr("gqa_o_b", [1, D, D], BF16)
    hg_in_b = dscr("hg_in_b", [1, D, 5 * D], BF16)
    hg_o_b = dscr("hg_o_b", [1, D, D], BF16)
    hq_T = dscr("hq_T", [D, T], F32)
    hlf_T = [dscr("hlf_T%d" % d_, [D, T], F32) for d_ in range(2)]
    hkk_T = [dscr("hkk_T%d" % d_, [D, T], F32) for d_ in range(2)]
    qkT = dscr("qkT", [16 * 128, T], BF16)
    vtok = dscr("vtok", [T, D], BF16)
    aT = dscr("aT", [D, T], BF16)

    with contextlib.ExitStack() as st:
        arena_t = st.enter_context(nc.sbuf_tensor("arena", [128, 50 * 1024], F32))
        PS = [st.enter_context(nc.psum_tensor("ps%d" % i, [128, 512], F32)) for i in range(8)]
        AR = Arena(arena_t[:, :])
        S = Sched(nc)
        PSB = [Buf("ps%d" % i, excl=True) for i in range(8)]

        ones_bf = AR.bf16(128)
        ones_f = AR.f32(128)
        ident_bf = AR.bf16(128)
        rotm_bf = AR.bf16(128)
        cst_f = AR.f32(384)
        eps_t = AR.f32(1)
        ccs = AR.f32(16)
        modL = AR.f32(DEPTH * 72)
        modC = AR.f32(DEPTH * 72)
        bmod_s = AR.f32(DEPTH * 72)
        norm_s = AR.f32(DEPTH * 48)
        SA = [AR.f32(DEPTH * 24), AR.f32(DEPTH * 24)]
        SG = [AR.f32(DEPTH * 24), AR.f32(DEPTH * 24)]
        MOD = [modL, modC]
        dal_s = AR.f32(512)
        dasub_s = AR.f32(2)
        gqn_s = AR.f32(2)
        da_neglam = AR.f32(2)
        da_subs = AR.f32(2)
        da_tmp = AR.f32(64)
        da_e = AR.f32(4)
        hlb_s = AR.f32(64)
        hgn_s = AR.f32(1)
        hg_lb = AR.f32(16)
        hg_oml = AR.f32(16)
        hg_sum = AR.f32(16)
        AR.freeze()
        b_const = Buf("const")
        b_mod = Buf("mod")

        conv_bufs = {}

        cv_inflight = [None, None]

        conv_jobs = []
        conv_pending = {}

        def convert(name, src, dst, rows):
            sems = [S.new_dma_sem("cv%d_%s" % (q, name)) for q in range(2)]
            bs = [Buf("cv%d_%s" % (q, name)) for q in range(2)]
            r0 = 0
            n = 0
            while r0 < rows:
                r1 = min(rows, r0 + 128)
                conv_jobs.append((name, n % 2, sems[n % 2], bs[n % 2], dst[r0:r1, :], src[r0:r1, :]))
                r0 = r1
                n += 1
            conv_pending[name] = n
            conv_bufs[name] = bs

        def pump(n):
            while n > 0 and conv_jobs:
                name, q, sem, b, dst, src = conv_jobs.pop(0)
                if cv_inflight[q] is not None:
                    S.wait_all("pool", [cv_inflight[q]])
                cv_inflight[q] = S.dma("pool", sem, dst, src, writes=[b])
                conv_pending[name] -= 1
                n -= 1

        def pump_until(*names):
            while any(conv_pending.get(nm, 0) > 0 for nm in names):
                pump(1)

        def convert_ffn(i, f):
            convert("wi%d%d" % (i, f), ffn_w_in[i, f], wi_b[i, f], D)
            convert("wo%d%d" % (i, f), ffn_w_out[i, f], wo_b[i, f], DFF)

        def convert_mixer(i):
            kind, j = i % 3, i // 3
            if kind == 0:
                convert("daqkv%d" % j, da_w_qkv[j], da_qkv_b[j], D)
                convert("dao%d" % j, da_w_o[j], da_o_b[j], D)
            elif kind == 2:
                convert("gqaqkv%d" % j, gqa_w_qkv[j], gqa_qkv_b[j], D)
                convert("gqao%d" % j, gqa_w_o[j], gqa_o_b[j], D)
            else:
                convert("hgin%d" % j, hg_w_in[j], hg_in_b[j], D)
                convert("hgo%d" % j, hg_w_o[j], hg_o_b[j], D)

        def mixer_names(i):
            kind, j = i % 3, i // 3
            return {0: ("daqkv%d" % j, "dao%d" % j), 1: ("hgin%d" % j, "hgo%d" % j), 2: ("gqaqkv%d" % j, "gqao%d" % j)}[kind]

        for li in range(start_layer, DEPTH):
            convert_ffn(li, 0)
            convert_mixer(li)
            convert_ffn(li, 1)
        pump_until("wi%d0" % start_layer, "wo%d0" % start_layer)
        pump(16)

        s_c0 = S.new_dma_sem("c0")
        S.dma("sp", s_c0, cst_f, consts[:, :], writes=[b_const])
        S.dma("sp", s_c0, ccs, ccT[:, :], writes=[b_const])
        S.dma("sp", s_c0, bmod_s, bmodT[:, :], writes=[b_const])
        S.dma("sp", s_c0, norm_s, normT[:, :], writes=[b_const])
        b_c2 = Buf("const2")
        S.op("dve", lambda e: e.memset(ones_bf, 1.0), writes=[b_c2])
        S.op("dve", lambda e: e.memset(ones_f, 1.0), writes=[b_c2])
        S.op("dve", lambda e: e.memset(eps_t, EPS), writes=[b_c2])
        S.op("dve", lambda e: e.tensor_copy(ident_bf, cst_f[:, 0:128]), reads=[b_const], writes=[b_c2])
        S.op("dve", lambda e: e.tensor_copy(rotm_bf, cst_f[:, 128:256]), reads=[b_const], writes=[b_c2])
        S.op("act", lambda e: e.activation(out=ccs, in_=ccs, func=AF.Silu), reads=[b_const], writes=[b_const])

        S.dma("sp", s_c0, dal_s, dal[:, :], writes=[b_const])
        S.dma("sp", s_c0, dasub_s, dasub[:, :], writes=[b_const])
        S.dma("sp", s_c0, gqn_s, gqn[:, :], writes=[b_const])
        b_da = Buf("da")
        for jj in range(2):
            li = 3 * jj
            lam_init = 0.8 - 0.6 * math.exp(-0.3 * li)
            for pair in range(2):
                o0 = jj * 256 + pair * 128
                S.op("dve", lambda e, o0=o0: e.tensor_tensor(da_tmp, dal_s[:, o0:o0 + 64], dal_s[:, o0 + 64:o0 + 128], ALU.mult),
                     reads=[b_const], writes=[b_da])
                S.op("dve", lambda e, jj=jj, pair=pair: e.tensor_reduce(
                    da_e[:, jj * 2 + pair:jj * 2 + pair + 1], da_tmp, mybir.AxisListType.X, ALU.add),
                    reads=[b_da], writes=[b_da])
            S.op("act", lambda e, jj=jj: e.activation(out=da_e[:, jj * 2:jj * 2 + 2], in_=da_e[:, jj * 2:jj * 2 + 2], func=AF.Exp),
                 reads=[b_da], writes=[b_da])
            S.op("dve", lambda e, jj=jj, lam_init=lam_init: e.scalar_tensor_tensor(
                da_neglam[:, jj:jj + 1], da_e[:, jj * 2 + 1:jj * 2 + 2], -lam_init, da_e[:, jj * 2:jj * 2 + 1], ALU.add, ALU.subtract),
                reads=[b_da], writes=[b_da])
            S.op("dve", lambda e, jj=jj, lam_init=lam_init: e.tensor_scalar(
                da_subs[:, jj:jj + 1], dasub_s[:, jj:jj + 1], 1.0 - lam_init, None, ALU.mult),
                reads=[b_const, b_da], writes=[b_da])

        S.dma("sp", s_c0, hlb_s, hlbT[:, :], writes=[b_const])
        S.dma("sp", s_c0, hgn_s, hgn[:, :], writes=[b_const])
        b_hg = Buf("hg")
        HL = 1
        S.op("act", lambda e: e.activation(out=hlb_s, in_=hlb_s, func=AF.Exp), reads=[b_const], writes=[b_const])
        hl4 = hlb_s.rearrange("p (d l k) -> p d l k", d=2, l=4)
        for dd in range(2):
            S.op("dve", lambda e, dd=dd: e.tensor_tensor(hg_sum[:, dd * 8:dd * 8 + 8], hl4[:, dd, 0, :], hl4[:, dd, 1, :], ALU.add),
                 reads=[b_const], writes=[b_hg])
            for l in (2, 3):
                S.op("dve", lambda e, dd=dd, l=l: e.tensor_tensor(hg_sum[:, dd * 8:dd * 8 + 8], hg_sum[:, dd * 8:dd * 8 + 8], hl4[:, dd, l, :], ALU.add),
                     reads=[b_const, b_hg], writes=[b_hg])
            S.op("dve", lambda e, dd=dd: e.tensor_copy(hg_lb[:, dd * 8:dd * 8 + 8], hl4[:, dd, 1, :]), reads=[b_const], writes=[b_hg])
            for l in range(2, HL + 1):
                S.op("dve", lambda e, dd=dd, l=l: e.tensor_tensor(hg_lb[:, dd * 8:dd * 8 + 8], hg_lb[:, dd * 8:dd * 8 + 8], hl4[:, dd, l, :], ALU.add),
                     reads=[b_const, b_hg], writes=[b_hg])
        S.op("dve", lambda e: e.reciprocal(hg_sum, hg_sum), reads=[b_hg], writes=[b_hg])
        S.op("dve", lambda e: e.tensor_tensor(hg_lb, hg_lb, hg_sum, ALU.mult), reads=[b_hg], writes=[b_hg])
        S.op("dve", lambda e: e.tensor_scalar(hg_oml, hg_lb, -1.0, 1.0, ALU.mult, ALU.add), reads=[b_hg], writes=[b_hg])

        wm_slots = [AR.f32(8 * 1024), AR.f32(8 * 1024)]
        wm_bufs = [Buf("wm0"), Buf("wm1")]
        wm_sems = [S.new_dma_sem("wm0"), S.new_dma_sem("wm1")]
        blk = 0
        for i in range(DEPTH):
            wv = w_mod[i].rearrange("(k p) n -> p k n", p=128)
            for m in range(9):
                sl = blk % 2
                wt = wm_slots[sl].rearrange("p (k n) -> p k n", k=8)
                S.dma("sp", wm_sems[sl], wt, wv[:, :, m * 1024:(m + 1) * 1024], writes=[wm_bufs[sl]])
                pb = blk % 2
                ps = PS[pb]
                for n in range(8):
                    for k in range(8):
                        S.op("pe", lambda e, ps=ps, wt=wt, n=n, k=k: e.matmul(
                            ps[:, n * 2:n * 2 + 2], wt[:, k, n * 128:(n + 1) * 128], ccs[:, 2 * k:2 * k + 2],
                            start=(k == 0), stop=(k == 7)),
                            reads=[wm_bufs[sl], b_const], writes=[PSB[pb]])
                ps3 = ps[:, 0:16].rearrange("p (n j) -> p n j", j=2)
                c0 = i * 72 + m * 8
                for j in range(2):
                    S.op("dve", lambda e, ps3=ps3, j=j, c0=c0: e.tensor_tensor(
                        MOD[j][:, c0:c0 + 8], ps3[:, :, j], bmod_s[:, c0:c0 + 8], ALU.add),
                        reads=[PSB[pb], b_const], writes=[b_mod])
                blk += 1
        for i in range(DEPTH):
            for s in range(3):
                wgt = 1.0 if s == 1 else 0.5
                c = (i * 3 + s) * 8
                gpre = norm_s[:, i * 48 + (2 * s) * 8: i * 48 + (2 * s) * 8 + 8]
                gpost = norm_s[:, i * 48 + (2 * s + 1) * 8: i * 48 + (2 * s + 1) * 8 + 8]
                for j in range(2):
                    scale = MOD[j][:, i * 72 + (3 * s + 1) * 8: i * 72 + (3 * s + 1) * 8 + 8]
                    gate = MOD[j][:, i * 72 + (3 * s + 2) * 8: i * 72 + (3 * s + 2) * 8 + 8]
                    S.op("dve", lambda e, j=j, c=c, scale=scale, gpre=gpre: e.scalar_tensor_tensor(
                        SA[j][:, c:c + 8], scale, 1.0, gpre, ALU.add, ALU.mult),
                        reads=[b_mod, b_const], writes=[b_mod])
                    S.op("dve", lambda e, j=j, c=c, gate=gate, gpost=gpost, wgt=wgt: e.scalar_tensor_tensor(
                        SG[j][:, c:c + 8], gate, wgt, gpost, ALU.mult, ALU.mult),
                        reads=[b_mod, b_const], writes=[b_mod])

        def scal(i, s, j):
            c = (i * 3 + s) * 8
            shift0 = i * 72 + (3 * s) * 8
            return (lambda k: SA[j][:, c + k:c + k + 1],
                    lambda k: MOD[j][:, shift0 + k:shift0 + k + 1],
                    lambda k: SG[j][:, c + k:c + k + 1])

        S.barrier()
        AR.reset()
        if stop in ("setup", "setup0"):
            S.emit()
            return nc

        xs_v = xs.rearrange("(k p) t -> p k t", p=128)
        xin_v = xT_in.rearrange("(k p) t -> p k t", p=128)
        y_v = yT.rearrange("(k p) t -> p k t", p=128)
        XB = [Buf("xs%d" % t) for t in range(len(TILES))]
        state = {"src_is_input": True}

        def rstd_from_sq(sq3, W, nk, ps, psb, rstd, b_rstd, b_sq, denom):
            for k in range(nk):
                S.op("pe", lambda e, k=k: e.matmul(ps[:, 0:W], ones_bf, sq3[:, k, :], start=(k == 0), stop=(k == nk - 1)),
                     reads=[b_sq, b_c2], writes=[psb])
            S.op("act", lambda e: e.activation(out=rstd[:, 0:W], in_=ps[:, 0:W], func=AF.Sqrt, scale=1.0 / denom, bias=eps_t),
                 reads=[psb, b_c2], writes=[b_rstd])
            S.op("dve", lambda e: e.reciprocal(rstd[:, 0:W], rstd[:, 0:W]), reads=[b_rstd], writes=[b_rstd])

        def ffn_phase(i, f, last=False):
            s = 0 if f == 0 else 2
            need_ctx = not (i == DEPTH - 1 and f == 1)
            tiles = [t for t in range(len(TILES)) if need_ctx or not TILES[t][2]]
            AR.reset()
            xt_s = [AR.f32(8 * 512) for _ in range(2)]
            ysb = AR.f32(8 * 512)
            sq_s = AR.bf16(8 * 512)
            rstd_s = [AR.f32(512) for _ in range(2)]
            t1_s = [AR.f32(512) for _ in range(2)]
            h_s = [AR.bf16(8 * 512) for _ in range(2)]
            act_s = AR.bf16(NJ * 512)
            wo_s = AR.bf16(NJ * 1024)
            wi_s = [AR.bf16(8 * 1024) for _ in range(2)]
            sg_s = [AR.f32(512) for _ in range(2)]
            b_xt = [Buf(), Buf()]
            b_ysb, b_sq, b_act, b_wo = Buf(), Buf(), Buf(), Buf()
            b_rstd = [Buf(), Buf()]
            b_t1 = [Buf(), Buf()]
            b_h = [Buf(), Buf()]
            b_wi = [Buf(), Buf()]
            b_sg = [Buf(), Buf()]
            tag = "%d%d" % (i, f)
            sem_x = [S.new_dma_sem("fx0_" + tag), S.new_dma_sem("fx1_" + tag)]
            sem_wi = [S.new_dma_sem("fwi0_" + tag), S.new_dma_sem("fwi1_" + tag)]
            sem_wo = S.new_dma_sem("fwo_" + tag)
            sem_st = S.new_dma_sem("fst_" + tag)
            pump_until("wi" + tag, "wo" + tag)
            cv_wi = conv_bufs["wi" + tag]
            cv_wo = conv_bufs["wo" + tag]
            wi_v = wi_b[i, f].rearrange("(k p) n -> p k n", p=128)
            wo_v = wo_b[i, f].rearrange("(j p) n -> p j n", p=128)
            src_v = xin_v if state["src_is_input"] else xs_v

            wo3 = wo_s.rearrange("p (j n) -> p j n", j=NJ)
            for a in range(0, NJ, 6):
                b = min(NJ, a + 6)
                S.dma("sp", sem_wo, wo3[:, a:b, :], wo_v[:, a:b, :], reads=cv_wo, writes=[b_wo])

            groups = [(a, min(NJ, a + 4)) for a in range(0, NJ, 4)]
            wi_ctr = [0]

            def load_x(ti):
                t = tiles[ti]
                t0, W, isc = TILES[t]
                sl = ti % 2
                x3 = xt_s[sl].rearrange("p (k t) -> p k t", k=8)
                S.dma("sp", sem_x[sl], x3[:, :, 0:W], src_v[:, :, t0:t0 + W], reads=[XB[t]], writes=[b_xt[sl]])

            def prenorm(ti):
                t = tiles[ti]
                t0, W, isc = TILES[t]
                sl = ti % 2
                A, Bs, G = scal(i, s, 1 if isc else 0)
                x3 = xt_s[sl].rearrange("p (k t) -> p k t", k=8)
                sq3 = sq_s.rearrange("p (k t) -> p k t", k=8)
                h3 = h_s[sl].rearrange("p (k t) -> p k t", k=8)
                S.op("act", lambda e: e.activation(out=sq3[:, :, 0:W], in_=x3[:, :, 0:W], func=AF.Square),
                     reads=[b_xt[sl]], writes=[b_sq])
                rstd_from_sq(sq3[:, :, 0:W], W, 8, PS[4], PSB[4], rstd_s[sl], b_rstd[sl], b_sq, float(D))
                for k in range(8):
                    q = k % 2
                    S.op("dve", lambda e, k=k, q=q: e.scalar_tensor_tensor(
                        t1_s[q][:, 0:W], x3[:, k, 0:W], A(k), rstd_s[sl][:, 0:W], ALU.mult, ALU.mult),
                        reads=[b_xt[sl], b_rstd[sl], b_mod], writes=[b_t1[q]])
                    S.op("act", lambda e, k=k, q=q: e.activation(
                        out=h3[:, k, 0:W], in_=t1_s[q][:, 0:W], func=AF.Identity, bias=Bs(k), scale=1.0),
                        reads=[b_t1[q], b_mod], writes=[b_h[sl]])

            def gateup(ti, gi):
                t = tiles[ti]
                t0, W, isc = TILES[t]
                sl = ti % 2
                h3 = h_s[sl].rearrange("p (k t) -> p k t", k=8)
                a, b = groups[gi]
                ng = b - a
                ws = wi_ctr[0] % 2
                wi_ctr[0] += 1
                w3 = wi_s[ws].rearrange("p (k n) -> p k n", k=8)
                S.dma("sp", sem_wi[ws], w3[:, :, 0:ng * 128], wi_v[:, :, a * 128:b * 128], reads=cv_wi, writes=[b_wi[ws]])
                S.dma("sp", sem_wi[ws], w3[:, :, 512:512 + ng * 128], wi_v[:, :, DFF + a * 128:DFF + b * 128],
                      reads=cv_wi, writes=[b_wi[ws]])
                act3 = act_s.rearrange("p (j t) -> p j t", j=NJ)
                pump(1)
                for jj in range(ng):
                    j = a + jj
                    pp = j % 2
                    pg, pu = PS[2 * pp], PS[2 * pp + 1]
                    for k in range(8):
                        S.op("pe", lambda e, k=k, jj=jj, pg=pg: e.matmul(
                            pg[:, 0:W], w3[:, k, jj * 128:(jj + 1) * 128], h3[:, k, 0:W], start=(k == 0), stop=(k == 7)),
                            reads=[b_wi[ws], b_h[sl]], writes=[PSB[2 * pp]])
                    for k in range(8):
                        S.op("pe", lambda e, k=k, jj=jj, pu=pu: e.matmul(
                            pu[:, 0:W], w3[:, k, 512 + jj * 128:512 + (jj + 1) * 128], h3[:, k, 0:W], start=(k == 0), stop=(k == 7)),
                            reads=[b_wi[ws], b_h[sl]], writes=[PSB[2 * pp + 1]])
                    S.op("act", lambda e, pp=pp, pg=pg: e.activation(out=sg_s[pp][:, 0:W], in_=pg[:, 0:W], func=AF.Silu),
                         reads=[PSB[2 * pp]], writes=[b_sg[pp]])
                    S.op("dve", lambda e, pp=pp, pu=pu, j=j: e.tensor_tensor(
                        act3[:, j, 0:W], sg_s[pp][:, 0:W], pu[:, 0:W], ALU.mult),
                        reads=[b_sg[pp], PSB[2 * pp + 1]], writes=[b_act])

            def wout(ti):
                t = tiles[ti]
                t0, W, isc = TILES[t]
                act3 = act_s.rearrange("p (j t) -> p j t", j=NJ)
                y3 = ysb.rearrange("p (k t) -> p k t", k=8)
                sq3 = sq_s.rearrange("p (k t) -> p k t", k=8)
                for dk in range(8):
                    pb = 5 + dk % 2
                    py = PS[pb]
                    for j in range(NJ):
                        S.op("pe", lambda e, j=j, dk=dk, py=py: e.matmul(
                            py[:, 0:W], wo3[:, j, dk * 128:(dk + 1) * 128], act3[:, j, 0:W], start=(j == 0), stop=(j == NJ - 1)),
                            reads=[b_wo, b_act], writes=[PSB[pb]])
                    S.op("act", lambda e, dk=dk, py=py: e.activation(out=sq3[:, dk, 0:W], in_=py[:, 0:W], func=AF.Square),
                         reads=[PSB[pb]], writes=[b_sq])
                    S.op("dve", lambda e, dk=dk, py=py: e.tensor_copy(y3[:, dk, 0:W], py[:, 0:W]),
                         reads=[PSB[pb]], writes=[b_ysb])

            def postnorm(ti):
                t = tiles[ti]
                t0, W, isc = TILES[t]
                sl = ti % 2
                A, Bs, G = scal(i, s, 1 if isc else 0)
                x3 = xt_s[sl].rearrange("p (k t) -> p k t", k=8)
                y3 = ysb.rearrange("p (k t) -> p k t", k=8)
                sq3 = sq_s.rearrange("p (k t) -> p k t", k=8)
                rs = rstd_s[sl]
                rstd_from_sq(sq3[:, :, 0:W], W, 8, PS[4], PSB[4], rs, b_rstd[sl], b_sq, float(D))
                for k in range(8):
                    S.op("dve", lambda e, k=k: e.scalar_tensor_tensor(
                        y3[:, k, 0:W], y3[:, k, 0:W], G(k), rs[:, 0:W], ALU.mult, ALU.mult),
                        reads=[b_rstd[sl], b_mod], writes=[b_ysb])
                S.op("pool", lambda e: e.tensor_tensor(y3[:, :, 0:W], y3[:, :, 0:W], x3[:, :, 0:W], ALU.add),
                     reads=[b_xt[sl], b_ysb], writes=[b_ysb])
                if last and not isc:
                    S.dma("pool", sem_st, y_v[:, :, t0:t0 + W], y3[:, :, 0:W], reads=[b_ysb], writes=[XB[t]])
                else:
                    S.dma("pool", sem_st, xs_v[:, :, t0:t0 + W], y3[:, :, 0:W], reads=[b_ysb], writes=[XB[t]])

            n = len(tiles)
            if DBG:
                load_x(0)
                prenorm(0)
                if DBG >= 2:
                    for gi in range(len(groups)):
                        gateup(0, gi)
                if DBG >= 3:
                    wout(0)
                if DBG >= 4:
                    postnorm(0)
                S.barrier()
                return
            load_x(0)
            prenorm(0)
            for ti in range(n):
                gateup(ti, 0)
                gateup(ti, 1)
                if ti + 1 < n:
                    load_x(ti + 1)
                for gi in range(2, len(groups)):
                    gateup(ti, gi)
                if ti + 1 < n:
                    prenorm(ti + 1)
                wout(ti)
                postnorm(ti)
            state["src_is_input"] = False
            S.barrier()

        def dump_and_finish():
            AR.reset()
            tb = AR.f32(8 * 512)
            t3 = tb.rearrange("p (k t) -> p k t", k=8)
            bb = Buf()
            s1 = S.new_dma_sem("dbg_l")
            s2 = S.new_dma_sem("dbg_s")
            for t in range(8):
                t0, W, _ = TILES[t]
                S.dma("sp", s1, t3, xs_v[:, :, t0:t0 + W], reads=[XB[t]], writes=[bb])
                S.dma("sp", s2, y_v[:, :, t0:t0 + W], t3, reads=[bb])
            S.barrier()

        def attn_qkv_phase(i, kind, j):
            is_da = kind == 0
            NQ = 8
            NKC = 8 if is_da else 2
            NV = NKC
            ncols = (NQ + 2 * NKC) * 128
            wsrc = (da_qkv_b if is_da else gqa_qkv_b)[j]
            rope = ropeD if is_da else ropeG
            cvb = conv_bufs[("daqkv%d" if is_da else "gqaqkv%d") % j]
            AR.reset()
            xt_s = [AR.f32(8 * 512) for _ in range(2)]
            sq_s = AR.bf16(8 * 512)
            rstd_s = [AR.f32(512) for _ in range(2)]
            t1_s = [AR.f32(512) for _ in range(2)]
            h_s = [AR.bf16(8 * 512) for _ in range(2)]
            w_s = AR.bf16(8 * ncols)
            cs_s = [AR.f32(2 * 512) for _ in range(2)]
            qk_s = [AR.bf16((NQ + NKC) * 512) for _ in range(2)]
            v_s = [AR.bf16(4 * NV * 128) for _ in range(2)]
            qb_s = [AR.bf16(512) for _ in range(2)]
            ta_s = [AR.f32(512) for _ in range(2)]
            tb_s = [AR.f32(512) for _ in range(2)]
            qn_s = [AR.f32(512) for _ in range(2)]
            rs_s = [AR.f32(512) for _ in range(2)]
            b_xt = [Buf(), Buf()]
            b_sq = Buf()
            b_rstd = [Buf(), Buf()]
            b_t1 = [Buf(), Buf()]
            b_h = [Buf(), Buf()]
            b_w = Buf()
            b_cs = [Buf(), Buf()]
            b_qk = [Buf(), Buf()]
            b_v = [Buf(), Buf()]
            b_qb = [Buf(), Buf()]
            b_ta = [Buf(), Buf()]
            b_tb = [Buf(), Buf()]
            b_qn = [Buf(), Buf()]
            b_rs = [Buf(), Buf()]
            tag = "q%d" % i
            sem_x = [S.new_dma_sem("ax0_" + tag), S.new_dma_sem("ax1_" + tag)]
            sem_cs = [S.new_dma_sem("acs0_" + tag), S.new_dma_sem("acs1_" + tag)]
            sem_w = S.new_dma_sem("aw_" + tag)
            sem_st = [S.new_dma_sem("ast0_" + tag), S.new_dma_sem("ast1_" + tag)]
            sem_sv = [S.new_dma_sem("asv0_" + tag), S.new_dma_sem("asv1_" + tag)]
            w3 = w_s.rearrange("p (k n) -> p k n", k=8)
            wv = wsrc.rearrange("(k p) n -> p k n", p=128)
            for a in range(0, ncols, 768):
                S.dma("sp", sem_w, w3[:, :, a:a + 768], wv[:, :, a:a + 768], reads=cvb, writes=[b_w])
            qk_v = qkT.rearrange("(c p) t -> p c t", p=128)
            v_v = vtok.rearrange("(tb p) e -> p tb e", p=128)
            gq = gqn_s[:, 0:1]
            gk = gqn_s[:, 1:2]
            n = len(TILES)
            cnt = [0]

            def load_x(ti):
                t0, W, isc = TILES[ti]
                sl = ti % 2
                x3 = xt_s[sl].rearrange("p (k t) -> p k t", k=8)
                S.dma("sp", sem_x[sl], x3[:, :, 0:W], xs_v[:, :, t0:t0 + W], reads=[XB[ti]], writes=[b_xt[sl]])
                if not isc:
                    c3 = cs_s[sl].rearrange("p (a t) -> p a t", a=2)
                    S.dma("sp", sem_cs[sl], c3[:, :, 0:W], rope.rearrange("a p t -> p a t")[:, :, t0:t0 + W], writes=[b_cs[sl]])

            def prenorm(ti):
                t0, W, isc = TILES[ti]
                sl = ti % 2
                A, Bs, G = scal(i, 1, 1 if isc else 0)
                x3 = xt_s[sl].rearrange("p (k t) -> p k t", k=8)
                sq3 = sq_s.rearrange("p (k t) -> p k t", k=8)
                h3 = h_s[sl].rearrange("p (k t) -> p k t", k=8)
                S.op("act", lambda e: e.activation(out=sq3[:, :, 0:W], in_=x3[:, :, 0:W], func=AF.Square),
                     reads=[b_xt[sl]], writes=[b_sq])
                rstd_from_sq(sq3[:, :, 0:W], W, 8, PS[7], PSB[7], rstd_s[sl], b_rstd[sl], b_sq, float(D))
                for k in range(8):
                    q = k % 2
                    S.op("dve", lambda e, k=k, q=q: e.scalar_tensor_tensor(
                        t1_s[q][:, 0:W], x3[:, k, 0:W], A(k), rstd_s[sl][:, 0:W], ALU.mult, ALU.mult),
                        reads=[b_xt[sl], b_rstd[sl], b_mod], writes=[b_t1[q]])
                    S.op("act", lambda e, k=k, q=q: e.activation(
                        out=h3[:, k, 0:W], in_=t1_s[q][:, 0:W], func=AF.Identity, bias=Bs(k), scale=1.0),
                        reads=[b_t1[q], b_mod], writes=[b_h[sl]])

            def project(ti):
                t0, W, isc = TILES[ti]
                sl = ti % 2
                h3 = h_s[sl].rearrange("p (k t) -> p k t", k=8)
                qk3 = qk_s[sl].rearrange("p (c t) -> p c t", c=NQ + NKC)
                cos_t = cs_s[sl][:, 0:W]
                sin_t = cs_s[sl][:, 512:512 + W]
                for c in range(NQ + NKC):
                    u = cnt[0] % 2
                    cnt[0] += 1
                    pb = u
                    pr = 2 + u
                    ps = PS[pb]
                    for k in range(8):
                        S.op("pe", lambda e, k=k, c=c, ps=ps: e.matmul(
                            ps[:, 0:W], w3[:, k, c * 128:(c + 1) * 128], h3[:, k, 0:W], start=(k == 0), stop=(k == 7)),
                            reads=[b_w, b_h[sl]], writes=[PSB[pb]])
                    if is_da:
                        src = ps[:, 0:W]
                        srcb = [PSB[pb]]
                    else:
                        gain = gq if c < NQ else gk
                        S.op("act", lambda e, u=u, ps=ps: e.activation(out=qb_s[u][:, 0:W], in_=ps[:, 0:W], func=AF.Square),
                             reads=[PSB[pb]], writes=[b_qb[u]])
                        S.op("pe", lambda e, u=u: e.matmul(PS[4 + u][:, 0:W], ones_bf, qb_s[u][:, 0:W], start=True, stop=True),
                             reads=[b_qb[u], b_c2], writes=[PSB[4 + u]])
                        S.op("act", lambda e, u=u: e.activation(out=rs_s[u][:, 0:W], in_=PS[4 + u][:, 0:W], func=AF.Sqrt,
                                                                scale=1.0 / 128.0, bias=eps_t),
                             reads=[PSB[4 + u], b_c2], writes=[b_rs[u]])
                        S.op("dve", lambda e, u=u: e.reciprocal(rs_s[u][:, 0:W], rs_s[u][:, 0:W]), reads=[b_rs[u]], writes=[b_rs[u]])
                        S.op("dve", lambda e, u=u, ps=ps, gain=gain: e.scalar_tensor_tensor(
                            qn_s[u][:, 0:W], ps[:, 0:W], gain, rs_s[u][:, 0:W], ALU.mult, ALU.mult),
                            reads=[PSB[pb], b_rs[u], b_const], writes=[b_qn[u]])
                        src = qn_s[u][:, 0:W]
                        srcb = [b_qn[u]]
                    if isc:
                        S.op("act", lambda e, c=c, src=src: e.activation(out=qk3[:, c, 0:W], in_=src, func=AF.Copy),
                             reads=srcb, writes=[b_qk[sl]])
                    else:
                        S.op("act", lambda e, u=u, src=src: e.activation(out=qb_s[u][:, 0:W], in_=src, func=AF.Copy),
                             reads=srcb, writes=[b_qb[u]])
                        S.op("pe", lambda e, u=u, pr=pr: e.matmul(PS[pr][:, 0:W], rotm_bf, qb_s[u][:, 0:W], start=True, stop=True),
                             reads=[b_qb[u], b_c2], writes=[PSB[pr]])
                        S.op("dve", lambda e, u=u, src=src: e.tensor_tensor(ta_s[u][:, 0:W], src, cos_t, ALU.mult),
                             reads=srcb + [b_cs[sl]], writes=[b_ta[u]])
                        S.op("dve", lambda e, u=u, pr=pr: e.tensor_tensor(tb_s[u][:, 0:W], PS[pr][:, 0:W], sin_t, ALU.mult),
                             reads=[PSB[pr], b_cs[sl]], writes=[b_tb[u]])
                        S.op("pool", lambda e, u=u, c=c: e.tensor_tensor(qk3[:, c, 0:W], ta_s[u][:, 0:W], tb_s[u][:, 0:W], ALU.add),
                             reads=[b_ta[u], b_tb[u]], writes=[b_qk[sl]])
                ntb = W // 128
                vcols = NV * 128
                v3 = v_s[sl].rearrange("p (tb e) -> p tb e", tb=4)
                voff = (NQ + NKC) * 128
                for tb in range(ntb):
                    for c0 in range(0, vcols, 512):
                        cw = min(512, vcols - c0)
                        u = cnt[0] % 2
                        cnt[0] += 1
                        pb = u
                        ps = PS[pb]
                        for k in range(8):
                            S.op("pe", lambda e, k=k, tb=tb, c0=c0, cw=cw, ps=ps: e.matmul(
                                ps[:, 0:cw], h3[:, k, tb * 128:(tb + 1) * 128], w3[:, k, voff + c0:voff + c0 + cw],
                                start=(k == 0), stop=(k == 7)),
                                reads=[b_w, b_h[sl]], writes=[PSB[pb]])
                        if u == 0:
                            S.op("act", lambda e, tb=tb, c0=c0, cw=cw, ps=ps: e.activation(
                                out=v3[:, tb, c0:c0 + cw], in_=ps[:, 0:cw], func=AF.Copy),
                                reads=[PSB[pb]], writes=[b_v[sl]])
                        else:
                            S.op("dve", lambda e, tb=tb, c0=c0, cw=cw, ps=ps: e.tensor_copy(
                                v3[:, tb, c0:c0 + cw], ps[:, 0:cw]),
                                reads=[PSB[pb]], writes=[b_v[sl]])
                S.dma("pool", sem_st[sl], qk_v[:, 0:NQ + NKC, t0:t0 + W], qk3[:, :, 0:W], reads=[b_qk[sl]])
                S.dma("pool", sem_sv[sl], v_v[:, t0 // 128:t0 // 128 + ntb, 0:vcols], v3[:, 0:ntb, 0:vcols],
                      reads=[b_v[sl]])

            load_x(0)
            prenorm(0)
            for ti in range(n):
                if ti + 1 < n:
                    load_x(ti + 1)
                project(ti)
                if ti + 1 < n:
                    prenorm(ti + 1)
            S.barrier()

        def attn_core_phase(i, kind, j):
            is_da = kind == 0
            need_ctx = i < DEPTH - 1
            NQ = 8
            NKC = 8 if is_da else 2
            jj = j
            AR.reset()
            q_s = [AR.bf16(T) for _ in range(2)]
            k_s = [AR.bf16(T) for _ in range(2)]
            k2_s = [AR.bf16(T) for _ in range(2)] if is_da else None
            v_s = [AR.bf16(34 * 128) for _ in range(2)]
            pT_s = [AR.bf16(512) for _ in range(3)]
            r_s = [AR.f32(512) for _ in range(2)]
            dacc_s = [AR.f32(512) for _ in range(2)]
            b_dacc = [Buf(), Buf()]
            o_s = [AR.f32(512) for _ in range(2)]
            osq_s = AR.bf16(512)
            rs_s = AR.f32(512)
            on_s = [AR.bf16(512) for _ in range(2)]
            b_hd = [Buf(), Buf()]
            b_pT = [Buf(), Buf(), Buf()]
            b_r = [Buf(), Buf()]
            b_o = [Buf(), Buf()]
            b_osq, b_rs = Buf(), Buf()
            b_on = [Buf(), Buf()]
            tag = "c%d" % i
            sem_h = [S.new_dma_sem("ch0_" + tag), S.new_dma_sem("ch1_" + tag)]
            sem_st = [S.new_dma_sem("cst0_" + tag), S.new_dma_sem("cst1_" + tag)]
            v_v = vtok.rearrange("(c p) e -> p c e", p=128)
            qtiles = [t for t in range(len(TILES)) if need_ctx or not TILES[t][2]]
            maps = [(0, 64), (64, 64)] if is_da else [(0, 128)]
            nm = len(maps)
            sc = 0.125 if is_da else 128.0 ** -0.5
            neglam = da_neglam[:, jj:jj + 1]
            subs = da_subs[:, jj:jj + 1]
            octr = [0]
            pctr = [0]

            def load_head(hq):
                sl = hq % 2
                kc = NQ + (hq if is_da else hq // 4)
                vc = hq if is_da else hq // 4
                S.dma("sp", sem_h[sl], q_s[sl], qkT[hq * 128:(hq + 1) * 128, :], writes=[b_hd[sl]])
                if is_da:
                    S.dma("sp", sem_h[sl], k_s[sl][0:64, :], qkT[kc * 128:kc * 128 + 64, :], reads=[b_kz], writes=[b_hd[sl]])
                    S.dma("sp", sem_h[sl], k2_s[sl][64:128, :], qkT[kc * 128 + 64:(kc + 1) * 128, :], reads=[b_kz], writes=[b_hd[sl]])
                else:
                    S.dma("sp", sem_h[sl], k_s[sl], qkT[kc * 128:(kc + 1) * 128, :], writes=[b_hd[sl]])
                vd = v_s[sl].rearrange("p (c e) -> p c e", c=34)
                for c0 in range(0, 34, 8):
                    c1 = min(34, c0 + 8)
                    S.dma("sp", sem_h[sl], vd[:, c0:c1, :], v_v[:, c0:c1, vc * 128:(vc + 1) * 128],
                          writes=[b_hd[sl]])

            def head(hq):
                sl = hq % 2
                v3 = v_s[sl].rearrange("p (c e) -> p c e", c=34)
                for t in qtiles:
                    qtile(hq, sl, v3, t)

            def qtile(hq, sl, v3, t):
                if True:
                    t0, W, isc = TILES[t]
                    kcs = [32, 33] if isc else list(range(34))
                    steps = [(kc, m) for kc in kcs for m in range(nm)]

                    def score(n):
                        kc, m = steps[n]
                        r0, rn = maps[m]
                        pb = pctr[0] % 2
                        kk_ = k2_s[sl] if (is_da and m == 1) else k_s[sl]
                        S.op("pe", lambda e, kc=kc, pb=pb, kk_=kk_: e.matmul(
                            PS[pb][:, 0:W], kk_[:, kc * 128:(kc + 1) * 128], q_s[sl][:, t0:t0 + W],
                            start=True, stop=True),
                            reads=[b_hd[sl]], writes=[PSB[pb]])
                        pctr[0] += 1
                        return pb

                    pend = score(0)
                    for n in range(len(steps)):
                        kc, m = steps[n]
                        pb = pend
                        if n + 1 < len(steps):
                            pend = score(n + 1)
                        u = n % 3
                        S.op("act", lambda e, pb=pb, u=u: e.activation(out=pT_s[u][:, 0:W], in_=PS[pb][:, 0:W], func=AF.Exp, scale=sc),
                             reads=[PSB[pb]], writes=[b_pT[u]])
                        first = (n < nm)
                        lastk = (n >= len(steps) - nm)
                        S.op("pe", lambda e, kc=kc, m=m, u=u, first=first, lastk=lastk: e.matmul(
                            PS[2 + m][:, 0:W], v3[:, kc, :], pT_s[u][:, 0:W], start=first, stop=lastk),
                            reads=[b_hd[sl], b_pT[u]], writes=[PSB[2 + m]])
                        if first:
                            S.op("dve", lambda e, m=m, u=u: e.tensor_copy(dacc_s[m][:, 0:W], pT_s[u][:, 0:W]),
                                 reads=[b_pT[u]], writes=[b_dacc[m]])
                        else:
                            S.op("dve", lambda e, m=m, u=u: e.tensor_tensor(dacc_s[m][:, 0:W], dacc_s[m][:, 0:W], pT_s[u][:, 0:W], ALU.add),
                                 reads=[b_pT[u]], writes=[b_dacc[m]])
                        if lastk:
                            S.op("pe", lambda e, m=m: e.matmul(PS[4 + m][:, 0:W], ones_f, dacc_s[m][:, 0:W], start=True, stop=True),
                                 reads=[b_c2, b_dacc[m]], writes=[PSB[4 + m]])
                    oc = octr[0] % 2
                    octr[0] += 1
                    for m in range(nm):
                        S.op("dve", lambda e, m=m: e.reciprocal(r_s[m][:, 0:W], PS[4 + m][:, 0:W]),
                             reads=[PSB[4 + m]], writes=[b_r[m]])
                    if is_da:
                        S.op("dve", lambda e: e.tensor_tensor(o_s[0][:, 0:W], PS[2][:, 0:W], r_s[0][:, 0:W], ALU.mult),
                             reads=[PSB[2], b_r[0]], writes=[b_o[0]])
                        S.op("dve", lambda e: e.scalar_tensor_tensor(o_s[1][:, 0:W], PS[3][:, 0:W], neglam, r_s[1][:, 0:W], ALU.mult, ALU.mult),
                             reads=[PSB[3], b_r[1], b_da], writes=[b_o[1]])
                        S.op("pool", lambda e: e.tensor_tensor(o_s[0][:, 0:W], o_s[0][:, 0:W], o_s[1][:, 0:W], ALU.add),
                             reads=[b_o[1]], writes=[b_o[0]])
                        S.op("act", lambda e: e.activation(out=osq_s[:, 0:W], in_=o_s[0][:, 0:W], func=AF.Square),
                             reads=[b_o[0]], writes=[b_osq])
                        S.op("pe", lambda e: e.matmul(PS[6][:, 0:W], ones_bf, osq_s[:, 0:W], start=True, stop=True),
                             reads=[b_osq, b_c2], writes=[PSB[6]])
                        S.op("act", lambda e: e.activation(out=rs_s[:, 0:W], in_=PS[6][:, 0:W], func=AF.Sqrt, scale=1.0 / 128.0, bias=eps_t),
                             reads=[PSB[6], b_c2], writes=[b_rs])
                        S.op("dve", lambda e: e.reciprocal(rs_s[:, 0:W], rs_s[:, 0:W]), reads=[b_rs], writes=[b_rs])
                        S.op("dve", lambda e, oc=oc: e.scalar_tensor_tensor(on_s[oc][:, 0:W], o_s[0][:, 0:W], subs, rs_s[:, 0:W], ALU.mult, ALU.mult),
                             reads=[b_o[0], b_rs, b_da], writes=[b_on[oc]])
                    else:
                        S.op("dve", lambda e, oc=oc: e.tensor_tensor(on_s[oc][:, 0:W], PS[2][:, 0:W], r_s[0][:, 0:W], ALU.mult),
                             reads=[PSB[2], b_r[0]], writes=[b_on[oc]])
                    S.dma("pool", sem_st[oc], aT[hq * 128:(hq + 1) * 128, t0:t0 + W], on_s[oc][:, 0:W], reads=[b_on[oc]])

            b_kz = Buf()
            if is_da:
                for sl_ in range(2):
                    S.op("pool", lambda e, sl_=sl_: e.memset(k_s[sl_][64:128, :], 0.0), writes=[b_kz])
                    S.op("pool", lambda e, sl_=sl_: e.memset(k2_s[sl_][0:64, :], 0.0), writes=[b_kz])
            load_head(0)
            for hq in range(NQ):
                if hq + 1 < NQ:
                    load_head(hq + 1)
                head(hq)
            S.barrier()

        def mixer_out_phase(i, wsrc, cvb):
            need_ctx = i < DEPTH - 1
            tiles = [t for t in range(len(TILES)) if need_ctx or not TILES[t][2]]
            AR.reset()
            xt_s = [AR.f32(8 * 512) for _ in range(2)]
            a_s = [AR.bf16(8 * 512) for _ in range(2)]
            ysb = AR.f32(8 * 512)
            sq_s = AR.bf16(8 * 512)
            rstd_s = AR.f32(512)
            w_s = AR.bf16(8 * 1024)
            b_xt = [Buf(), Buf()]
            b_a = [Buf(), Buf()]
            b_ysb, b_sq, b_rstd, b_w = Buf(), Buf(), Buf(), Buf()
            tag = "o%d" % i
            sem_x = [S.new_dma_sem("ox0_" + tag), S.new_dma_sem("ox1_" + tag)]
            sem_a = [S.new_dma_sem("oa0_" + tag), S.new_dma_sem("oa1_" + tag)]
            sem_w = S.new_dma_sem("ow_" + tag)
            sem_st = S.new_dma_sem("ost_" + tag)
            w3 = w_s.rearrange("p (k n) -> p k n", k=8)
            S.dma("sp", sem_w, w3, wsrc.rearrange("(k p) n -> p k n", p=128), reads=cvb, writes=[b_w])
            a_v = aT.rearrange("(k p) t -> p k t", p=128)

            def load(ti):
                t = tiles[ti]
                t0, W, isc = TILES[t]
                sl = ti % 2
                x3 = xt_s[sl].rearrange("p (k t) -> p k t", k=8)
                a3 = a_s[sl].rearrange("p (k t) -> p k t", k=8)
                S.dma("sp", sem_x[sl], x3[:, :, 0:W], xs_v[:, :, t0:t0 + W], reads=[XB[t]], writes=[b_xt[sl]])
                S.dma("sp", sem_a[sl], a3[:, :, 0:W], a_v[:, :, t0:t0 + W], writes=[b_a[sl]])

            def compute(ti):
                t = tiles[ti]
                t0, W, isc = TILES[t]
                sl = ti % 2
                A, Bs, G = scal(i, 1, 1 if isc else 0)
                x3 = xt_s[sl].rearrange("p (k t) -> p k t", k=8)
                a3 = a_s[sl].rearrange("p (k t) -> p k t", k=8)
                y3 = ysb.rearrange("p (k t) -> p k t", k=8)
                sq3 = sq_s.rearrange("p (k t) -> p k t", k=8)
                for dk in range(8):
                    pb = dk % 2
                    py = PS[pb]
                    for k in range(8):
                        S.op("pe", lambda e, k=k, dk=dk, py=py: e.matmul(
                            py[:, 0:W], w3[:, k, dk * 128:(dk + 1) * 128], a3[:, k, 0:W], start=(k == 0), stop=(k == 7)),
                            reads=[b_w, b_a[sl]], writes=[PSB[pb]])
                    S.op("act", lambda e, dk=dk, py=py: e.activation(out=sq3[:, dk, 0:W], in_=py[:, 0:W], func=AF.Square),
                         reads=[PSB[pb]], writes=[b_sq])
                    S.op("dve", lambda e, dk=dk, py=py: e.tensor_copy(y3[:, dk, 0:W], py[:, 0:W]),
                         reads=[PSB[pb]], writes=[b_ysb])
                rstd_from_sq(sq3[:, :, 0:W], W, 8, PS[4], PSB[4], rstd_s, b_rstd, b_sq, float(D))
                for k in range(8):
                    S.op("dve", lambda e, k=k: e.scalar_tensor_tensor(
                        y3[:, k, 0:W], y3[:, k, 0:W], G(k), rstd_s[:, 0:W], ALU.mult, ALU.mult),
                        reads=[b_rstd, b_mod], writes=[b_ysb])
                S.op("pool", lambda e: e.tensor_tensor(y3[:, :, 0:W], y3[:, :, 0:W], x3[:, :, 0:W], ALU.add),
                     reads=[b_xt[sl], b_ysb], writes=[b_ysb])
                S.dma("pool", sem_st, xs_v[:, :, t0:t0 + W], y3[:, :, 0:W], reads=[b_ysb], writes=[XB[t]])

            n = len(tiles)
            load(0)
            for ti in range(n):
                if ti + 1 < n:
                    load(ti + 1)
                compute(ti)
            S.barrier()

        NCH = T // 64
        NBL = T // 128

        def hgrn_proj_phase(i, j):
            cvb = conv_bufs["hgin%d" % j]
            AR.reset()
            xt_s = AR.f32(8 * 512)
            sq_s = AR.bf16(8 * 512)
            rstd_s = AR.f32(512)
            t1_s = [AR.f32(512) for _ in range(2)]
            h_s = [AR.bf16(8 * 512) for _ in range(2)]
            w_s = AR.bf16(8 * 5120)
            stg = [AR.f32(8 * 512) for _ in range(2)]
            v_s = AR.bf16(4 * 1024)
            sg_s = [AR.f32(512) for _ in range(2)]
            b_xt, b_sq, b_rstd, b_w, b_v = Buf(), Buf(), Buf(), Buf(), Buf()
            b_t1 = [Buf(), Buf()]
            b_h = [Buf(), Buf()]
            b_stg = [Buf(), Buf()]
            b_sg = [Buf(), Buf()]
            tag = "hp%d" % i
            sem_x = S.new_dma_sem("hx_" + tag)
            sem_w = S.new_dma_sem("hw_" + tag)
            sem_st = [S.new_dma_sem("hst0_" + tag), S.new_dma_sem("hst1_" + tag)]
            sem_sv = S.new_dma_sem("hsv_" + tag)
            w3 = w_s.rearrange("p (k n) -> p k n", k=8)
            wv = hg_in_b[j].rearrange("(k p) n -> p k n", p=128)
            for a in range(0, 5120, 1024):
                S.dma("sp", sem_w, w3[:, :, a:a + 1024], wv[:, :, a:a + 1024], reads=cvb, writes=[b_w])
            v_v = vtok.rearrange("(tb p) e -> p tb e", p=128)
            dst_f32 = [hq_T, hlf_T[0], hkk_T[0], hlf_T[1], hkk_T[1]]
            n = len(TILES)
            cnt = [0]
            scnt = [0]

            def load_x(ti):
                t0, W, isc = TILES[ti]
                x3 = xt_s.rearrange("p (k t) -> p k t", k=8)
                S.dma("sp", sem_x, x3[:, :, 0:W], xs_v[:, :, t0:t0 + W], reads=[XB[ti]], writes=[b_xt])

            def prenorm(ti):
                t0, W, isc = TILES[ti]
                sl = ti % 2
                A, Bs, G = scal(i, 1, 1 if isc else 0)
                x3 = xt_s.rearrange("p (k t) -> p k t", k=8)
                sq3 = sq_s.rearrange("p (k t) -> p k t", k=8)
                h3 = h_s[sl].rearrange("p (k t) -> p k t", k=8)
                S.op("act", lambda e: e.activation(out=sq3[:, :, 0:W], in_=x3[:, :, 0:W], func=AF.Square),
                     reads=[b_xt], writes=[b_sq])
                rstd_from_sq(sq3[:, :, 0:W], W, 8, PS[7], PSB[7], rstd_s, b_rstd, b_sq, float(D))
                for k in range(8):
                    q = k % 2
                    S.op("dve", lambda e, k=k, q=q: e.scalar_tensor_tensor(
                        t1_s[q][:, 0:W], x3[:, k, 0:W], A(k), rstd_s[:, 0:W], ALU.mult, ALU.mult),
                        reads=[b_xt, b_rstd, b_mod], writes=[b_t1[q]])
                    S.op("act", lambda e, k=k, q=q: e.activation(
                        out=h3[:, k, 0:W], in_=t1_s[q][:, 0:W], func=AF.Identity, bias=Bs(k), scale=1.0),
                        reads=[b_t1[q], b_mod], writes=[b_h[sl]])

            def proj_chunk(ti, col0):
                t0, W, isc = TILES[ti]
                sl = ti % 2
                h3 = h_s[sl].rearrange("p (k t) -> p k t", k=8)
                u = cnt[0] % 4
                cnt[0] += 1
                ps = PS[u]
                for k in range(8):
                    S.op("pe", lambda e, k=k, ps=ps: e.matmul(
                        ps[:, 0:W], w3[:, k, col0:col0 + 128], h3[:, k, 0:W], start=(k == 0), stop=(k == 7)),
                        reads=[b_w, b_h[sl]], writes=[PSB[u]])
                return u

            def project(ti):
                t0, W, isc = TILES[ti]
                sl = ti % 2
                h3 = h_s[sl].rearrange("p (k t) -> p k t", k=8)
                g = scnt[0] % 2
                scnt[0] += 1
                s3 = stg[g].rearrange("p (k t) -> p k t", k=8)
                for hh in range(8):
                    u = proj_chunk(ti, hh * 128)
                    S.op("act", lambda e, hh=hh, u=u, s3=s3: e.activation(out=s3[:, hh, 0:W], in_=PS[u][:, 0:W], func=AF.Silu),
                         reads=[PSB[u]], writes=[b_stg[g]])
                S.dma("pool", sem_st[g], hq_T.rearrange("(k p) t -> p k t", p=128)[:, :, t0:t0 + W], s3[:, :, 0:W], reads=[b_stg[g]])
                for d in range(2):
                    g1 = scnt[0] % 2
                    g2 = (scnt[0] + 1) % 2
                    scnt[0] += 2
                    l3 = stg[g1].rearrange("p (k t) -> p k t", k=8)
                    k3 = stg[g2].rearrange("p (k t) -> p k t", k=8)
                    for hh in range(8):
                        u = proj_chunk(ti, (1 + d) * 1024 + hh * 128)
                        q = hh % 2
                        S.op("act", lambda e, u=u, q=q: e.activation(out=sg_s[q][:, 0:W], in_=PS[u][:, 0:W], func=AF.Sigmoid),
                             reads=[PSB[u]], writes=[b_sg[q]])
                        S.op("dve", lambda e, q=q, d=d, hh=hh: e.tensor_scalar(
                            sg_s[q][:, 0:W], sg_s[q][:, 0:W], hg_oml[:, d * 8 + hh:d * 8 + hh + 1], hg_lb[:, d * 8 + hh:d * 8 + hh + 1],
                            ALU.mult, ALU.add),
                            reads=[b_hg], writes=[b_sg[q]])
                        S.op("act", lambda e, q=q, hh=hh, l3=l3: e.activation(out=l3[:, hh, 0:W], in_=sg_s[q][:, 0:W], func=AF.Ln),
                             reads=[b_sg[q]], writes=[b_stg[g1]])
                        S.op("dve", lambda e, q=q, hh=hh, k3=k3: e.tensor_scalar(
                            k3[:, hh, 0:W], sg_s[q][:, 0:W], -1.0, 1.0, ALU.mult, ALU.add),
                            reads=[b_sg[q]], writes=[b_stg[g2]])
                    S.dma("pool", sem_st[g1], hlf_T[d].rearrange("(k p) t -> p k t", p=128)[:, :, t0:t0 + W], l3[:, :, 0:W], reads=[b_stg[g1]])
                    S.dma("pool", sem_st[g2], hkk_T[d].rearrange("(k p) t -> p k t", p=128)[:, :, t0:t0 + W], k3[:, :, 0:W], reads=[b_stg[g2]])
                g = scnt[0] % 2
                scnt[0] += 1
                gb3 = stg[g].bitcast(BF16)[:, 0:8 * 512].rearrange("p (k t) -> p k t", k=8)
                for hh in range(8):
                    u = proj_chunk(ti, 4096 + hh * 128)
                    S.op("act", lambda e, hh=hh, u=u, gb3=gb3: e.activation(out=gb3[:, hh, 0:W], in_=PS[u][:, 0:W], func=AF.Silu),
                         reads=[PSB[u]], writes=[b_stg[g]])
                S.dma("pool", sem_st[g], qkT.rearrange("(c p) t -> p c t", p=128)[:, 0:8, t0:t0 + W], gb3[:, :, 0:W], reads=[b_stg[g]])
                ntb = W // 128
                v3 = v_s.rearrange("p (tb e) -> p tb e", tb=4)
                for tb in range(ntb):
                    for c0 in range(0, 1024, 512):
                        u = cnt[0] % 4
                        cnt[0] += 1
                        ps = PS[u]
                        for k in range(8):
                            S.op("pe", lambda e, k=k, tb=tb, c0=c0, ps=ps: e.matmul(
                                ps[:, 0:512], h3[:, k, tb * 128:(tb + 1) * 128], w3[:, k, 3072 + c0:3072 + c0 + 512],
                                start=(k == 0), stop=(k == 7)),
                                reads=[b_w, b_h[sl]], writes=[PSB[u]])
                        S.op("dve", lambda e, tb=tb, c0=c0, ps=ps: e.tensor_copy(v3[:, tb, c0:c0 + 512], ps[:, 0:512]),
                             reads=[PSB[u]], writes=[b_v])
                S.dma("pool", sem_sv, v_v[:, t0 // 128:t0 // 128 + ntb, :], v3[:, 0:ntb, :], reads=[b_v])

            load_x(0)
            prenorm(0)
            for ti in range(n):
                if ti + 1 < n:
                    load_x(ti + 1)
                project(ti)
                if ti + 1 < n:
                    prenorm(ti + 1)
            S.barrier()

        def hgrn_scan_phase(i, j):
            AR.reset()
            bufA = AR.f32(T)
            bufB = AR.f32(T)
            bufC = AR.f32(T)
            bufQ = AR.f32(T)
            v_s = AR.bf16(NBL * 128)
            gs_s = AR.bf16(T)
            qt_s = AR.bf16(T)
            qh_s = AR.bf16(T)
            kt_s = AR.bf16(T)
            kh_s = AR.bf16(T)
            gm_s = AR.f32(NCH)
            ktok_s = AR.bf16(NBL * 128)
            sbef_s = AR.bf16(NCH * 128)
            st_s = [AR.f32(128) for _ in range(2)]
            att_s = [AR.bf16(512) for _ in range(2)]
            attf_s = AR.f32(512)
            b_attf = Buf()
            oacc = AR.f32(T)
            gl_s = AR.f32(NCH)
            dec_s = AR.f32(NCH)
            mask_s = AR.f32(512)
            msk_s = [AR.bf16(512), AR.bf16(512)]
            osq_s = AR.bf16(512)
            rs_s = AR.f32(512)
            on_s = [AR.bf16(512) for _ in range(2)]
            bA, bB, bC, bQ, bV, bG = Buf(), Buf(), Buf(), Buf(), Buf(), Buf()
            b_qt, b_qh, b_kt, b_ktok, b_sbef = Buf(), Buf(), Buf(), Buf(), Buf()
            b_kh, b_gm = Buf(), Buf()
            b_st = [Buf(), Buf()]
            b_att = [Buf(), Buf()]
            b_oacc, b_gl, b_dec, b_mask = Buf(), Buf(), Buf(), Buf()
            b_osq, b_rs = Buf(), Buf()
            b_on = [Buf(), Buf()]
            tag = "hs%d" % i
            semA, semB, semQ, semV, semG = (S.new_dma_sem(n_ + tag) for n_ in ("hA", "hB", "hQ", "hV", "hG"))
            sem_o = [S.new_dma_sem("ho0" + tag), S.new_dma_sem("ho1" + tag)]
            semM = S.new_dma_sem("hM" + tag)
            v_v = vtok.rearrange("(c p) e -> p c e", p=128)
            S.op("dve", lambda e: e.memset(mask_s, 1.0), writes=[b_mask])
            S.op("dve", lambda e: e.memset(mask_s.rearrange("p (c j) -> p c j", j=64)[:, :, 0:1], 0.0), writes=[b_mask])
            mtmp = AR.f32(256)
            b_mt = Buf()
            S.dma("sp", semM, mtmp, hmask[:, :], writes=[b_mt])
            for dd in range(2):
                for r in range(4):
                    S.op("dve", lambda e, dd=dd, r=r: e.tensor_copy(msk_s[dd][:, r * 128:(r + 1) * 128], mtmp[:, dd * 128:(dd + 1) * 128]),
                         reads=[b_mt], writes=[b_mask])
            ord_f = [64, 65, 66, 67] + list(range(64))
            ord_b = [67, 66, 65, 64] + list(range(63, -1, -1))
            A3 = bufA.rearrange("p (c j) -> p c j", j=64)
            C3 = bufC.rearrange("p (c j) -> p c j", j=64)
            glb = gl_s.unsqueeze(2).to_broadcast([128, NCH, 64])
            pctr = [0]
            octr = [0]

            def direction(hh, d):
                S.dma("sp", semA, bufA, hlf_T[d][hh * 128:(hh + 1) * 128, :], writes=[bA])
                S.dma("sp", semB, bufB, hkk_T[d][hh * 128:(hh + 1) * 128, :], writes=[bB])
                for (t0, W, isc) in TILES:
                    S.op("dve", lambda e, t0=t0, W=W: e.tensor_tensor_scan(
                        bufC[:, t0:t0 + W], mask_s[:, 0:W], bufA[:, t0:t0 + W], 0.0, ALU.mult, ALU.add),
                        reads=[bA, b_mask], writes=[bC])
                S.op("dve", lambda e: e.tensor_copy(gl_s.unsqueeze(2), C3[:, :, 63:64]), reads=[bC], writes=[b_gl])
                S.op("act", lambda e: e.activation(out=dec_s, in_=gl_s, func=AF.Exp), reads=[b_gl], writes=[b_dec])
                if d == 0:
                    S.op("dve", lambda e: e.tensor_tensor(A3, C3, glb, ALU.subtract), reads=[bC, b_gl], writes=[bA])
                else:
                    S.op("dve", lambda e: e.tensor_tensor(bufA, bufA, bufC, ALU.subtract), reads=[bC], writes=[bA])
                    S.op("dve", lambda e: e.tensor_tensor(C3, A3, glb, ALU.add), reads=[bA, b_gl], writes=[bC])
                S.op("act", lambda e: e.activation(out=bufC, in_=bufC, func=AF.Exp), reads=[], writes=[bC])
                S.op("dve", lambda e: e.tensor_tensor(qt_s, bufQ, bufC, ALU.mult), reads=[bQ, bC], writes=[b_qt])
                S.op("act", lambda e: e.activation(out=bufC, in_=bufA, func=AF.Exp, scale=-1.0), reads=[bA], writes=[bC])
                S.op("dve", lambda e: e.tensor_tensor(kt_s, bufB, bufC, ALU.mult), reads=[bB, bC], writes=[b_kt])
                S.op("dve", lambda e: e.tensor_copy(gm_s.unsqueeze(2), A3[:, :, 32:33]), reads=[bA], writes=[b_gm])
                S.op("dve", lambda e: e.tensor_tensor(A3, A3, gm_s.unsqueeze(2).to_broadcast([128, NCH, 64]), ALU.subtract),
                     reads=[b_gm], writes=[bA])
                S.op("dve", lambda e: e.tensor_scalar(bufA, bufA, -80.0, 80.0, ALU.max, ALU.min), reads=[], writes=[bA])
                S.op("act", lambda e: e.activation(out=bufC, in_=bufA, func=AF.Exp), reads=[bA], writes=[bC])
                S.op("dve", lambda e: e.tensor_tensor(qh_s, bufQ, bufC, ALU.mult), reads=[bQ, bC], writes=[b_qh])
                S.op("act", lambda e: e.activation(out=bufC, in_=bufA, func=AF.Exp, scale=-1.0), reads=[bA], writes=[bC])
                S.op("dve", lambda e: e.tensor_tensor(kh_s, bufB, bufC, ALU.mult), reads=[bB, bC], writes=[b_kh])
                kt3 = ktok_s.rearrange("p (b k) -> p b k", b=NBL)
                for b0 in range(0, NBL, 8):
                    b1 = min(NBL, b0 + 8)
                    pb = 6 + (pctr[0] % 2)
                    pctr[0] += 1
                    pbf = PS[pb][:, :].bitcast(BF16)
                    for b in range(b0, b1):
                        S.op("pe", lambda e, b=b, b0=b0, pbf=pbf: e.transpose(
                            pbf[:, (b - b0) * 128:(b - b0 + 1) * 128], kt_s[:, b * 128:(b + 1) * 128], ident_bf),
                            reads=[b_kt, b_c2], writes=[PSB[pb]])
                    S.op("act", lambda e, b0=b0, b1=b1, pbf=pbf: e.activation(
                        out=ktok_s[:, b0 * 128:b1 * 128], in_=pbf[:, 0:(b1 - b0) * 128], func=AF.Copy),
                        reads=[PSB[pb]], writes=[b_ktok])
                order = ord_f if d == 0 else ord_b
                v3 = v_s.rearrange("p (b e) -> p b e", b=NBL)
                sb3 = sbef_s.rearrange("p (c v) -> p c v", c=NCH)
                cur = None
                for g0 in range(0, NCH, 8):
                    grp = order[g0:g0 + 8]
                    slot = {}
                    used = [0, 0]
                    for c in grp:
                        half = c % 2
                        slot[c] = (4 + half, used[half])
                        used[half] += 1
                    for c in grp:
                        blk, half = c // 2, c % 2
                        pb, gi = slot[c]
                        S.op("pe", lambda e, gi=gi, blk=blk, half=half, pb=pb: e.matmul(
                            PS[pb][:, gi * 128:(gi + 1) * 128], kt3[half * 64:(half + 1) * 64, blk, :], v3[half * 64:(half + 1) * 64, blk, :],
                            start=True, stop=True),
                            reads=[b_ktok, bV], writes=[PSB[pb]])
                    for gj, c in enumerate(grp):
                        pos = g0 + gj
                        nxt = pos % 2
                        pb, gi = slot[c]
                        if pos == 0:
                            S.op("dve", lambda e, gi=gi, pb=pb, nxt=nxt: e.tensor_copy(st_s[nxt], PS[pb][:, gi * 128:(gi + 1) * 128]),
                                 reads=[PSB[pb]], writes=[b_st[nxt]])
                        else:
                            prv = 1 - nxt
                            S.op("act", lambda e, c=c, prv=prv: e.activation(out=sb3[:, c, :], in_=st_s[prv], func=AF.Copy),
                                 reads=[b_st[prv]], writes=[b_sbef])
                            if pos < NCH - 1:
                                S.op("dve", lambda e, gi=gi, pb=pb, c=c, nxt=nxt, prv=prv: e.scalar_tensor_tensor(
                                    st_s[nxt], st_s[prv], dec_s[:, c:c + 1], PS[pb][:, gi * 128:(gi + 1) * 128], ALU.mult, ALU.add),
                                    reads=[b_st[prv], PSB[pb], b_dec], writes=[b_st[nxt]])
                first_c = order[0]
                for q0 in range(0, NBL, 4):
                    q1 = min(NBL, q0 + 4)
                    nb = q1 - q0
                    pa = pctr[0] % 2
                    po = 2 + (pctr[0] % 2)
                    pctr[0] += 1
                    au = pa
                    for b in range(q0, q1):
                        S.op("pe", lambda e, b=b, q0=q0, pa=pa: e.matmul(
                            PS[pa][:, (b - q0) * 128:(b - q0 + 1) * 128], kh_s[:, b * 128:(b + 1) * 128], qh_s[:, b * 128:(b + 1) * 128],
                            start=True, stop=True),
                            reads=[b_kh, b_qh], writes=[PSB[pa]])
                    S.op("dve", lambda e, nb=nb, pa=pa: e.tensor_scalar(
                        attf_s[:, 0:nb * 128], PS[pa][:, 0:nb * 128], -3.0e38, 3.0e38, ALU.max, ALU.min),
                        reads=[PSB[pa]], writes=[b_attf])
                    S.op("dve", lambda e, nb=nb, au=au, d=d: e.tensor_tensor(
                        att_s[au][:, 0:nb * 128], attf_s[:, 0:nb * 128], msk_s[d][:, 0:nb * 128], ALU.mult),
                        reads=[b_attf, b_mask], writes=[b_att[au]])
                    for b in range(q0, q1):
                        o0 = (b - q0) * 128
                        halves = [hf for hf in range(2) if 2 * b + hf != first_c]
                        S.op("pe", lambda e, b=b, o0=o0, po=po, au=au, halves=halves: e.matmul(
                            PS[po][:, o0:o0 + 128], v3[:, b, :], att_s[au][:, o0:o0 + 128], start=True, stop=(len(halves) == 0)),
                            reads=[bV, b_att[au]], writes=[PSB[po]])
                        for half in halves:
                            c = 2 * b + half
                            S.op("pe", lambda e, c=c, o0=o0, half=half, po=po, lasth=(half == halves[-1]): e.matmul(
                                PS[po][:, o0 + half * 64:o0 + half * 64 + 64], sb3[:, c, :], qt_s[:, c * 64:(c + 1) * 64],
                                start=False, stop=lasth),
                                reads=[b_sbef, b_qt], writes=[PSB[po]])
                    tt0 = q0 * 128
                    if d == 0:
                        S.op("act", lambda e, tt0=tt0, nb=nb, po=po: e.activation(
                            out=oacc[:, tt0:tt0 + nb * 128], in_=PS[po][:, 0:nb * 128], func=AF.Copy),
                            reads=[PSB[po]], writes=[b_oacc])
                    else:
                        S.op("dve", lambda e, tt0=tt0, nb=nb, po=po: e.tensor_tensor(
                            oacc[:, tt0:tt0 + nb * 128], oacc[:, tt0:tt0 + nb * 128], PS[po][:, 0:nb * 128], ALU.add),
                            reads=[PSB[po]], writes=[b_oacc])

            def finish_head(hh):
                gnorm = hgn_s[:, 0:1]
                for (t0, W, isc) in TILES:
                    oc = octr[0] % 2
                    octr[0] += 1
                    S.op("act", lambda e, t0=t0, W=W: e.activation(out=osq_s[:, 0:W], in_=oacc[:, t0:t0 + W], func=AF.Square),
                         reads=[b_oacc], writes=[b_osq])
                    S.op("pe", lambda e, W=W: e.matmul(PS[6][:, 0:W], ones_bf, osq_s[:, 0:W], start=True, stop=True),
                         reads=[b_osq, b_c2], writes=[PSB[6]])
                    S.op("act", lambda e, W=W: e.activation(out=rs_s[:, 0:W], in_=PS[6][:, 0:W], func=AF.Sqrt, scale=1.0 / 128.0, bias=eps_t),
                         reads=[PSB[6], b_c2], writes=[b_rs])
                    S.op("dve", lambda e, W=W: e.reciprocal(rs_s[:, 0:W], rs_s[:, 0:W]), reads=[b_rs], writes=[b_rs])
                    S.op("dve", lambda e, t0=t0, W=W: e.scalar_tensor_tensor(
                        rs_s[:, 0:W], oacc[:, t0:t0 + W], gnorm, rs_s[:, 0:W], ALU.mult, ALU.mult),
                        reads=[b_oacc, b_hg], writes=[b_rs])
                    S.op("dve", lambda e, t0=t0, W=W, oc=oc: e.tensor_tensor(on_s[oc][:, 0:W], rs_s[:, 0:W], gs_s[:, t0:t0 + W], ALU.mult),
                         reads=[b_rs, bG], writes=[b_on[oc]])
                    S.dma("pool", sem_o[oc], aT[hh * 128:(hh + 1) * 128, t0:t0 + W], on_s[oc][:, 0:W], reads=[b_on[oc]])

            for hh in range(8):
                S.dma("sp", semQ, bufQ, hq_T[hh * 128:(hh + 1) * 128, :], writes=[bQ])
                vd = v_s.rearrange("p (c e) -> p c e", c=NBL)
                for c0 in range(0, NBL, 8):
                    c1 = min(NBL, c0 + 8)
                    S.dma("sp", semV, vd[:, c0:c1, :], v_v[:, c0:c1, hh * 128:(hh + 1) * 128], writes=[bV])
                S.dma("sp", semG, gs_s, qkT[hh * 128:(hh + 1) * 128, :], writes=[bG])
                direction(hh, 0)
                direction(hh, 1)
                finish_head(hh)
            S.barrier()

        b_qkT, b_vtok, b_aT = Buf("qkT"), Buf("vtok"), Buf("aT")

        def mixer(i):
            kind, j = i % 3, i // 3
            pump_until(*mixer_names(i))
            if kind == 0:
                attn_qkv_phase(i, 0, j)
                attn_core_phase(i, 0, j)
                mixer_out_phase(i, da_o_b[j], conv_bufs["dao%d" % j])
            elif kind == 2:
                attn_qkv_phase(i, 2, j)
                attn_core_phase(i, 2, j)
                mixer_out_phase(i, gqa_o_b[j], conv_bufs["gqao%d" % j])
            else:
                hgrn_proj_phase(i, j)
                if stop == "hp":
                    return
                hgrn_scan_phase(i, j)
                if stop == "hs":
                    return
                mixer_out_phase(i, hg_o_b[j], conv_bufs["hgo%d" % j])

        done = False
        start_layer_mixer = 0
        for i in range(start_layer, DEPTH):
            ffn_phase(i, 0)
            if stop == "f%d0" % i:
                dump_and_finish()
                done = True
                break
            if i >= start_layer_mixer:
                mixer(i)
            if stop == "m%d" % i or stop in ("hp", "hs"):
                dump_and_finish()
                done = True
                break
            ffn_phase(i, 1, last=(i == DEPTH - 1))
            if stop == "f%d1" % i and i < DEPTH - 1:
                dump_and_finish()
                done = True
                break
        S.barrier()
        S.emit()
    return nc


def _feat_major(v):
    v = np.asarray(v, dtype=np.float32)
    lead = v.shape[:-1]
    r = v.reshape(lead + (8, 128))
    r = np.moveaxis(r, -1, 0)
    return np.ascontiguousarray(r)


def _consts():
    ident = np.eye(128, dtype=np.float32)
    rot = np.zeros((128, 128), np.float32)
    for j in range(64):
        rot[2 * j + 1, 2 * j] = -1.0
        rot[2 * j, 2 * j + 1] = 1.0
    tril = np.zeros((128, 128), np.float32)
    return np.ascontiguousarray(np.concatenate([ident, rot, tril], axis=1))


def _rope_table(head_dim):
    pairs = head_dim // 4
    inv_freq = np.power(np.float32(10000.0), -np.arange(pairs, dtype=np.float32) / np.float32(pairs)).astype(np.float32)
    rows = TL // GRID_W
    r = np.repeat(np.arange(rows, dtype=np.float32), GRID_W)
    col = np.tile(np.arange(GRID_W, dtype=np.float32), rows)
    ang = np.concatenate([r[:, None] * inv_freq, col[:, None] * inv_freq], axis=-1).astype(np.float32)
    p = np.arange(128)
    a = (p % head_dim) // 2
    tab = ang[:, a].T
    return np.ascontiguousarray(np.stack([np.cos(tab), np.sin(tab)], axis=0).astype(np.float32))


def _hmask():
    m = np.zeros((2, 128, 128), np.float32)
    for blk in range(2):
        for s_ in range(64):
            for t_ in range(64):
                if s_ <= t_:
                    m[0, blk * 64 + s_, blk * 64 + t_] = 1.0
                if s_ >= t_:
                    m[1, blk * 64 + s_, blk * 64 + t_] = 1.0
    return np.ascontiguousarray(np.concatenate([m[0], m[1]], axis=1))


_PROG = {}


def prepare_inputs(inputs):
    x = np.asarray(inputs["x"], np.float32)
    ctx = np.asarray(inputs["ctx"], np.float32)
    c = np.asarray(inputs["c"], np.float32)
    c_ctx = np.asarray(inputs["c_ctx"], np.float32)
    B = x.shape[0]
    shared = {
        "w_mod": np.ascontiguousarray(inputs["w_mod"], dtype=np.float32),
        "bmodT": _feat_major(np.asarray(inputs["b_mod"]).reshape(DEPTH, 9, D)).reshape(128, DEPTH * 72),
        "normT": _feat_major(np.asarray(inputs["norm_g"])).reshape(128, DEPTH * 48),
        "ffn_w_in": np.ascontiguousarray(inputs["ffn_w_in"], dtype=np.float32),
        "ffn_w_out": np.ascontiguousarray(inputs["ffn_w_out"], dtype=np.float32),
        "consts": _consts(),
        "da_w_qkv": np.ascontiguousarray(inputs["da_w_qkv"], dtype=np.float32),
        "da_w_o": np.ascontiguousarray(inputs["da_w_o"], dtype=np.float32),
        "dal": np.ascontiguousarray(np.broadcast_to(np.asarray(inputs["da_lambda"], np.float32).reshape(1, 512), (128, 512))),
        "dasub": np.ascontiguousarray(np.asarray(inputs["da_subln"], np.float32).T),
        "ropeD": _rope_table(64),
        "gqa_w_qkv": np.ascontiguousarray(inputs["gqa_w_qkv"], dtype=np.float32),
        "gqa_w_o": np.ascontiguousarray(inputs["gqa_w_o"], dtype=np.float32),
        "gqn": np.ascontiguousarray(np.stack([np.asarray(inputs["gqa_q_norm"], np.float32)[0],
                                              np.asarray(inputs["gqa_k_norm"], np.float32)[0]], axis=1)),
        "ropeG": _rope_table(128),
        "hg_w_in": np.ascontiguousarray(inputs["hg_w_in"], dtype=np.float32),
        "hg_w_o": np.ascontiguousarray(inputs["hg_w_o"], dtype=np.float32),
        "hlbT": _feat_major(np.asarray(inputs["hg_lower_bound"], np.float32)).reshape(128, 64),
        "hgn": np.ascontiguousarray(np.asarray(inputs["hg_norm"], np.float32).reshape(128, 1)),
        "hmask": _hmask(),
    }
    in_maps = []
    for b in range(B):
        xT = np.ascontiguousarray(np.concatenate([x[b].T, ctx[b].T], axis=1))
        cc = np.stack([c[b], c_ctx], axis=0)
        ccT = _feat_major(cc)
        ccT = np.ascontiguousarray(np.transpose(ccT, (0, 2, 1))).reshape(128, 16)
        m = dict(shared)
        m["xT"] = xT
        m["ccT"] = ccT
        in_maps.append(m)
    return in_maps


def kernel(**inputs):
    stop = inputs.pop("_stop", None)
    cores = inputs.pop("_cores", None)
    start = inputs.pop("_start", 0)
    bsel = inputs.pop("_batch", None)
    in_maps = prepare_inputs(inputs)
    if bsel is not None:
        in_maps = [in_maps[bsel]]
    if cores is not None:
        in_maps = in_maps[:cores]
    key = (stop, start)
    if key not in _PROG:
        _PROG[key] = build_program(stop, start)
    nc = _PROG[key]
    res = run_bass_kernel_spmd(nc, in_maps, core_ids=list(range(len(in_maps))))
    out = np.stack([np.ascontiguousarray(np.asarray(r["yT"]).T) for r in res.results], axis=0)
    return out.astype(np.float32)
```

```python
import contextlib
import math
import numpy as np
import concourse.bass as bass
import concourse.mybir as mybir
from concourse.bass_utils import run_bass_kernel_spmd

F32 = mybir.dt.float32
BF16 = mybir.dt.bfloat16
AF = mybir.ActivationFunctionType
ALU = mybir.AluOpType

D = 1024
TL = 4096
TC = 256
T = TL + TC
DEPTH = 4
DFF = 2816
NJ = DFF // 128
EPS = 1e-6
GRID_W = 64
import os
DBG = int(os.environ.get('KDBG', '0'))


class Buf:
    __slots__ = ("name", "w", "r", "excl")

    def __init__(self, name="", excl=False):
        self.name = name
        self.w = None
        self.r = []
        self.excl = excl


class Sched:
    ENGS = ("pe", "act", "dve", "pool", "sp")

    def __init__(self, nc):
        self.nc = nc
        self.prog = {e: [] for e in self.ENGS}
        self.cnt = {}
        self.known = {e: {} for e in self.ENGS}
        for e in self.ENGS:
            self.cnt["E" + e] = 0
        self.all_dma = [[], []]
        self.free_dma = [[], []]
        self.persist = set()

    def _need(self, eng, deps):
        kn = self.known[eng]
        best = {}
        for d in deps:
            if d is None:
                continue
            k, v = d
            if kn.get(k, 0) >= v:
                continue
            if best.get(k, 0) < v:
                best[k] = v
        out = []
        for k, v in best.items():
            kn[k] = v
            out.append((k, v))
        return out

    def _deps(self, eng, reads, writes):
        deps = []
        me = "E" + eng
        pe = eng == "pe"
        for b in reads:
            if b.w is not None and not (pe and b.w[0] == me):
                deps.append(b.w)
        for b in writes:
            if b.w is not None and not (pe and b.w[0] == me):
                deps.append(b.w)
            for r in b.r:
                if r[0] != me:
                    deps.append(r)
        return deps

    def _commit(self, tok, reads, writes):
        for b in reads:
            b.r.append(tok)
            if len(b.r) > 64:
                best = {}
                for k, v in b.r:
                    if best.get(k, 0) < v:
                        best[k] = v
                b.r = list(best.items())
        for b in writes:
            b.w = tok
            b.r = []

    def op(self, eng, fn, reads=(), writes=()):
        if any(b.excl for b in reads):
            writes = list(writes) + [b for b in reads if b.excl]
            reads = [b for b in reads if not b.excl]
        waits = self._need(eng, self._deps(eng, reads, writes))
        key = "E" + eng
        self.cnt[key] += 1
        tok = (key, self.cnt[key])
        self.prog[eng].append((waits, fn, key, 1))
        self._commit(tok, reads, writes)
        return tok

    def new_dma_sem(self, name):
        return [None, name.startswith("cv")]

    def _bind(self, sem, eng):
        if sem[0] is None:
            q = 1 if eng == "pool" else 0
            if self.free_dma[q]:
                sem[0] = self.free_dma[q].pop()
            else:
                key = "%s%d" % ("PQ"[q], len(self.all_dma[q]))
                self.all_dma[q].append(key)
                self.cnt[key] = 0
                sem[0] = key
            if sem[1]:
                self.persist.add(sem[0])
        return sem[0]

    def dma(self, eng, sem, out, in_, reads=(), writes=()):
        sem = self._bind(sem, eng)
        waits = self._need(eng, [d for d in self._deps(eng, reads, writes) if d[0] != sem])
        self.cnt[sem] += 16
        tok = (sem, self.cnt[sem])
        self.prog[eng].append((waits, lambda e: e.dma_start(out=out, in_=in_), sem, 16))
        self._commit(tok, reads, writes)
        return tok

    def wait_all(self, eng, toks):
        waits = self._need(eng, toks)
        if waits:
            self.prog[eng].append((waits, None, None, 0))

    def barrier(self):
        toks = [(k, v) for k, v in self.cnt.items() if v > 0]
        for e in self.ENGS:
            self.wait_all(e, [t for t in toks if t[0] != "E" + e])
        self.free_dma = [[k for k in reversed(self.all_dma[q]) if k not in self.persist] for q in range(2)]

    def emit(self):
        nc = self.nc
        with contextlib.ExitStack() as st:
            sems = {k: st.enter_context(nc.semaphore(k)) for k in self.cnt}
            block = st.enter_context(nc.Block())
            hmap = {"pe": block.tensor, "act": block.scalar, "dve": block.vector,
                    "pool": block.gpsimd, "sp": block.sync}
            for e in self.ENGS:
                prog = self.prog[e]
                if not prog:
                    continue

                def body(h, prog=prog):
                    for waits, fn, key, inc in prog:
                        for (k, v) in waits:
                            h.wait_ge(sems[k], v)
                        if fn is not None:
                            fn(h).then_inc(sems[key], inc)
                hmap[e](body)


class Arena:
    def __init__(self, ap):
        self.ap = ap
        self.n = ap.shape[1]
        self.base = 0
        self.off = 0

    def freeze(self):
        self.base = self.off

    def reset(self):
        self.off = self.base

    def f32(self, n):
        o = self.off
        self.off += n
        assert self.off <= self.n, "arena overflow %d > %d" % (self.off, self.n)
        return self.ap[:, o:o + n]

    def bf16(self, n):
        w = (n + 1) // 2
        return self.f32(w).bitcast(BF16)[:, 0:n]


TILES = [(i * 512, 512, False) for i in range(8)] + [(TL, TC, True)]


def build_program(stop=None, start_layer=0):
    nc = bass.Bass("TRN2", target_bir_lowering=False)

    def din(name, shape, dt=F32):
        return nc.dram_tensor(name, list(shape), dt, kind="ExternalInput").ap()

    def dscr(name, shape, dt):
        return nc.dram_tensor(name, list(shape), dt).ap()

    xT_in = din("xT", [D, T])
    ccT = din("ccT", [128, 16])
    w_mod = din("w_mod", [DEPTH, D, 9 * D])
    bmodT = din("bmodT", [128, DEPTH * 72])
    normT = din("normT", [128, DEPTH * 48])
    ffn_w_in = din("ffn_w_in", [DEPTH, 2, D, 2 * DFF])
    ffn_w_out = din("ffn_w_out", [DEPTH, 2, DFF, D])
    consts = din("consts", [128, 3 * 128])
    da_w_qkv = din("da_w_qkv", [2, D, 3 * D])
    da_w_o = din("da_w_o", [2, D, D])
    dal = din("dal", [128, 2 * 256])
    dasub = din("dasub", [128, 2])
    ropeD = din("ropeD", [2, 128, TL])
    gqa_w_qkv = din("gqa_w_qkv", [1, D, 1536])
    gqa_w_o = din("gqa_w_o", [1, D, D])
    gqn = din("gqn", [128, 2])
    ropeG = din("ropeG", [2, 128, TL])
    hg_w_in = din("hg_w_in", [1, D, 5 * D])
    hg_w_o = din("hg_w_o", [1, D, D])
    hlbT = din("hlbT", [128, 64])
    hgn = din("hgn", [128, 1])
    hmask = din("hmask", [128, 256])
    yT = nc.dram_tensor("yT", [D, TL], F32, kind="ExternalOutput").ap()

    xs = dscr("xs", [D, T], F32)
    wi_b = dscr("wi_b", [DEPTH, 2, D, 2 * DFF], BF16)
    wo_b = dscr("wo_b", [DEPTH, 2, DFF, D], BF16)
    da_qkv_b = dscr("da_qkv_b", [2, D, 3 * D], BF16)
    da_o_b = dscr("da_o_b", [2, D, D], BF16)
    gqa_qkv_b = dscr("gqa_qkv_b", [1, D, 1536], BF16)
    gqa_o_b = dscr("gqa_o_b", [1, D, D], BF16)
    hg_in_b = dscr("hg_in_b", [1, D, 5 * D], BF16)
    hg_o_b = dscr("hg_o_b", [1, D, D], BF16)
    hq_T = dscr("hq_T", [D, T], F32)
    hlf_T = [dscr("hlf_T%d" % d_, [D, T], F32) for d_ in range(2)]
    hkk_T = [dscr("hkk_T%d" % d_, [D, T], F32) for d_ in range(2)]
    qkT = dscr("qkT", [16 * 128, T], BF16)
    vtok = dscr("vtok", [T, D], BF16)
    aT = dscr("aT", [D, T], BF16)

    with contextlib.ExitStack() as st:
        arena_t = st.enter_context(nc.sbuf_tensor("arena", [128, 50 * 1024], F32))
        PS = [st.enter_context(nc.psum_tensor("ps%d" % i, [128, 512], F32)) for i in range(8)]
        AR = Arena(arena_t[:, :])
        S = Sched(nc)
        PSB = [Buf("ps%d" % i, excl=True) for i in range(8)]

        ones_bf = AR.bf16(128)
        ones_f = AR.f32(128)
        ident_bf = AR.bf16(128)
        rotm_bf = AR.bf16(128)
        cst_f = AR.f32(384)
        eps_t = AR.f32(1)
        ccs = AR.f32(16)
        modL = AR.f32(DEPTH * 72)
        modC = AR.f32(DEPTH * 72)
        bmod_s = AR.f32(DEPTH * 72)
        norm_s = AR.f32(DEPTH * 48)
        SA = [AR.f32(DEPTH * 24), AR.f32(DEPTH * 24)]
        SG = [AR.f32(DEPTH * 24), AR.f32(DEPTH * 24)]
        MOD = [modL, modC]
        dal_s = AR.f32(512)
        dasub_s = AR.f32(2)
        gqn_s = AR.f32(2)
        da_neglam = AR.f32(2)
        da_subs = AR.f32(2)
        da_tmp = AR.f32(64)
        da_e = AR.f32(4)
        hlb_s = AR.f32(64)
        hgn_s = AR.f32(1)
        hg_lb = AR.f32(16)
        hg_oml = AR.f32(16)
        hg_sum = AR.f32(16)
        AR.freeze()
        b_const = Buf("const")
        b_mod = Buf("mod")

        conv_bufs = {}

        cv_inflight = [None, None]

        conv_jobs = []
        conv_pending = {}

        def convert(name, src, dst, rows):
            sems = [S.new_dma_sem("cv%d_%s" % (q, name)) for q in range(2)]
            bs = [Buf("cv%d_%s" % (q, name)) for q in range(2)]
            r0 = 0
            n = 0
            while r0 < rows:
                r1 = min(rows, r0 + 128)
                conv_jobs.append((name, n % 2, sems[n % 2], bs[n % 2], dst[r0:r1, :], src[r0:r1, :]))
                r0 = r1
                n += 1
            conv_pending[name] = n
            conv_bufs[name] = bs

        def pump(n):
            while n > 0 and conv_jobs:
                name, q, sem, b, dst, src = conv_jobs.pop(0)
                if cv_inflight[q] is not None:
                    S.wait_all("pool", [cv_inflight[q]])
                cv_inflight[q] = S.dma("pool", sem, dst, src, writes=[b])
                conv_pending[name] -= 1
                n -= 1

        def pump_until(*names):
            while any(conv_pending.get(nm, 0) > 0 for nm in names):
                pump(1)

        def convert_ffn(i, f):
            convert("wi%d%d" % (i, f), ffn_w_in[i, f], wi_b[i, f], D)
            convert("wo%d%d" % (i, f), ffn_w_out[i, f], wo_b[i, f], DFF)

        def convert_mixer(i):
            kind, j = i % 3, i // 3
            if kind == 0:
                convert("daqkv%d" % j, da_w_qkv[j], da_qkv_b[j], D)
                convert("dao%d" % j, da_w_o[j], da_o_b[j], D)
            elif kind == 2:
                convert("gqaqkv%d" % j, gqa_w_qkv[j], gqa_qkv_b[j], D)
                convert("gqao%d" % j, gqa_w_o[j], gqa_o_b[j], D)
            else:
                convert("hgin%d" % j, hg_w_in[j], hg_in_b[j], D)
                convert("hgo%d" % j, hg_w_o[j], hg_o_b[j], D)

        def mixer_names(i):
            kind, j = i % 3, i // 3
            return {0: ("daqkv%d" % j, "dao%d" % j), 1: ("hgin%d" % j, "hgo%d" % j), 2: ("gqaqkv%d" % j, "gqao%d" % j)}[kind]

        for li in range(start_layer, DEPTH):
            convert_ffn(li, 0)
            convert_mixer(li)
            convert_ffn(li, 1)
        pump_until("wi%d0" % start_layer, "wo%d0" % start_layer)
        pump(16)

        s_c0 = S.new_dma_sem("c0")
        S.dma("sp", s_c0, cst_f, consts[:, :], writes=[b_const])
        S.dma("sp", s_c0, ccs, ccT[:, :], writes=[b_const])
        S.dma("sp", s_c0, bmod_s, bmodT[:, :], writes=[b_const])
        S.dma("sp", s_c0, norm_s, normT[:, :], writes=[b_const])
        b_c2 = Buf("const2")
        S.op("dve", lambda e: e.memset(ones_bf, 1.0), writes=[b_c2])
        S.op("dve", lambda e: e.memset(ones_f, 1.0), writes=[b_c2])
        S.op("dve", lambda e: e.memset(eps_t, EPS), writes=[b_c2])
        S.op("dve", lambda e: e.tensor_copy(ident_bf, cst_f[:, 0:128]), reads=[b_const], writes=[b_c2])
        S.op("dve", lambda e: e.tensor_copy(rotm_bf, cst_f[:, 128:256]), reads=[b_const], writes=[b_c2])
        S.op("act", lambda e: e.activation(out=ccs, in_=ccs, func=AF.Silu), reads=[b_const], writes=[b_const])

        S.dma("sp", s_c0, dal_s, dal[:, :], writes=[b_const])
        S.dma("sp", s_c0, dasub_s, dasub[:, :], writes=[b_const])
        S.dma("sp", s_c0, gqn_s, gqn[:, :], writes=[b_const])
        b_da = Buf("da")
        for jj in range(2):
            li = 3 * jj
            lam_init = 0.8 - 0.6 * math.exp(-0.3 * li)
            for pair in range(2):
                o0 = jj * 256 + pair * 128
                S.op("dve", lambda e, o0=o0: e.tensor_tensor(da_tmp, dal_s[:, o0:o0 + 64], dal_s[:, o0 + 64:o0 + 128], ALU.mult),
                     reads=[b_const], writes=[b_da])
                S.op("dve", lambda e, jj=jj, pair=pair: e.tensor_reduce(
                    da_e[:, jj * 2 + pair:jj * 2 + pair + 1], da_tmp, mybir.AxisListType.X, ALU.add),
                    reads=[b_da], writes=[b_da])
            S.op("act", lambda e, jj=jj: e.activation(out=da_e[:, jj * 2:jj * 2 + 2], in_=da_e[:, jj * 2:jj * 2 + 2], func=AF.Exp),
                 reads=[b_da], writes=[b_da])
            S.op("dve", lambda e, jj=jj, lam_init=lam_init: e.scalar_tensor_tensor(
                da_neglam[:, jj:jj + 1], da_e[:, jj * 2 + 1:jj * 2 + 2], -lam_init, da_e[:, jj * 2:jj * 2 + 1], ALU.add, ALU.subtract),
                reads=[b_da], writes=[b_da])
            S.op("dve", lambda e, jj=jj, lam_init=lam_init: e.tensor_scalar(
                da_subs[:, jj:jj + 1], dasub_s[:, jj:jj + 1], 1.0 - lam_init, None, ALU.mult),
                reads=[b_const, b_da], writes=[b_da])

        S.dma("sp", s_c0, hlb_s, hlbT[:, :], writes=[b_const])
        S.dma("sp", s_c0, hgn_s, hgn[:, :], writes=[b_const])
        b_hg = Buf("hg")
        HL = 1
        S.op("act", lambda e: e.activation(out=hlb_s, in_=hlb_s, func=AF.Exp), reads=[b_const], writes=[b_const])
        hl4 = hlb_s.rearrange("p (d l k) -> p d l k", d=2, l=4)
        for dd in range(2):
            S.op("dve", lambda e, dd=dd: e.tensor_tensor(hg_sum[:, dd * 8:dd * 8 + 8], hl4[:, dd, 0, :], hl4[:, dd, 1, :], ALU.add),
                 reads=[b_const], writes=[b_hg])
            for l in (2, 3):
                S.op("dve", lambda e, dd=dd, l=l: e.tensor_tensor(hg_sum[:, dd * 8:dd * 8 + 8], hg_sum[:, dd * 8:dd * 8 + 8], hl4[:, dd, l, :], ALU.add),
                     reads=[b_const, b_hg], writes=[b_hg])
            S.op("dve", lambda e, dd=dd: e.tensor_copy(hg_lb[:, dd * 8:dd * 8 + 8], hl4[:, dd, 1, :]), reads=[b_const], writes=[b_hg])
            for l in range(2, HL + 1):
                S.op("dve", lambda e, dd=dd, l=l: e.tensor_tensor(hg_lb[:, dd * 8:dd * 8 + 8], hg_lb[:, dd * 8:dd * 8 + 8], hl4[:, dd, l, :], ALU.add),
                     reads=[b_const, b_hg], writes=[b_hg])
        S.op("dve", lambda e: e.reciprocal(hg_sum, hg_sum), reads=[b_hg], writes=[b_hg])
        S.op("dve", lambda e: e.tensor_tensor(hg_lb, hg_lb, hg_sum, ALU.mult), reads=[b_hg], writes=[b_hg])
        S.op("dve", lambda e: e.tensor_scalar(hg_oml, hg_lb, -1.0, 1.0, ALU.mult, ALU.add), reads=[b_hg], writes=[b_hg])

        wm_slots = [AR.f32(8 * 1024), AR.f32(8 * 1024)]
        wm_bufs = [Buf("wm0"), Buf("wm1")]
        wm_sems = [S.new_dma_sem("wm0"), S.new_dma_sem("wm1")]
        blk = 0
        for i in range(DEPTH):
            wv = w_mod[i].rearrange("(k p) n -> p k n", p=128)
            for m in range(9):
                sl = blk % 2
                wt = wm_slots[sl].rearrange("p (k n) -> p k n", k=8)
                S.dma("sp", wm_sems[sl], wt, wv[:, :, m * 1024:(m + 1) * 1024], writes=[wm_bufs[sl]])
                pb = blk % 2
                ps = PS[pb]
                for n in range(8):
                    for k in range(8):
                        S.op("pe", lambda e, ps=ps, wt=wt, n=n, k=k: e.matmul(
                            ps[:, n * 2:n * 2 + 2], wt[:, k, n * 128:(n + 1) * 128], ccs[:, 2 * k:2 * k + 2],
                            start=(k == 0), stop=(k == 7)),
                            reads=[wm_bufs[sl], b_const], writes=[PSB[pb]])
                ps3 = ps[:, 0:16].rearrange("p (n j) -> p n j", j=2)
                c0 = i * 72 + m * 8
                for j in range(2):
                    S.op("dve", lambda e, ps3=ps3, j=j, c0=c0: e.tensor_tensor(
                        MOD[j][:, c0:c0 + 8], ps3[:, :, j], bmod_s[:, c0:c0 + 8], ALU.add),
                        reads=[PSB[pb], b_const], writes=[b_mod])
                blk += 1
        for i in range(DEPTH):
            for s in range(3):
                wgt = 1.0 if s == 1 else 0.5
                c = (i * 3 + s) * 8
                gpre = norm_s[:, i * 48 + (2 * s) * 8: i * 48 + (2 * s) * 8 + 8]
                gpost = norm_s[:, i * 48 + (2 * s + 1) * 8: i * 48 + (2 * s + 1) * 8 + 8]
                for j in range(2):
                    scale = MOD[j][:, i * 72 + (3 * s + 1) * 8: i * 72 + (3 * s + 1) * 8 + 8]
                    gate = MOD[j][:, i * 72 + (3 * s + 2) * 8: i * 72 + (3 * s + 2) * 8 + 8]
                    S.op("dve", lambda e, j=j, c=c, scale=scale, gpre=gpre: e.scalar_tensor_tensor(
                        SA[j][:, c:c + 8], scale, 1.0, gpre, ALU.add, ALU.mult),
                        reads=[b_mod, b_const], writes=[b_mod])
                    S.op("dve", lambda e, j=j, c=c, gate=gate, gpost=gpost, wgt=wgt: e.scalar_tensor_tensor(
                        SG[j][:, c:c + 8], gate, wgt, gpost, ALU.mult, ALU.mult),
                        reads=[b_mod, b_const], writes=[b_mod])

        def scal(i, s, j):
            c = (i * 3 + s) * 8
            shift0 = i * 72 + (3 * s) * 8
            return (lambda k: SA[j][:, c + k:c + k + 1],
                    lambda k: MOD[j][:, shift0 + k:shift0 + k + 1],
                    lambda k: SG[j][:, c + k:c + k + 1])

        S.barrier()
        AR.reset()
        if stop in ("setup", "setup0"):
            S.emit()
            return nc

        xs_v = xs.rearrange("(k p) t -> p k t", p=128)
        xin_v = xT_in.rearrange("(k p) t -> p k t", p=128)
        y_v = yT.rearrange("(k p) t -> p k t", p=128)
        XB = [Buf("xs%d" % t) for t in range(len(TILES))]
        state = {"src_is_input": True}

        def rstd_from_sq(sq3, W, nk, ps, psb, rstd, b_rstd, b_sq, denom):
            for k in range(nk):
                S.op("pe", lambda e, k=k: e.matmul(ps[:, 0:W], ones_bf, sq3[:, k, :], start=(k == 0), stop=(k == nk - 1)),
                     reads=[b_sq, b_c2], writes=[psb])
            S.op("act", lambda e: e.activation(out=rstd[:, 0:W], in_=ps[:, 0:W], func=AF.Sqrt, scale=1.0 / denom, bias=eps_t),
                 reads=[psb, b_c2], writes=[b_rstd])
            S.op("dve", lambda e: e.reciprocal(rstd[:, 0:W], rstd[:, 0:W]), reads=[b_rstd], writes=[b_rstd])

        def ffn_phase(i, f, last=False):
            s = 0 if f == 0 else 2
            need_ctx = not (i == DEPTH - 1 and f == 1)
            tiles = [t for t in range(len(TILES)) if need_ctx or not TILES[t][2]]
            AR.reset()
            xt_s = [AR.f32(8 * 512) for _ in range(2)]
            ysb = AR.f32(8 * 512)
            sq_s = AR.bf16(8 * 512)
            rstd_s = [AR.f32(512) for _ in range(2)]
            t1_s = [AR.f32(512) for _ in range(2)]
            h_s = [AR.bf16(8 * 512) for _ in range(2)]
            act_s = AR.bf16(NJ * 512)
            wo_s = AR.bf16(NJ * 1024)
            wi_s = [AR.bf16(8 * 1024) for _ in range(2)]
            sg_s = [AR.f32(512) for _ in range(2)]
            b_xt = [Buf(), Buf()]
            b_ysb, b_sq, b_act, b_wo = Buf(), Buf(), Buf(), Buf()
            b_rstd = [Buf(), Buf()]
            b_t1 = [Buf(), Buf()]
            b_h = [Buf(), Buf()]
            b_wi = [Buf(), Buf()]
            b_sg = [Buf(), Buf()]
            tag = "%d%d" % (i, f)
            sem_x = [S.new_dma_sem("fx0_" + tag), S.new_dma_sem("fx1_" + tag)]
            sem_wi = [S.new_dma_sem("fwi0_" + tag), S.new_dma_sem("fwi1_" + tag)]
            sem_wo = S.new_dma_sem("fwo_" + tag)
            sem_st = S.new_dma_sem("fst_" + tag)
            pump_until("wi" + tag, "wo" + tag)
            cv_wi = conv_bufs["wi" + tag]
            cv_wo = conv_bufs["wo" + tag]
            wi_v = wi_b[i, f].rearrange("(k p) n -> p k n", p=128)
            wo_v = wo_b[i, f].rearrange("(j p) n -> p j n", p=128)
            src_v = xin_v if state["src_is_input"] else xs_v

            wo3 = wo_s.rearrange("p (j n) -> p j n", j=NJ)
            for a in range(0, NJ, 6):
                b = min(NJ, a + 6)
                S.dma("sp", sem_wo, wo3[:, a:b, :], wo_v[:, a:b, :], reads=cv_wo, writes=[b_wo])

            groups = [(a, min(NJ, a + 4)) for a in range(0, NJ, 4)]
            wi_ctr = [0]

            def load_x(ti):
                t = tiles[ti]
                t0, W, isc = TILES[t]
                sl = ti % 2
                x3 = xt_s[sl].rearrange("p (k t) -> p k t", k=8)
                S.dma("sp", sem_x[sl], x3[:, :, 0:W], src_v[:, :, t0:t0 + W], reads=[XB[t]], writes=[b_xt[sl]])

            def prenorm(ti):
                t = tiles[ti]
                t0, W, isc = TILES[t]
                sl = ti % 2
                A, Bs, G = scal(i, s, 1 if isc else 0)
                x3 = xt_s[sl].rearrange("p (k t) -> p k t", k=8)
                sq3 = sq_s.rearrange("p (k t) -> p k t", k=8)
                h3 = h_s[sl].rearrange("p (k t) -> p k t", k=8)
                S.op("act", lambda e: e.activation(out=sq3[:, :, 0:W], in_=x3[:, :, 0:W], func=AF.Square),
                     reads=[b_xt[sl]], writes=[b_sq])
                rstd_from_sq(sq3[:, :, 0:W], W, 8, PS[4], PSB[4], rstd_s[sl], b_rstd[sl], b_sq, float(D))
                for k in range(8):
                    q = k % 2
                    S.op("dve", lambda e, k=k, q=q: e.scalar_tensor_tensor(
                        t1_s[q][:, 0:W], x3[:, k, 0:W], A(k), rstd_s[sl][:, 0:W], ALU.mult, ALU.mult),
                        reads=[b_xt[sl], b_rstd[sl], b_mod], writes=[b_t1[q]])
                    S.op("act", lambda e, k=k, q=q: e.activation(
                        out=h3[:, k, 0:W], in_=t1_s[q][:, 0:W], func=AF.Identity, bias=Bs(k), scale=1.0),
                        reads=[b_t1[q], b_mod], writes=[b_h[sl]])

            def gateup(ti, gi):
                t = tiles[ti]
                t0, W, isc = TILES[t]
                sl = ti % 2
                h3 = h_s[sl].rearrange("p (k t) -> p k t", k=8)
                a, b = groups[gi]
                ng = b - a
                ws = wi_ctr[0] % 2
                wi_ctr[0] += 1
                w3 = wi_s[ws].rearrange("p (k n) -> p k n", k=8)
                S.dma("sp", sem_wi[ws], w3[:, :, 0:ng * 128], wi_v[:, :, a * 128:b * 128], reads=cv_wi, writes=[b_wi[ws]])
                S.dma("sp", sem_wi[ws], w3[:, :, 512:512 + ng * 128], wi_v[:, :, DFF + a * 128:DFF + b * 128],
                      reads=cv_wi, writes=[b_wi[ws]])
                act3 = act_s.rearrange("p (j t) -> p j t", j=NJ)
                for jj in range(ng):
                    j = a + jj
                    pp = j % 2
                    pg, pu = PS[2 * pp], PS[2 * pp + 1]
                    for k in range(8):
                        S.op("pe", lambda e, k=k, jj=jj, pg=pg: e.matmul(
                            pg[:, 0:W], w3[:, k, jj * 128:(jj + 1) * 128], h3[:, k, 0:W], start=(k == 0), stop=(k == 7)),
                            reads=[b_wi[ws], b_h[sl]], writes=[PSB[2 * pp]])
                    for k in range(8):
                        S.op("pe", lambda e, k=k, jj=jj, pu=pu: e.matmul(
                            pu[:, 0:W], w3[:, k, 512 + jj * 128:512 + (jj + 1) * 128], h3[:, k, 0:W], start=(k == 0), stop=(k == 7)),
                            reads=[b_wi[ws], b_h[sl]], writes=[PSB[2 * pp + 1]])
                    S.op("act", lambda e, pp=pp, pg=pg: e.activation(out=sg_s[pp][:, 0:W], in_=pg[:, 0:W], func=AF.Silu),
                         reads=[PSB[2 * pp]], writes=[b_sg[pp]])
                    S.op("dve", lambda e, pp=pp, pu=pu, j=j: e.tensor_tensor(
                        act3[:, j, 0:W], sg_s[pp][:, 0:W], pu[:, 0:W], ALU.mult),
                        reads=[b_sg[pp], PSB[2 * pp + 1]], writes=[b_act])

            def wout(ti):
                t = tiles[ti]
                t0, W, isc = TILES[t]
                act3 = act_s.rearrange("p (j t) -> p j t", j=NJ)
                y3 = ysb.rearrange("p (k t) -> p k t", k=8)
                sq3 = sq_s.rearrange("p (k t) -> p k t", k=8)
                for dk in range(8):
                    pb = 5 + dk % 2
                    py = PS[pb]
                    for j in range(NJ):
                        S.op("pe", lambda e, j=j, dk=dk, py=py: e.matmul(
                            py[:, 0:W], wo3[:, j, dk * 128:(dk + 1) * 128], act3[:, j, 0:W], start=(j == 0), stop=(j == NJ - 1)),
                            reads=[b_wo, b_act], writes=[PSB[pb]])
                    S.op("act", lambda e, dk=dk, py=py: e.activation(out=sq3[:, dk, 0:W], in_=py[:, 0:W], func=AF.Square),
                         reads=[PSB[pb]], writes=[b_sq])
                    S.op("dve", lambda e, dk=dk, py=py: e.tensor_copy(y3[:, dk, 0:W], py[:, 0:W]),
                         reads=[PSB[pb]], writes=[b_ysb])

            def postnorm(ti):
                t = tiles[ti]
                t0, W, isc = TILES[t]
                sl = ti % 2
                A, Bs, G = scal(i, s, 1 if isc else 0)
                x3 = xt_s[sl].rearrange("p (k t) -> p k t", k=8)
                y3 = ysb.rearrange("p (k t) -> p k t", k=8)
                sq3 = sq_s.rearrange("p (k t) -> p k t", k=8)
                rs = rstd_s[sl]
                rstd_from_sq(sq3[:, :, 0:W], W, 8, PS[4], PSB[4], rs, b_rstd[sl], b_sq, float(D))
                for k in range(8):
                    S.op("dve", lambda e, k=k: e.scalar_tensor_tensor(
                        y3[:, k, 0:W], y3[:, k, 0:W], G(k), rs[:, 0:W], ALU.mult, ALU.mult),
                        reads=[b_rstd[sl], b_mod], writes=[b_ysb])
                S.op("pool", lambda e: e.tensor_tensor(y3[:, :, 0:W], y3[:, :, 0:W], x3[:, :, 0:W], ALU.add),
                     reads=[b_xt[sl], b_ysb], writes=[b_ysb])
                if last and not isc:
                    S.dma("pool", sem_st, y_v[:, :, t0:t0 + W], y3[:, :, 0:W], reads=[b_ysb], writes=[XB[t]])
                else:
                    S.dma("pool", sem_st, xs_v[:, :, t0:t0 + W], y3[:, :, 0:W], reads=[b_ysb], writes=[XB[t]])

            n = len(tiles)
            if DBG:
                load_x(0)
                prenorm(0)
                if DBG >= 2:
                    for gi in range(len(groups)):
                        gateup(0, gi)
                if DBG >= 3:
                    wout(0)
                if DBG >= 4:
                    postnorm(0)
                S.barrier()
                return
            load_x(0)
            prenorm(0)
            for ti in range(n):
                gateup(ti, 0)
                gateup(ti, 1)
                if ti + 1 < n:
                    load_x(ti + 1)
                for gi in range(2, len(groups)):
                    gateup(ti, gi)
                if ti + 1 < n:
                    prenorm(ti + 1)
                wout(ti)
                postnorm(ti)
                pump(5)
            state["src_is_input"] = False
            S.barrier()

        def dump_and_finish():
            AR.reset()
            tb = AR.f32(8 * 512)
            t3 = tb.rearrange("p (k t) -> p k t", k=8)
            bb = Buf()
            s1 = S.new_dma_sem("dbg_l")
            s2 = S.new_dma_sem("dbg_s")
            for t in range(8):
                t0, W, _ = TILES[t]
                S.dma("sp", s1, t3, xs_v[:, :, t0:t0 + W], reads=[XB[t]], writes=[bb])
                S.dma("sp", s2, y_v[:, :, t0:t0 + W], t3, reads=[bb])
            S.barrier()

        def attn_qkv_phase(i, kind, j):
            is_da = kind == 0
            NQ = 8
            NKC = 8 if is_da else 2
            NV = NKC
            ncols = (NQ + 2 * NKC) * 128
            wsrc = (da_qkv_b if is_da else gqa_qkv_b)[j]
            rope = ropeD if is_da else ropeG
            cvb = conv_bufs[("daqkv%d" if is_da else "gqaqkv%d") % j]
            AR.reset()
            xt_s = [AR.f32(8 * 512) for _ in range(2)]
            sq_s = AR.bf16(8 * 512)
            rstd_s = [AR.f32(512) for _ in range(2)]
            t1_s = [AR.f32(512) for _ in range(2)]
            h_s = [AR.bf16(8 * 512) for _ in range(2)]
            w_s = AR.bf16(8 * ncols)
            cs_s = [AR.f32(2 * 512) for _ in range(2)]
            qk_s = [AR.bf16((NQ + NKC) * 512) for _ in range(2)]
            v_s = [AR.bf16(4 * NV * 128) for _ in range(2)]
            qb_s = [AR.bf16(512) for _ in range(2)]
            ta_s = [AR.f32(512) for _ in range(2)]
            tb_s = [AR.f32(512) for _ in range(2)]
            qn_s = [AR.f32(512) for _ in range(2)]
            rs_s = [AR.f32(512) for _ in range(2)]
            b_xt = [Buf(), Buf()]
            b_sq = Buf()
            b_rstd = [Buf(), Buf()]
            b_t1 = [Buf(), Buf()]
            b_h = [Buf(), Buf()]
            b_w = Buf()
            b_cs = [Buf(), Buf()]
            b_qk = [Buf(), Buf()]
            b_v = [Buf(), Buf()]
            b_qb = [Buf(), Buf()]
            b_ta = [Buf(), Buf()]
            b_tb = [Buf(), Buf()]
            b_qn = [Buf(), Buf()]
            b_rs = [Buf(), Buf()]
            tag = "q%d" % i
            sem_x = [S.new_dma_sem("ax0_" + tag), S.new_dma_sem("ax1_" + tag)]
            sem_cs = [S.new_dma_sem("acs0_" + tag), S.new_dma_sem("acs1_" + tag)]
            sem_w = S.new_dma_sem("aw_" + tag)
            sem_st = [S.new_dma_sem("ast0_" + tag), S.new_dma_sem("ast1_" + tag)]
            sem_sv = [S.new_dma_sem("asv0_" + tag), S.new_dma_sem("asv1_" + tag)]
            w3 = w_s.rearrange("p (k n) -> p k n", k=8)
            wv = wsrc.rearrange("(k p) n -> p k n", p=128)
            for a in range(0, ncols, 768):
                S.dma("sp", sem_w, w3[:, :, a:a + 768], wv[:, :, a:a + 768], reads=cvb, writes=[b_w])
            qk_v = qkT.rearrange("(c p) t -> p c t", p=128)
            v_v = vtok.rearrange("(tb p) e -> p tb e", p=128)
            gq = gqn_s[:, 0:1]
            gk = gqn_s[:, 1:2]
            n = len(TILES)
            cnt = [0]

            def load_x(ti):
                t0, W, isc = TILES[ti]
                sl = ti % 2
                x3 = xt_s[sl].rearrange("p (k t) -> p k t", k=8)
                S.dma("sp", sem_x[sl], x3[:, :, 0:W], xs_v[:, :, t0:t0 + W], reads=[XB[ti]], writes=[b_xt[sl]])
                if not isc:
                    c3 = cs_s[sl].rearrange("p (a t) -> p a t", a=2)
                    S.dma("sp", sem_cs[sl], c3[:, :, 0:W], rope.rearrange("a p t -> p a t")[:, :, t0:t0 + W], writes=[b_cs[sl]])

            def prenorm(ti):
                t0, W, isc = TILES[ti]
                sl = ti % 2
                A, Bs, G = scal(i, 1, 1 if isc else 0)
                x3 = xt_s[sl].rearrange("p (k t) -> p k t", k=8)
                sq3 = sq_s.rearrange("p (k t) -> p k t", k=8)
                h3 = h_s[sl].rearrange("p (k t) -> p k t", k=8)
                S.op("act", lambda e: e.activation(out=sq3[:, :, 0:W], in_=x3[:, :, 0:W], func=AF.Square),
                     reads=[b_xt[sl]], writes=[b_sq])
                rstd_from_sq(sq3[:, :, 0:W], W, 8, PS[7], PSB[7], rstd_s[sl], b_rstd[sl], b_sq, float(D))
                for k in range(8):
                    q = k % 2
                    S.op("dve", lambda e, k=k, q=q: e.scalar_tensor_tensor(
                        t1_s[q][:, 0:W], x3[:, k, 0:W], A(k), rstd_s[sl][:, 0:W], ALU.mult, ALU.mult),
                        reads=[b_xt[sl], b_rstd[sl], b_mod], writes=[b_t1[q]])
                    S.op("act", lambda e, k=k, q=q: e.activation(
                        out=h3[:, k, 0:W], in_=t1_s[q][:, 0:W], func=AF.Identity, bias=Bs(k), scale=1.0),
                        reads=[b_t1[q], b_mod], writes=[b_h[sl]])

            def project(ti):
                t0, W, isc = TILES[ti]
                sl = ti % 2
                h3 = h_s[sl].rearrange("p (k t) -> p k t", k=8)
                qk3 = qk_s[sl].rearrange("p (c t) -> p c t", c=NQ + NKC)
                cos_t = cs_s[sl][:, 0:W]
                sin_t = cs_s[sl][:, 512:512 + W]
                for c in range(NQ + NKC):
                    u = cnt[0] % 2
                    cnt[0] += 1
                    pb = u
                    pr = 2 + u
                    ps = PS[pb]
                    for k in range(8):
                        S.op("pe", lambda e, k=k, c=c, ps=ps: e.matmul(
                            ps[:, 0:W], w3[:, k, c * 128:(c + 1) * 128], h3[:, k, 0:W], start=(k == 0), stop=(k == 7)),
                            reads=[b_w, b_h[sl]], writes=[PSB[pb]])
                    if is_da:
                        src = ps[:, 0:W]
                        srcb = [PSB[pb]]
                    else:
                        gain = gq if c < NQ else gk
                        S.op("act", lambda e, u=u, ps=ps: e.activation(out=qb_s[u][:, 0:W], in_=ps[:, 0:W], func=AF.Square),
                             reads=[PSB[pb]], writes=[b_qb[u]])
                        S.op("pe", lambda e, u=u: e.matmul(PS[4 + u][:, 0:W], ones_bf, qb_s[u][:, 0:W], start=True, stop=True),
                             reads=[b_qb[u], b_c2], writes=[PSB[4 + u]])
                        S.op("act", lambda e, u=u: e.activation(out=rs_s[u][:, 0:W], in_=PS[4 + u][:, 0:W], func=AF.Sqrt,
                                                                scale=1.0 / 128.0, bias=eps_t),
                             reads=[PSB[4 + u], b_c2], writes=[b_rs[u]])
                        S.op("dve", lambda e, u=u: e.reciprocal(rs_s[u][:, 0:W], rs_s[u][:, 0:W]), reads=[b_rs[u]], writes=[b_rs[u]])
                        S.op("dve", lambda e, u=u, ps=ps, gain=gain: e.scalar_tensor_tensor(
                            qn_s[u][:, 0:W], ps[:, 0:W], gain, rs_s[u][:, 0:W], ALU.mult, ALU.mult),
                            reads=[PSB[pb], b_rs[u], b_const], writes=[b_qn[u]])
                        src = qn_s[u][:, 0:W]
                        srcb = [b_qn[u]]
                    if isc:
                        S.op("act", lambda e, c=c, src=src: e.activation(out=qk3[:, c, 0:W], in_=src, func=AF.Copy),
                             reads=srcb, writes=[b_qk[sl]])
                    else:
                        S.op("act", lambda e, u=u, src=src: e.activation(out=qb_s[u][:, 0:W], in_=src, func=AF.Copy),
                             reads=srcb, writes=[b_qb[u]])
                        S.op("pe", lambda e, u=u, pr=pr: e.matmul(PS[pr][:, 0:W], rotm_bf, qb_s[u][:, 0:W], start=True, stop=True),
                             reads=[b_qb[u], b_c2], writes=[PSB[pr]])
                        S.op("dve", lambda e, u=u, src=src: e.tensor_tensor(ta_s[u][:, 0:W], src, cos_t, ALU.mult),
                             reads=srcb + [b_cs[sl]], writes=[b_ta[u]])
                        S.op("dve", lambda e, u=u, pr=pr: e.tensor_tensor(tb_s[u][:, 0:W], PS[pr][:, 0:W], sin_t, ALU.mult),
                             reads=[PSB[pr], b_cs[sl]], writes=[b_tb[u]])
                        S.op("pool", lambda e, u=u, c=c: e.tensor_tensor(qk3[:, c, 0:W], ta_s[u][:, 0:W], tb_s[u][:, 0:W], ALU.add),
                             reads=[b_ta[u], b_tb[u]], writes=[b_qk[sl]])
                ntb = W // 128
                vcols = NV * 128
                v3 = v_s[sl].rearrange("p (tb e) -> p tb e", tb=4)
                voff = (NQ + NKC) * 128
                for tb in range(ntb):
                    for c0 in range(0, vcols, 512):
                        cw = min(512, vcols - c0)
                        u = cnt[0] % 2
                        cnt[0] += 1
                        pb = u
                        ps = PS[pb]
                        for k in range(8):
                            S.op("pe", lambda e, k=k, tb=tb, c0=c0, cw=cw, ps=ps: e.matmul(
                                ps[:, 0:cw], h3[:, k, tb * 128:(tb + 1) * 128], w3[:, k, voff + c0:voff + c0 + cw],
                                start=(k == 0), stop=(k == 7)),
                                reads=[b_w, b_h[sl]], writes=[PSB[pb]])
                        if u == 0:
                            S.op("act", lambda e, tb=tb, c0=c0, cw=cw, ps=ps: e.activation(
                                out=v3[:, tb, c0:c0 + cw], in_=ps[:, 0:cw], func=AF.Copy),
                                reads=[PSB[pb]], writes=[b_v[sl]])
                        else:
                            S.op("dve", lambda e, tb=tb, c0=c0, cw=cw, ps=ps: e.tensor_copy(
                                v3[:, tb, c0:c0 + cw], ps[:, 0:cw]),
                                reads=[PSB[pb]], writes=[b_v[sl]])
                S.dma("pool", sem_st[sl], qk_v[:, 0:NQ + NKC, t0:t0 + W], qk3[:, :, 0:W], reads=[b_qk[sl]])
                S.dma("pool", sem_sv[sl], v_v[:, t0 // 128:t0 // 128 + ntb, 0:vcols], v3[:, 0:ntb, 0:vcols],
                      reads=[b_v[sl]])

            load_x(0)
            prenorm(0)
            for ti in range(n):
                if ti + 1 < n:
                    load_x(ti + 1)
                project(ti)
                if ti + 1 < n:
                    prenorm(ti + 1)
            S.barrier()

        def attn_core_phase(i, kind, j):
            is_da = kind == 0
            need_ctx = i < DEPTH - 1
            NQ = 8
            NKC = 8 if is_da else 2
            jj = j
            AR.reset()
            q_s = [AR.bf16(T) for _ in range(2)]
            k_s = [AR.bf16(T) for _ in range(2)]
            k2_s = [AR.bf16(T) for _ in range(2)] if is_da else None
            v_s = [AR.bf16(34 * 128) for _ in range(2)]
            NPT = 6
            pT_s = [AR.bf16(512) for _ in range(NPT)]
            pr_s = [AR.bf16(512) for _ in range(2)]
            b_pr = [Buf(), Buf()]
            r_s = [AR.f32(512) for _ in range(2)]
            dacc_s = [AR.f32(512) for _ in range(2)]
            b_dacc = [Buf(), Buf()]
            o_s = [AR.f32(512) for _ in range(2)]
            osq_s = AR.bf16(512)
            rs_s = AR.f32(512)
            on_s = [AR.bf16(512) for _ in range(2)]
            b_hd = [Buf(), Buf()]
            b_pT = [Buf() for _ in range(NPT)]
            b_r = [Buf(), Buf()]
            b_o = [Buf(), Buf()]
            b_osq, b_rs = Buf(), Buf()
            b_on = [Buf(), Buf()]
            tag = "c%d" % i
            sem_h = [S.new_dma_sem("ch0_" + tag), S.new_dma_sem("ch1_" + tag)]
            sem_st = [S.new_dma_sem("cst0_" + tag), S.new_dma_sem("cst1_" + tag)]
            v_v = vtok.rearrange("(c p) e -> p c e", p=128)
            qtiles = [t for t in range(len(TILES)) if need_ctx or not TILES[t][2]]
            maps = [(0, 64), (64, 64)] if is_da else [(0, 128)]
            nm = len(maps)
            sc = 0.125 if is_da else 128.0 ** -0.5
            neglam = da_neglam[:, jj:jj + 1]
            subs = da_subs[:, jj:jj + 1]
            octr = [0]
            pctr = [0]

            def load_head(hq):
                sl = hq % 2
                kc = NQ + (hq if is_da else hq // 4)
                vc = hq if is_da else hq // 4
                S.dma("sp", sem_h[sl], q_s[sl], qkT[hq * 128:(hq + 1) * 128, :], writes=[b_hd[sl]])
                if is_da:
                    S.dma("sp", sem_h[sl], k_s[sl][0:64, :], qkT[kc * 128:kc * 128 + 64, :], reads=[b_kz], writes=[b_hd[sl]])
                    S.dma("sp", sem_h[sl], k2_s[sl][64:128, :], qkT[kc * 128 + 64:(kc + 1) * 128, :], reads=[b_kz], writes=[b_hd[sl]])
                else:
                    S.dma("sp", sem_h[sl], k_s[sl], qkT[kc * 128:(kc + 1) * 128, :], writes=[b_hd[sl]])
                vd = v_s[sl].rearrange("p (c e) -> p c e", c=34)
                for c0 in range(0, 34, 8):
                    c1 = min(34, c0 + 8)
                    S.dma("sp", sem_h[sl], vd[:, c0:c1, :], v_v[:, c0:c1, vc * 128:(vc + 1) * 128],
                          writes=[b_hd[sl]])

            def head(hq):
                sl = hq % 2
                v3 = v_s[sl].rearrange("p (c e) -> p c e", c=34)
                for t in qtiles:
                    qtile(hq, sl, v3, t)

            def qtile(hq, sl, v3, t):
                if True:
                    t0, W, isc = TILES[t]
                    kcs = [32, 33] if isc else list(range(34))
                    steps = [(kc, m) for kc in kcs for m in range(nm)]

                    def score(n):
                        kc, m = steps[n]
                        r0, rn = maps[m]
                        pb = pctr[0] % 2
                        kk_ = k2_s[sl] if (is_da and m == 1) else k_s[sl]
                        S.op("pe", lambda e, kc=kc, pb=pb, kk_=kk_: e.matmul(
                            PS[pb][:, 0:W], kk_[:, kc * 128:(kc + 1) * 128], q_s[sl][:, t0:t0 + W],
                            start=True, stop=True),
                            reads=[b_hd[sl]], writes=[PSB[pb]])
                        pctr[0] += 1
                        return pb

                    pend = score(0)
                    for n in range(len(steps)):
                        kc, m = steps[n]
                        pb = pend
                        if n + 1 < len(steps):
                            pend = score(n + 1)
                        u = n % NPT
                        S.op("act", lambda e, pb=pb, u=u: e.activation(out=pT_s[u][:, 0:W], in_=PS[pb][:, 0:W], func=AF.Exp, scale=sc),
                             reads=[PSB[pb]], writes=[b_pT[u]])
                        first = (n < nm)
                        lastk = (n >= len(steps) - nm)
                        S.op("pe", lambda e, kc=kc, m=m, u=u, first=first, lastk=lastk: e.matmul(
                            PS[2 + m][:, 0:W], v3[:, kc, :], pT_s[u][:, 0:W], start=first, stop=lastk),
                            reads=[b_hd[sl], b_pT[u]], writes=[PSB[2 + m]])
                        ik = n // nm
                        if ik % 2 == 1:
                            up = (n - nm) % NPT
                            if ik == 1:
                                S.op("dve", lambda e, m=m, u=u, up=up: e.tensor_tensor(
                                    dacc_s[m][:, 0:W], pT_s[up][:, 0:W], pT_s[u][:, 0:W], ALU.add),
                                    reads=[b_pT[up], b_pT[u]], writes=[b_dacc[m]])
                            else:
                                S.op("dve", lambda e, m=m, u=u, up=up: e.tensor_tensor(
                                    pr_s[m][:, 0:W], pT_s[up][:, 0:W], pT_s[u][:, 0:W], ALU.add),
                                    reads=[b_pT[up], b_pT[u]], writes=[b_pr[m]])
                                S.op("dve", lambda e, m=m: e.tensor_tensor(
                                    dacc_s[m][:, 0:W], dacc_s[m][:, 0:W], pr_s[m][:, 0:W], ALU.add),
                                    reads=[b_pr[m]], writes=[b_dacc[m]])
                        if lastk:
                            S.op("pe", lambda e, m=m: e.matmul(PS[4 + m][:, 0:W], ones_f, dacc_s[m][:, 0:W], start=True, stop=True),
                                 reads=[b_c2, b_dacc[m]], writes=[PSB[4 + m]])
                    oc = octr[0] % 2
                    octr[0] += 1
                    for m in range(nm):
                        S.op("dve", lambda e, m=m: e.reciprocal(r_s[m][:, 0:W], PS[4 + m][:, 0:W]),
                             reads=[PSB[4 + m]], writes=[b_r[m]])
                    if is_da:
                        S.op("dve", lambda e: e.tensor_tensor(o_s[0][:, 0:W], PS[2][:, 0:W], r_s[0][:, 0:W], ALU.mult),
                             reads=[PSB[2], b_r[0]], writes=[b_o[0]])
                        S.op("dve", lambda e: e.scalar_tensor_tensor(o_s[1][:, 0:W], PS[3][:, 0:W], neglam, r_s[1][:, 0:W], ALU.mult, ALU.mult),
                             reads=[PSB[3], b_r[1], b_da], writes=[b_o[1]])
                        S.op("pool", lambda e: e.tensor_tensor(o_s[0][:, 0:W], o_s[0][:, 0:W], o_s[1][:, 0:W], ALU.add),
                             reads=[b_o[1]], writes=[b_o[0]])
                        S.op("act", lambda e: e.activation(out=osq_s[:, 0:W], in_=o_s[0][:, 0:W], func=AF.Square),
                             reads=[b_o[0]], writes=[b_osq])
                        S.op("pe", lambda e: e.matmul(PS[6][:, 0:W], ones_bf, osq_s[:, 0:W], start=True, stop=True),
                             reads=[b_osq, b_c2], writes=[PSB[6]])
                        S.op("act", lambda e: e.activation(out=rs_s[:, 0:W], in_=PS[6][:, 0:W], func=AF.Sqrt, scale=1.0 / 128.0, bias=eps_t),
                             reads=[PSB[6], b_c2], writes=[b_rs])
                        S.op("dve", lambda e: e.reciprocal(rs_s[:, 0:W], rs_s[:, 0:W]), reads=[b_rs], writes=[b_rs])
                        S.op("dve", lambda e, oc=oc: e.scalar_tensor_tensor(on_s[oc][:, 0:W], o_s[0][:, 0:W], subs, rs_s[:, 0:W], ALU.mult, ALU.mult),
                             reads=[b_o[0], b_rs, b_da], writes=[b_on[oc]])
                    else:
                        S.op("dve", lambda e, oc=oc: e.tensor_tensor(on_s[oc][:, 0:W], PS[2][:, 0:W], r_s[0][:, 0:W], ALU.mult),
                             reads=[PSB[2], b_r[0]], writes=[b_on[oc]])
                    S.dma("pool", sem_st[oc], aT[hq * 128:(hq + 1) * 128, t0:t0 + W], on_s[oc][:, 0:W], reads=[b_on[oc]])

            b_kz = Buf()
            if is_da:
                for sl_ in range(2):
                    S.op("pool", lambda e, sl_=sl_: e.memset(k_s[sl_][64:128, :], 0.0), writes=[b_kz])
                    S.op("pool", lambda e, sl_=sl_: e.memset(k2_s[sl_][0:64, :], 0.0), writes=[b_kz])
            load_head(0)
            for hq in range(NQ):
                if hq + 1 < NQ:
                    load_head(hq + 1)
                head(hq)
            S.barrier()

        def mixer_out_phase(i, wsrc, cvb):
            need_ctx = i < DEPTH - 1
            tiles = [t for t in range(len(TILES)) if need_ctx or not TILES[t][2]]
            AR.reset()
            xt_s = [AR.f32(8 * 512) for _ in range(2)]
            a_s = [AR.bf16(8 * 512) for _ in range(2)]
            ysb = AR.f32(8 * 512)
            sq_s = AR.bf16(8 * 512)
            rstd_s = AR.f32(512)
            w_s = AR.bf16(8 * 1024)
            b_xt = [Buf(), Buf()]
            b_a = [Buf(), Buf()]
            b_ysb, b_sq, b_rstd, b_w = Buf(), Buf(), Buf(), Buf()
            tag = "o%d" % i
            sem_x = [S.new_dma_sem("ox0_" + tag), S.new_dma_sem("ox1_" + tag)]
            sem_a = [S.new_dma_sem("oa0_" + tag), S.new_dma_sem("oa1_" + tag)]
            sem_w = S.new_dma_sem("ow_" + tag)
            sem_st = S.new_dma_sem("ost_" + tag)
            w3 = w_s.rearrange("p (k n) -> p k n", k=8)
            S.dma("sp", sem_w, w3, wsrc.rearrange("(k p) n -> p k n", p=128), reads=cvb, writes=[b_w])
            a_v = aT.rearrange("(k p) t -> p k t", p=128)

            def load(ti):
                t = tiles[ti]
                t0, W, isc = TILES[t]
                sl = ti % 2
                x3 = xt_s[sl].rearrange("p (k t) -> p k t", k=8)
                a3 = a_s[sl].rearrange("p (k t) -> p k t", k=8)
                S.dma("sp", sem_x[sl], x3[:, :, 0:W], xs_v[:, :, t0:t0 + W], reads=[XB[t]], writes=[b_xt[sl]])
                S.dma("sp", sem_a[sl], a3[:, :, 0:W], a_v[:, :, t0:t0 + W], writes=[b_a[sl]])

            def compute(ti):
                t = tiles[ti]
                t0, W, isc = TILES[t]
                sl = ti % 2
                A, Bs, G = scal(i, 1, 1 if isc else 0)
                x3 = xt_s[sl].rearrange("p (k t) -> p k t", k=8)
                a3 = a_s[sl].rearrange("p (k t) -> p k t", k=8)
                y3 = ysb.rearrange("p (k t) -> p k t", k=8)
                sq3 = sq_s.rearrange("p (k t) -> p k t", k=8)
                for dk in range(8):
                    pb = dk % 2
                    py = PS[pb]
                    for k in range(8):
                        S.op("pe", lambda e, k=k, dk=dk, py=py: e.matmul(
                            py[:, 0:W], w3[:, k, dk * 128:(dk + 1) * 128], a3[:, k, 0:W], start=(k == 0), stop=(k == 7)),
                            reads=[b_w, b_a[sl]], writes=[PSB[pb]])
                    S.op("act", lambda e, dk=dk, py=py: e.activation(out=sq3[:, dk, 0:W], in_=py[:, 0:W], func=AF.Square),
                         reads=[PSB[pb]], writes=[b_sq])
                    S.op("dve", lambda e, dk=dk, py=py: e.tensor_copy(y3[:, dk, 0:W], py[:, 0:W]),
                         reads=[PSB[pb]], writes=[b_ysb])
                rstd_from_sq(sq3[:, :, 0:W], W, 8, PS[4], PSB[4], rstd_s, b_rstd, b_sq, float(D))
                for k in range(8):
                    S.op("dve", lambda e, k=k: e.scalar_tensor_tensor(
                        y3[:, k, 0:W], y3[:, k, 0:W], G(k), rstd_s[:, 0:W], ALU.mult, ALU.mult),
                        reads=[b_rstd, b_mod], writes=[b_ysb])
                S.op("pool", lambda e: e.tensor_tensor(y3[:, :, 0:W], y3[:, :, 0:W], x3[:, :, 0:W], ALU.add),
                     reads=[b_xt[sl], b_ysb], writes=[b_ysb])
                S.dma("pool", sem_st, xs_v[:, :, t0:t0 + W], y3[:, :, 0:W], reads=[b_ysb], writes=[XB[t]])

            n = len(tiles)
            load(0)
            for ti in range(n):
                if ti + 1 < n:
                    load(ti + 1)
                compute(ti)
            S.barrier()

        NCH = T // 64
        NBL = T // 128

        def hgrn_proj_phase(i, j):
            cvb = conv_bufs["hgin%d" % j]
            AR.reset()
            xt_s = AR.f32(8 * 512)
            sq_s = AR.bf16(8 * 512)
            rstd_s = AR.f32(512)
            t1_s = [AR.f32(512) for _ in range(2)]
            h_s = [AR.bf16(8 * 512) for _ in range(2)]
            w_s = AR.bf16(8 * 5120)
            stg = [AR.f32(8 * 512) for _ in range(2)]
            v_s = AR.bf16(4 * 1024)
            sg_s = [AR.f32(512) for _ in range(2)]
            b_xt, b_sq, b_rstd, b_w, b_v = Buf(), Buf(), Buf(), Buf(), Buf()
            b_t1 = [Buf(), Buf()]
            b_h = [Buf(), Buf()]
            b_stg = [Buf(), Buf()]
            b_sg = [Buf(), Buf()]
            tag = "hp%d" % i
            sem_x = S.new_dma_sem("hx_" + tag)
            sem_w = S.new_dma_sem("hw_" + tag)
            sem_st = [S.new_dma_sem("hst0_" + tag), S.new_dma_sem("hst1_" + tag)]
            sem_sv = S.new_dma_sem("hsv_" + tag)
            w3 = w_s.rearrange("p (k n) -> p k n", k=8)
            wv = hg_in_b[j].rearrange("(k p) n -> p k n", p=128)
            for a in range(0, 5120, 1024):
                S.dma("sp", sem_w, w3[:, :, a:a + 1024], wv[:, :, a:a + 1024], reads=cvb, writes=[b_w])
            v_v = vtok.rearrange("(tb p) e -> p tb e", p=128)
            dst_f32 = [hq_T, hlf_T[0], hkk_T[0], hlf_T[1], hkk_T[1]]
            n = len(TILES)
            cnt = [0]
            scnt = [0]

            def load_x(ti):
                t0, W, isc = TILES[ti]
                x3 = xt_s.rearrange("p (k t) -> p k t", k=8)
                S.dma("sp", sem_x, x3[:, :, 0:W], xs_v[:, :, t0:t0 + W], reads=[XB[ti]], writes=[b_xt])

            def prenorm(ti):
                t0, W, isc = TILES[ti]
                sl = ti % 2
                A, Bs, G = scal(i, 1, 1 if isc else 0)
                x3 = xt_s.rearrange("p (k t) -> p k t", k=8)
                sq3 = sq_s.rearrange("p (k t) -> p k t", k=8)
                h3 = h_s[sl].rearrange("p (k t) -> p k t", k=8)
                S.op("act", lambda e: e.activation(out=sq3[:, :, 0:W], in_=x3[:, :, 0:W], func=AF.Square),
                     reads=[b_xt], writes=[b_sq])
                rstd_from_sq(sq3[:, :, 0:W], W, 8, PS[7], PSB[7], rstd_s, b_rstd, b_sq, float(D))
                for k in range(8):
                    q = k % 2
                    S.op("dve", lambda e, k=k, q=q: e.scalar_tensor_tensor(
                        t1_s[q][:, 0:W], x3[:, k, 0:W], A(k), rstd_s[:, 0:W], ALU.mult, ALU.mult),
                        reads=[b_xt, b_rstd, b_mod], writes=[b_t1[q]])
                    S.op("act", lambda e, k=k, q=q: e.activation(
                        out=h3[:, k, 0:W], in_=t1_s[q][:, 0:W], func=AF.Identity, bias=Bs(k), scale=1.0),
                        reads=[b_t1[q], b_mod], writes=[b_h[sl]])

            def proj_chunk(ti, col0):
                t0, W, isc = TILES[ti]
                sl = ti % 2
                h3 = h_s[sl].rearrange("p (k t) -> p k t", k=8)
                u = cnt[0] % 4
                cnt[0] += 1
                ps = PS[u]
                for k in range(8):
                    S.op("pe", lambda e, k=k, ps=ps: e.matmul(
                        ps[:, 0:W], w3[:, k, col0:col0 + 128], h3[:, k, 0:W], start=(k == 0), stop=(k == 7)),
                        reads=[b_w, b_h[sl]], writes=[PSB[u]])
                return u

            def project(ti):
                t0, W, isc = TILES[ti]
                sl = ti % 2
                h3 = h_s[sl].rearrange("p (k t) -> p k t", k=8)
                g = scnt[0] % 2
                scnt[0] += 1
                s3 = stg[g].rearrange("p (k t) -> p k t", k=8)
                for hh in range(8):
                    u = proj_chunk(ti, hh * 128)
                    S.op("act", lambda e, hh=hh, u=u, s3=s3: e.activation(out=s3[:, hh, 0:W], in_=PS[u][:, 0:W], func=AF.Silu),
                         reads=[PSB[u]], writes=[b_stg[g]])
                S.dma("pool", sem_st[g], hq_T.rearrange("(k p) t -> p k t", p=128)[:, :, t0:t0 + W], s3[:, :, 0:W], reads=[b_stg[g]])
                for d in range(2):
                    g1 = scnt[0] % 2
                    g2 = (scnt[0] + 1) % 2
                    scnt[0] += 2
                    l3 = stg[g1].rearrange("p (k t) -> p k t", k=8)
                    k3 = stg[g2].rearrange("p (k t) -> p k t", k=8)
                    for hh in range(8):
                        u = proj_chunk(ti, (1 + d) * 1024 + hh * 128)
                        q = hh % 2
                        S.op("act", lambda e, u=u, q=q: e.activation(out=sg_s[q][:, 0:W], in_=PS[u][:, 0:W], func=AF.Sigmoid),
                             reads=[PSB[u]], writes=[b_sg[q]])
                        S.op("dve", lambda e, q=q, d=d, hh=hh: e.tensor_scalar(
                            sg_s[q][:, 0:W], sg_s[q][:, 0:W], hg_oml[:, d * 8 + hh:d * 8 + hh + 1], hg_lb[:, d * 8 + hh:d * 8 + hh + 1],
                            ALU.mult, ALU.add),
                            reads=[b_hg], writes=[b_sg[q]])
                        S.op("act", lambda e, q=q, hh=hh, l3=l3: e.activation(out=l3[:, hh, 0:W], in_=sg_s[q][:, 0:W], func=AF.Ln),
                             reads=[b_sg[q]], writes=[b_stg[g1]])
                        S.op("dve", lambda e, q=q, hh=hh, k3=k3: e.tensor_scalar(
                            k3[:, hh, 0:W], sg_s[q][:, 0:W], -1.0, 1.0, ALU.mult, ALU.add),
                            reads=[b_sg[q]], writes=[b_stg[g2]])
                    S.dma("pool", sem_st[g1], hlf_T[d].rearrange("(k p) t -> p k t", p=128)[:, :, t0:t0 + W], l3[:, :, 0:W], reads=[b_stg[g1]])
                    S.dma("pool", sem_st[g2], hkk_T[d].rearrange("(k p) t -> p k t", p=128)[:, :, t0:t0 + W], k3[:, :, 0:W], reads=[b_stg[g2]])
                g = scnt[0] % 2
                scnt[0] += 1
                gb3 = stg[g].bitcast(BF16)[:, 0:8 * 512].rearrange("p (k t) -> p k t", k=8)
                for hh in range(8):
                    u = proj_chunk(ti, 4096 + hh * 128)
                    S.op("act", lambda e, hh=hh, u=u, gb3=gb3: e.activation(out=gb3[:, hh, 0:W], in_=PS[u][:, 0:W], func=AF.Silu),
                         reads=[PSB[u]], writes=[b_stg[g]])
                S.dma("pool", sem_st[g], qkT.rearrange("(c p) t -> p c t", p=128)[:, 0:8, t0:t0 + W], gb3[:, :, 0:W], reads=[b_stg[g]])
                ntb = W // 128
                v3 = v_s.rearrange("p (tb e) -> p tb e", tb=4)
                for tb in range(ntb):
                    for c0 in range(0, 1024, 512):
                        u = cnt[0] % 4
                        cnt[0] += 1
                        ps = PS[u]
                        for k in range(8):
                            S.op("pe", lambda e, k=k, tb=tb, c0=c0, ps=ps: e.matmul(
                                ps[:, 0:512], h3[:, k, tb * 128:(tb + 1) * 128], w3[:, k, 3072 + c0:3072 + c0 + 512],
                                start=(k == 0), stop=(k == 7)),
                                reads=[b_w, b_h[sl]], writes=[PSB[u]])
                        S.op("dve", lambda e, tb=tb, c0=c0, ps=ps: e.tensor_copy(v3[:, tb, c0:c0 + 512], ps[:, 0:512]),
                             reads=[PSB[u]], writes=[b_v])
                S.dma("pool", sem_sv, v_v[:, t0 // 128:t0 // 128 + ntb, :], v3[:, 0:ntb, :], reads=[b_v])

            load_x(0)
            prenorm(0)
            for ti in range(n):
                if ti + 1 < n:
                    load_x(ti + 1)
                project(ti)
                if ti + 1 < n:
                    prenorm(ti + 1)
            S.barrier()

        def hgrn_scan_phase(i, j):
            AR.reset()
            bufA = AR.f32(T)
            bufB = AR.f32(T)
            bufC = AR.f32(T)
            bufQ = AR.f32(T)
            v_s = AR.bf16(NBL * 128)
            gs_s = AR.bf16(T)
            qt_s = AR.bf16(T)
            qh_s = AR.bf16(T)
            kt_s = AR.bf16(T)
            kh_s = AR.bf16(T)
            gm_s = AR.f32(NCH)
            ktok_s = AR.bf16(NBL * 128)
            sbef_s = AR.bf16(NCH * 128)
            st_s = [AR.f32(128) for _ in range(2)]
            att_s = [AR.bf16(512) for _ in range(2)]
            attf_s = AR.f32(512)
            b_attf = Buf()
            oacc = AR.f32(T)
            gl_s = AR.f32(NCH)
            dec_s = AR.f32(NCH)
            mask_s = AR.f32(512)
            msk_s = [AR.bf16(512), AR.bf16(512)]
            osq_s = AR.bf16(512)
            rs_s = AR.f32(512)
            on_s = [AR.bf16(512) for _ in range(2)]
            bA, bB, bC, bQ, bV, bG = Buf(), Buf(), Buf(), Buf(), Buf(), Buf()
            b_qt, b_qh, b_kt, b_ktok, b_sbef = Buf(), Buf(), Buf(), Buf(), Buf()
            b_kh, b_gm = Buf(), Buf()
            b_st = [Buf(), Buf()]
            b_att = [Buf(), Buf()]
            b_oacc, b_gl, b_dec, b_mask = Buf(), Buf(), Buf(), Buf()
            b_osq, b_rs = Buf(), Buf()
            b_on = [Buf(), Buf()]
            tag = "hs%d" % i
            semA, semB, semQ, semV, semG = (S.new_dma_sem(n_ + tag) for n_ in ("hA", "hB", "hQ", "hV", "hG"))
            sem_o = [S.new_dma_sem("ho0" + tag), S.new_dma_sem("ho1" + tag)]
            semM = S.new_dma_sem("hM" + tag)
            v_v = vtok.rearrange("(c p) e -> p c e", p=128)
            S.op("dve", lambda e: e.memset(mask_s, 1.0), writes=[b_mask])
            S.op("dve", lambda e: e.memset(mask_s.rearrange("p (c j) -> p c j", j=64)[:, :, 0:1], 0.0), writes=[b_mask])
            mtmp = AR.f32(256)
            b_mt = Buf()
            S.dma("sp", semM, mtmp, hmask[:, :], writes=[b_mt])
            for dd in range(2):
                for r in range(4):
                    S.op("dve", lambda e, dd=dd, r=r: e.tensor_copy(msk_s[dd][:, r * 128:(r + 1) * 128], mtmp[:, dd * 128:(dd + 1) * 128]),
                         reads=[b_mt], writes=[b_mask])
            ord_f = [64, 65, 66, 67] + list(range(64))
            ord_b = [67, 66, 65, 64] + list(range(63, -1, -1))
            A3 = bufA.rearrange("p (c j) -> p c j", j=64)
            C3 = bufC.rearrange("p (c j) -> p c j", j=64)
            glb = gl_s.unsqueeze(2).to_broadcast([128, NCH, 64])
            pctr = [0]
            octr = [0]

            def direction(hh, d):
                S.dma("sp", semA, bufA, hlf_T[d][hh * 128:(hh + 1) * 128, :], writes=[bA])
                S.dma("sp", semB, bufB, hkk_T[d][hh * 128:(hh + 1) * 128, :], writes=[bB])
                for (t0, W, isc) in TILES:
                    S.op("dve", lambda e, t0=t0, W=W: e.tensor_tensor_scan(
                        bufC[:, t0:t0 + W], mask_s[:, 0:W], bufA[:, t0:t0 + W], 0.0, ALU.mult, ALU.add),
                        reads=[bA, b_mask], writes=[bC])
                S.op("dve", lambda e: e.tensor_copy(gl_s.unsqueeze(2), C3[:, :, 63:64]), reads=[bC], writes=[b_gl])
                S.op("act", lambda e: e.activation(out=dec_s, in_=gl_s, func=AF.Exp), reads=[b_gl], writes=[b_dec])
                if d == 0:
                    S.op("dve", lambda e: e.tensor_tensor(A3, C3, glb, ALU.subtract), reads=[bC, b_gl], writes=[bA])
                else:
                    S.op("dve", lambda e: e.tensor_tensor(bufA, bufA, bufC, ALU.subtract), reads=[bC], writes=[bA])
                    S.op("dve", lambda e: e.tensor_tensor(C3, A3, glb, ALU.add), reads=[bA, b_gl], writes=[bC])
                S.op("act", lambda e: e.activation(out=bufC, in_=bufC, func=AF.Exp), reads=[], writes=[bC])
                S.op("dve", lambda e: e.tensor_tensor(qt_s, bufQ, bufC, ALU.mult), reads=[bQ, bC], writes=[b_qt])
                S.op("act", lambda e: e.activation(out=bufC, in_=bufA, func=AF.Exp, scale=-1.0), reads=[bA], writes=[bC])
                S.op("dve", lambda e: e.tensor_tensor(kt_s, bufB, bufC, ALU.mult), reads=[bB, bC], writes=[b_kt])
                S.op("dve", lambda e: e.tensor_copy(gm_s.unsqueeze(2), A3[:, :, 32:33]), reads=[bA], writes=[b_gm])
                S.op("dve", lambda e: e.tensor_tensor(A3, A3, gm_s.unsqueeze(2).to_broadcast([128, NCH, 64]), ALU.subtract),
                     reads=[b_gm], writes=[bA])
                S.op("dve", lambda e: e.tensor_scalar(bufA, bufA, -80.0, 80.0, ALU.max, ALU.min), reads=[], writes=[bA])
                S.op("act", lambda e: e.activation(out=bufC, in_=bufA, func=AF.Exp), reads=[bA], writes=[bC])
                S.op("dve", lambda e: e.tensor_tensor(qh_s, bufQ, bufC, ALU.mult), reads=[bQ, bC], writes=[b_qh])
                S.op("act", lambda e: e.activation(out=bufC, in_=bufA, func=AF.Exp, scale=-1.0), reads=[bA], writes=[bC])
                S.op("dve", lambda e: e.tensor_tensor(kh_s, bufB, bufC, ALU.mult), reads=[bB, bC], writes=[b_kh])
                kt3 = ktok_s.rearrange("p (b k) -> p b k", b=NBL)
                for b0 in range(0, NBL, 8):
                    b1 = min(NBL, b0 + 8)
                    pb = 6 + (pctr[0] % 2)
                    pctr[0] += 1
                    pbf = PS[pb][:, :].bitcast(BF16)
                    for b in range(b0, b1):
                        S.op("pe", lambda e, b=b, b0=b0, pbf=pbf: e.transpose(
                            pbf[:, (b - b0) * 128:(b - b0 + 1) * 128], kt_s[:, b * 128:(b + 1) * 128], ident_bf),
                            reads=[b_kt, b_c2], writes=[PSB[pb]])
                    S.op("act", lambda e, b0=b0, b1=b1, pbf=pbf: e.activation(
                        out=ktok_s[:, b0 * 128:b1 * 128], in_=pbf[:, 0:(b1 - b0) * 128], func=AF.Copy),
                        reads=[PSB[pb]], writes=[b_ktok])
                order = ord_f if d == 0 else ord_b
                v3 = v_s.rearrange("p (b e) -> p b e", b=NBL)
                sb3 = sbef_s.rearrange("p (c v) -> p c v", c=NCH)
                cur = None
                for g0 in range(0, NCH, 8):
                    grp = order[g0:g0 + 8]
                    slot = {}
                    used = [0, 0]
                    for c in grp:
                        half = c % 2
                        slot[c] = (4 + half, used[half])
                        used[half] += 1
                    for c in grp:
                        blk, half = c // 2, c % 2
                        pb, gi = slot[c]
                        S.op("pe", lambda e, gi=gi, blk=blk, half=half, pb=pb: e.matmul(
                            PS[pb][:, gi * 128:(gi + 1) * 128], kt3[half * 64:(half + 1) * 64, blk, :], v3[half * 64:(half + 1) * 64, blk, :],
                            start=True, stop=True),
                            reads=[b_ktok, bV], writes=[PSB[pb]])
                    for gj, c in enumerate(grp):
                        pos = g0 + gj
                        nxt = pos % 2
                        pb, gi = slot[c]
                        if pos == 0:
                            S.op("dve", lambda e, gi=gi, pb=pb, nxt=nxt: e.tensor_copy(st_s[nxt], PS[pb][:, gi * 128:(gi + 1) * 128]),
                                 reads=[PSB[pb]], writes=[b_st[nxt]])
                        else:
                            prv = 1 - nxt
                            S.op("act", lambda e, c=c, prv=prv: e.activation(out=sb3[:, c, :], in_=st_s[prv], func=AF.Copy),
                                 reads=[b_st[prv]], writes=[b_sbef])
                            if pos < NCH - 1:
                                S.op("dve", lambda e, gi=gi, pb=pb, c=c, nxt=nxt, prv=prv: e.scalar_tensor_tensor(
                                    st_s[nxt], st_s[prv], dec_s[:, c:c + 1], PS[pb][:, gi * 128:(gi + 1) * 128], ALU.mult, ALU.add),
                                    reads=[b_st[prv], PSB[pb], b_dec], writes=[b_st[nxt]])
                first_c = order[0]
                for q0 in range(0, NBL, 4):
                    q1 = min(NBL, q0 + 4)
                    nb = q1 - q0
                    pa = pctr[0] % 2
                    po = 2 + (pctr[0] % 2)
                    pctr[0] += 1
                    au = pa
                    for b in range(q0, q1):
                        S.op("pe", lambda e, b=b, q0=q0, pa=pa: e.matmul(
                            PS[pa][:, (b - q0) * 128:(b - q0 + 1) * 128], kh_s[:, b * 128:(b + 1) * 128], qh_s[:, b * 128:(b + 1) * 128],
                            start=True, stop=True),
                            reads=[b_kh, b_qh], writes=[PSB[pa]])
                    S.op("dve", lambda e, nb=nb, pa=pa: e.tensor_scalar(
                        attf_s[:, 0:nb * 128], PS[pa][:, 0:nb * 128], -3.0e38, 3.0e38, ALU.max, ALU.min),
                        reads=[PSB[pa]], writes=[b_attf])
                    S.op("dve", lambda e, nb=nb, au=au, d=d: e.tensor_tensor(
                        att_s[au][:, 0:nb * 128], attf_s[:, 0:nb * 128], msk_s[d][:, 0:nb * 128], ALU.mult),
                        reads=[b_attf, b_mask], writes=[b_att[au]])
                    for b in range(q0, q1):
                        o0 = (b - q0) * 128
                        halves = [hf for hf in range(2) if 2 * b + hf != first_c]
                        S.op("pe", lambda e, b=b, o0=o0, po=po, au=au, halves=halves: e.matmul(
                            PS[po][:, o0:o0 + 128], v3[:, b, :], att_s[au][:, o0:o0 + 128], start=True, stop=(len(halves) == 0)),
                            reads=[bV, b_att[au]], writes=[PSB[po]])
                        for half in halves:
                            c = 2 * b + half
                            S.op("pe", lambda e, c=c, o0=o0, half=half, po=po, lasth=(half == halves[-1]): e.matmul(
                                PS[po][:, o0 + half * 64:o0 + half * 64 + 64], sb3[:, c, :], qt_s[:, c * 64:(c + 1) * 64],
                                start=False, stop=lasth),
                                reads=[b_sbef, b_qt], writes=[PSB[po]])
                    tt0 = q0 * 128
                    if d == 0:
                        S.op("act", lambda e, tt0=tt0, nb=nb, po=po: e.activation(
                            out=oacc[:, tt0:tt0 + nb * 128], in_=PS[po][:, 0:nb * 128], func=AF.Copy),
                            reads=[PSB[po]], writes=[b_oacc])
                    else:
                        S.op("dve", lambda e, tt0=tt0, nb=nb, po=po: e.tensor_tensor(
                            oacc[:, tt0:tt0 + nb * 128], oacc[:, tt0:tt0 + nb * 128], PS[po][:, 0:nb * 128], ALU.add),
                            reads=[PSB[po]], writes=[b_oacc])

            def finish_head(hh):
                gnorm = hgn_s[:, 0:1]
                for (t0, W, isc) in TILES:
                    oc = octr[0] % 2
                    octr[0] += 1
                    S.op("act", lambda e, t0=t0, W=W: e.activation(out=osq_s[:, 0:W], in_=oacc[:, t0:t0 + W], func=AF.Square),
                         reads=[b_oacc], writes=[b_osq])
                    S.op("pe", lambda e, W=W: e.matmul(PS[6][:, 0:W], ones_bf, osq_s[:, 0:W], start=True, stop=True),
                         reads=[b_osq, b_c2], writes=[PSB[6]])
                    S.op("act", lambda e, W=W: e.activation(out=rs_s[:, 0:W], in_=PS[6][:, 0:W], func=AF.Sqrt, scale=1.0 / 128.0, bias=eps_t),
                         reads=[PSB[6], b_c2], writes=[b_rs])
                    S.op("dve", lambda e, W=W: e.reciprocal(rs_s[:, 0:W], rs_s[:, 0:W]), reads=[b_rs], writes=[b_rs])
                    S.op("dve", lambda e, t0=t0, W=W: e.scalar_tensor_tensor(
                        rs_s[:, 0:W], oacc[:, t0:t0 + W], gnorm, rs_s[:, 0:W], ALU.mult, ALU.mult),
                        reads=[b_oacc, b_hg], writes=[b_rs])
                    S.op("dve", lambda e, t0=t0, W=W, oc=oc: e.tensor_tensor(on_s[oc][:, 0:W], rs_s[:, 0:W], gs_s[:, t0:t0 + W], ALU.mult),
                         reads=[b_rs, bG], writes=[b_on[oc]])
                    S.dma("pool", sem_o[oc], aT[hh * 128:(hh + 1) * 128, t0:t0 + W], on_s[oc][:, 0:W], reads=[b_on[oc]])

            for hh in range(8):
                S.dma("sp", semQ, bufQ, hq_T[hh * 128:(hh + 1) * 128, :], writes=[bQ])
                vd = v_s.rearrange("p (c e) -> p c e", c=NBL)
                for c0 in range(0, NBL, 8):
                    c1 = min(NBL, c0 + 8)
                    S.dma("sp", semV, vd[:, c0:c1, :], v_v[:, c0:c1, hh * 128:(hh + 1) * 128], writes=[bV])
                S.dma("sp", semG, gs_s, qkT[hh * 128:(hh + 1) * 128, :], writes=[bG])
                direction(hh, 0)
                direction(hh, 1)
                finish_head(hh)
            S.barrier()

        b_qkT, b_vtok, b_aT = Buf("qkT"), Buf("vtok"), Buf("aT")

        def mixer(i):
            kind, j = i % 3, i // 3
            pump_until(*mixer_names(i))
            if kind == 0:
                attn_qkv_phase(i, 0, j)
                attn_core_phase(i, 0, j)
                mixer_out_phase(i, da_o_b[j], conv_bufs["dao%d" % j])
            elif kind == 2:
                attn_qkv_phase(i, 2, j)
                attn_core_phase(i, 2, j)
                mixer_out_phase(i, gqa_o_b[j], conv_bufs["gqao%d" % j])
            else:
                hgrn_proj_phase(i, j)
                if stop == "hp":
                    return
                hgrn_scan_phase(i, j)
                if stop == "hs":
                    return
                mixer_out_phase(i, hg_o_b[j], conv_bufs["hgo%d" % j])

        done = False
        start_layer_mixer = 0
        for i in range(start_layer, DEPTH):
            ffn_phase(i, 0)
            if stop == "f%d0" % i:
                dump_and_finish()
                done = True
                break
            if i >= start_layer_mixer:
                mixer(i)
            if stop == "m%d" % i or stop in ("hp", "hs"):
                dump_and_finish()
                done = True
                break
            ffn_phase(i, 1, last=(i == DEPTH - 1))
            if stop == "f%d1" % i and i < DEPTH - 1:
                dump_and_finish()
                done = True
                break
        S.barrier()
        S.emit()
    return nc


def _feat_major(v):
    v = np.asarray(v, dtype=np.float32)
    lead = v.shape[:-1]
    r = v.reshape(lead + (8, 128))
    r = np.moveaxis(r, -1, 0)
    return np.ascontiguousarray(r)


def _consts():
    ident = np.eye(128, dtype=np.float32)
    rot = np.zeros((128, 128), np.float32)
    for j in range(64):
        rot[2 * j + 1, 2 * j] = -1.0
        rot[2 * j, 2 * j + 1] = 1.0
    tril = np.zeros((128, 128), np.float32)
    return np.ascontiguousarray(np.concatenate([ident, rot, tril], axis=1))


def _rope_table(head_dim):
    pairs = head_dim // 4
    inv_freq = np.power(np.float32(10000.0), -np.arange(pairs, dtype=np.float32) / np.float32(pairs)).astype(np.float32)
    rows = TL // GRID_W
    r = np.repeat(np.arange(rows, dtype=np.float32), GRID_W)
    col = np.tile(np.arange(GRID_W, dtype=np.float32), rows)
    ang = np.concatenate([r[:, None] * inv_freq, col[:, None] * inv_freq], axis=-1).astype(np.float32)
    p = np.arange(128)
    a = (p % head_dim) // 2
    tab = ang[:, a].T
    return np.ascontiguousarray(np.stack([np.cos(tab), np.sin(tab)], axis=0).astype(np.float32))


def _hmask():
    m = np.zeros((2, 128, 128), np.float32)
    for blk in range(2):
        for s_ in range(64):
            for t_ in range(64):
                if s_ <= t_:
                    m[0, blk * 64 + s_, blk * 64 + t_] = 1.0
                if s_ >= t_:
                    m[1, blk * 64 + s_, blk * 64 + t_] = 1.0
    return np.ascontiguousarray(np.concatenate([m[0], m[1]], axis=1))


_PROG = {}


def prepare_inputs(inputs):
    x = np.asarray(inputs["x"], np.float32)
    ctx = np.asarray(inputs["ctx"], np.float32)
    c = np.asarray(inputs["c"], np.float32)
    c_ctx = np.asarray(inputs["c_ctx"], np.float32)
    B = x.shape[0]
    shared = {
        "w_mod": np.ascontiguousarray(inputs["w_mod"], dtype=np.float32),
        "bmodT": _feat_major(np.asarray(inputs["b_mod"]).reshape(DEPTH, 9, D)).reshape(128, DEPTH * 72),
        "normT": _feat_major(np.asarray(inputs["norm_g"])).reshape(128, DEPTH * 48),
        "ffn_w_in": np.ascontiguousarray(inputs["ffn_w_in"], dtype=np.float32),
        "ffn_w_out": np.ascontiguousarray(inputs["ffn_w_out"], dtype=np.float32),
        "consts": _consts(),
        "da_w_qkv": np.ascontiguousarray(inputs["da_w_qkv"], dtype=np.float32),
        "da_w_o": np.ascontiguousarray(inputs["da_w_o"], dtype=np.float32),
        "dal": np.ascontiguousarray(np.broadcast_to(np.asarray(inputs["da_lambda"], np.float32).reshape(1, 512), (128, 512))),
        "dasub": np.ascontiguousarray(np.asarray(inputs["da_subln"], np.float32).T),
        "ropeD": _rope_table(64),
        "gqa_w_qkv": np.ascontiguousarray(inputs["gqa_w_qkv"], dtype=np.float32),
        "gqa_w_o": np.ascontiguousarray(inputs["gqa_w_o"], dtype=np.float32),
        "gqn": np.ascontiguousarray(np.stack([np.asarray(inputs["gqa_q_norm"], np.float32)[0],
                                              np.asarray(inputs["gqa_k_norm"], np.float32)[0]], axis=1)),
        "ropeG": _rope_table(128),
        "hg_w_in": np.ascontiguousarray(inputs["hg_w_in"], dtype=np.float32),
        "hg_w_o": np.ascontiguousarray(inputs["hg_w_o"], dtype=np.float32),
        "hlbT": _feat_major(np.asarray(inputs["hg_lower_bound"], np.float32)).reshape(128, 64),
        "hgn": np.ascontiguousarray(np.asarray(inputs["hg_norm"], np.float32).reshape(128, 1)),
        "hmask": _hmask(),
    }
    in_maps = []
    for b in range(B):
        xT = np.ascontiguousarray(np.concatenate([x[b].T, ctx[b].T], axis=1))
        cc = np.stack([c[b], c_ctx], axis=0)
        ccT = _feat_major(cc)
        ccT = np.ascontiguousarray(np.transpose(ccT, (0, 2, 1))).reshape(128, 16)
        m = dict(shared)
        m["xT"] = xT
        m["ccT"] = ccT
        in_maps.append(m)
    return in_maps


def kernel(**inputs):
    stop = inputs.pop("_stop", None)
    cores = inputs.pop("_cores", None)
    start = inputs.pop("_start", 0)
    bsel = inputs.pop("_batch", None)
    in_maps = prepare_inputs(inputs)
    if bsel is not None:
        in_maps = [in_maps[bsel]]
    if cores is not None:
        in_maps = in_maps[:cores]
    key = (stop, start)
    if key not in _PROG:
        _PROG[key] = build_program(stop, start)
    nc = _PROG[key]
    res = run_bass_kernel_spmd(nc, in_maps, core_ids=list(range(len(in_maps))))
    out = np.stack([np.ascontiguousarray(np.asarray(r["yT"]).T) for r in res.results], axis=0)
    return out.astype(np.float32)
```

```python
import contextlib
import math
import numpy as np
import concourse.bass as bass
import concourse.mybir as mybir
from concourse.bass_utils import run_bass_kernel_spmd

F32 = mybir.dt.float32
BF16 = mybir.dt.bfloat16
AF = mybir.ActivationFunctionType
ALU = mybir.AluOpType

D = 1024
TL = 4096
TC = 256
T = TL + TC
DEPTH = 4
DFF = 2816
NJ = DFF // 128
EPS = 1e-6
GRID_W = 64
import os
DBG = int(os.environ.get('KDBG', '0'))


class Buf:
    __slots__ = ("name", "w", "r", "excl")

    def __init__(self, name="", excl=False):
        self.name = name
        self.w = None
        self.r = []
        self.excl = excl


class Sched:
    ENGS = ("pe", "act", "dve", "pool", "sp")

    def __init__(self, nc):
        self.nc = nc
        self.prog = {e: [] for e in self.ENGS}
        self.cnt = {}
        self.known = {e: {} for e in self.ENGS}
        for e in self.ENGS:
            self.cnt["E" + e] = 0
        self.all_dma = [[], []]
        self.free_dma = [[], []]
        self.persist = set()

    def _need(self, eng, deps):
        kn = self.known[eng]
        best = {}
        for d in deps:
            if d is None:
                continue
            k, v = d
            if kn.get(k, 0) >= v:
                continue
            if best.get(k, 0) < v:
                best[k] = v
        out = []
        for k, v in best.items():
            kn[k] = v
            out.append((k, v))
        return out

    def _deps(self, eng, reads, writes):
        deps = []
        me = "E" + eng
        pe = eng == "pe"
        for b in reads:
            if b.w is not None and not (pe and b.w[0] == me):
                deps.append(b.w)
        for b in writes:
            if b.w is not None and not (pe and b.w[0] == me):
                deps.append(b.w)
            for r in b.r:
                if r[0] != me:
                    deps.append(r)
        return deps

    def _commit(self, tok, reads, writes):
        for b in reads:
            b.r.append(tok)
            if len(b.r) > 64:
                best = {}
                for k, v in b.r:
                    if best.get(k, 0) < v:
                        best[k] = v
                b.r = list(best.items())
        for b in writes:
            b.w = tok
            b.r = []

    def op(self, eng, fn, reads=(), writes=()):
        if any(b.excl for b in reads):
            writes = list(writes) + [b for b in reads if b.excl]
            reads = [b for b in reads if not b.excl]
        waits = self._need(eng, self._deps(eng, reads, writes))
        key = "E" + eng
        self.cnt[key] += 1
        tok = (key, self.cnt[key])
        self.prog[eng].append((waits, fn, key, 1))
        self._commit(tok, reads, writes)
        return tok

    def new_dma_sem(self, name):
        return [None, name.startswith("cv")]

    def _bind(self, sem, eng):
        if sem[0] is None:
            q = 1 if eng == "pool" else 0
            if self.free_dma[q]:
                sem[0] = self.free_dma[q].pop()
            else:
                key = "%s%d" % ("PQ"[q], len(self.all_dma[q]))
                self.all_dma[q].append(key)
                self.cnt[key] = 0
                sem[0] = key
            if sem[1]:
                self.persist.add(sem[0])
        return sem[0]

    def dma(self, eng, sem, out, in_, reads=(), writes=()):
        sem = self._bind(sem, eng)
        waits = self._need(eng, [d for d in self._deps(eng, reads, writes) if d[0] != sem])
        self.cnt[sem] += 16
        tok = (sem, self.cnt[sem])
        self.prog[eng].append((waits, lambda e: e.dma_start(out=out, in_=in_), sem, 16))
        self._commit(tok, reads, writes)
        return tok

    def wait_all(self, eng, toks):
        waits = self._need(eng, toks)
        if waits:
            self.prog[eng].append((waits, None, None, 0))

    def barrier(self):
        toks = [(k, v) for k, v in self.cnt.items() if v > 0]
        for e in self.ENGS:
            self.wait_all(e, [t for t in toks if t[0] != "E" + e])
        self.free_dma = [[k for k in reversed(self.all_dma[q]) if k not in self.persist] for q in range(2)]

    def emit(self):
        nc = self.nc
        with contextlib.ExitStack() as st:
            sems = {k: st.enter_context(nc.semaphore(k)) for k in self.cnt}
            block = st.enter_context(nc.Block())
            hmap = {"pe": block.tensor, "act": block.scalar, "dve": block.vector,
                    "pool": block.gpsimd, "sp": block.sync}
            for e in self.ENGS:
                prog = self.prog[e]
                if not prog:
                    continue

                def body(h, prog=prog):
                    for waits, fn, key, inc in prog:
                        for (k, v) in waits:
                            h.wait_ge(sems[k], v)
                        if fn is not None:
                            fn(h).then_inc(sems[key], inc)
                hmap[e](body)


class Arena:
    def __init__(self, ap):
        self.ap = ap
        self.n = ap.shape[1]
        self.base = 0
        self.off = 0

    def freeze(self):
        self.base = self.off

    def reset(self):
        self.off = self.base

    def f32(self, n):
        o = self.off
        self.off += n
        assert self.off <= self.n, "arena overflow %d > %d" % (self.off, self.n)
        return self.ap[:, o:o + n]

    def bf16(self, n):
        w = (n + 1) // 2
        return self.f32(w).bitcast(BF16)[:, 0:n]


TILES = [(i * 512, 512, False) for i in range(8)] + [(TL, TC, True)]


def build_program(stop=None, start_layer=0):
    nc = bass.Bass("TRN2", target_bir_lowering=False)

    def din(name, shape, dt=F32):
        return nc.dram_tensor(name, list(shape), dt, kind="ExternalInput").ap()

    def dscr(name, shape, dt):
        return nc.dram_tensor(name, list(shape), dt).ap()

    xT_in = din("xT", [D, T])
    ccT = din("ccT", [128, 16])
    w_mod = din("w_mod", [DEPTH, D, 9 * D])
    bmodT = din("bmodT", [128, DEPTH * 72])
    normT = din("normT", [128, DEPTH * 48])
    ffn_w_in = din("ffn_w_in", [DEPTH, 2, D, 2 * DFF])
    ffn_w_out = din("ffn_w_out", [DEPTH, 2, DFF, D])
    consts = din("consts", [128, 3 * 128])
    da_w_qkv = din("da_w_qkv", [2, D, 3 * D])
    da_w_o = din("da_w_o", [2, D, D])
    dal = din("dal", [128, 2 * 256])
    dasub = din("dasub", [128, 2])
    ropeD = din("ropeD", [2, 128, TL])
    gqa_w_qkv = din("gqa_w_qkv", [1, D, 1536])
    gqa_w_o = din("gqa_w_o", [1, D, D])
    gqn = din("gqn", [128, 2])
    ropeG = din("ropeG", [2, 128, TL])
    hg_w_in = din("hg_w_in", [1, D, 5 * D])
    hg_w_o = din("hg_w_o", [1, D, D])
    hlbT = din("hlbT", [128, 64])
    hgn = din("hgn", [128, 1])
    hmask = din("hmask", [128, 256])
    yT = nc.dram_tensor("yT", [D, TL], F32, kind="ExternalOutput").ap()

    xs = dscr("xs", [D, T], F32)
    wi_b = dscr("wi_b", [DEPTH, 2, D, 2 * DFF], BF16)
    wo_b = dscr("wo_b", [DEPTH, 2, DFF, D], BF16)
    da_qkv_b = dscr("da_qkv_b", [2, D, 3 * D], BF16)
    da_o_b = dscr("da_o_b", [2, D, D], BF16)
    gqa_qkv_b = dscr("gqa_qkv_b", [1, D, 1536], BF16)
    gqa_o_b = dscr("gqa_o_b", [1, D, D], BF16)
    hg_in_b = dscr("hg_in_b", [1, D, 5 * D], BF16)
    hg_o_b = dscr("hg_o_b", [1, D, D], BF16)
    hq_T = dscr("hq_T", [D, T], F32)
    hlf_T = [dscr("hlf_T%d" % d_, [D, T], F32) for d_ in range(2)]
    hkk_T = [dscr("hkk_T%d" % d_, [D, T], F32) for d_ in range(2)]
    qkT = dscr("qkT", [16 * 128, T], BF16)
    vtok = dscr("vtok", [T, D], BF16)
    aT = dscr("aT", [D, T], BF16)

    with contextlib.ExitStack() as st:
        arena_t = st.enter_context(nc.sbuf_tensor("arena", [128, 50 * 1024], F32))
        PS = [st.enter_context(nc.psum_tensor("ps%d" % i, [128, 512], F32)) for i in range(8)]
        AR = Arena(arena_t[:, :])
        S = Sched(nc)
        PSB = [Buf("ps%d" % i, excl=True) for i in range(8)]

        ones_bf = AR.bf16(128)
        ones_f = AR.f32(128)
        ident_bf = AR.bf16(128)
        rotm_bf = AR.bf16(128)
        cst_f = AR.f32(384)
        eps_t = AR.f32(1)
        ccs = AR.f32(16)
        modL = AR.f32(DEPTH * 72)
        modC = AR.f32(DEPTH * 72)
        bmod_s = AR.f32(DEPTH * 72)
        norm_s = AR.f32(DEPTH * 48)
        SA = [AR.f32(DEPTH * 24), AR.f32(DEPTH * 24)]
        SG = [AR.f32(DEPTH * 24), AR.f32(DEPTH * 24)]
        MOD = [modL, modC]
        dal_s = AR.f32(512)
        dasub_s = AR.f32(2)
        gqn_s = AR.f32(2)
        da_neglam = AR.f32(2)
        da_subs = AR.f32(2)
        da_tmp = AR.f32(64)
        da_e = AR.f32(4)
        hlb_s = AR.f32(64)
        hgn_s = AR.f32(1)
        hg_lb = AR.f32(16)
        hg_oml = AR.f32(16)
        hg_sum = AR.f32(16)
        AR.freeze()
        b_const = Buf("const")
        b_mod = Buf("mod")

        conv_bufs = {}

        cv_inflight = [None, None]

        conv_jobs = []
        conv_pending = {}

        def convert(name, src, dst, rows):
            sems = [S.new_dma_sem("cv%d_%s" % (q, name)) for q in range(2)]
            bs = [Buf("cv%d_%s" % (q, name)) for q in range(2)]
            r0 = 0
            n = 0
            while r0 < rows:
                r1 = min(rows, r0 + 128)
                conv_jobs.append((name, n % 2, sems[n % 2], bs[n % 2], dst[r0:r1, :], src[r0:r1, :]))
                r0 = r1
                n += 1
            conv_pending[name] = n
            conv_bufs[name] = bs

        def pump(n):
            while n > 0 and conv_jobs:
                name, q, sem, b, dst, src = conv_jobs.pop(0)
                if cv_inflight[q] is not None:
                    S.wait_all("pool", [cv_inflight[q]])
                cv_inflight[q] = S.dma("pool", sem, dst, src, writes=[b])
                conv_pending[name] -= 1
                n -= 1

        def pump_until(*names):
            while any(conv_pending.get(nm, 0) > 0 for nm in names):
                pump(1)

        def convert_ffn(i, f):
            convert("wi%d%d" % (i, f), ffn_w_in[i, f], wi_b[i, f], D)
            convert("wo%d%d" % (i, f), ffn_w_out[i, f], wo_b[i, f], DFF)

        def convert_mixer(i):
            kind, j = i % 3, i // 3
            if kind == 0:
                convert("daqkv%d" % j, da_w_qkv[j], da_qkv_b[j], D)
                convert("dao%d" % j, da_w_o[j], da_o_b[j], D)
            elif kind == 2:
                convert("gqaqkv%d" % j, gqa_w_qkv[j], gqa_qkv_b[j], D)
                convert("gqao%d" % j, gqa_w_o[j], gqa_o_b[j], D)
            else:
                convert("hgin%d" % j, hg_w_in[j], hg_in_b[j], D)
                convert("hgo%d" % j, hg_w_o[j], hg_o_b[j], D)

        def mixer_names(i):
            kind, j = i % 3, i // 3
            return {0: ("daqkv%d" % j, "dao%d" % j), 1: ("hgin%d" % j, "hgo%d" % j), 2: ("gqaqkv%d" % j, "gqao%d" % j)}[kind]

        for li in range(start_layer, DEPTH):
            convert_ffn(li, 0)
            convert_mixer(li)
            convert_ffn(li, 1)
        pump_until("wi%d0" % start_layer, "wo%d0" % start_layer)
        pump(16)

        s_c0 = S.new_dma_sem("c0")
        S.dma("sp", s_c0, cst_f, consts[:, :], writes=[b_const])
        S.dma("sp", s_c0, ccs, ccT[:, :], writes=[b_const])
        S.dma("sp", s_c0, bmod_s, bmodT[:, :], writes=[b_const])
        S.dma("sp", s_c0, norm_s, normT[:, :], writes=[b_const])
        b_c2 = Buf("const2")
        S.op("dve", lambda e: e.memset(ones_bf, 1.0), writes=[b_c2])
        S.op("dve", lambda e: e.memset(ones_f, 1.0), writes=[b_c2])
        S.op("dve", lambda e: e.memset(eps_t, EPS), writes=[b_c2])
        S.op("dve", lambda e: e.tensor_copy(ident_bf, cst_f[:, 0:128]), reads=[b_const], writes=[b_c2])
        S.op("dve", lambda e: e.tensor_copy(rotm_bf, cst_f[:, 128:256]), reads=[b_const], writes=[b_c2])
        S.op("act", lambda e: e.activation(out=ccs, in_=ccs, func=AF.Silu), reads=[b_const], writes=[b_const])

        S.dma("sp", s_c0, dal_s, dal[:, :], writes=[b_const])
        S.dma("sp", s_c0, dasub_s, dasub[:, :], writes=[b_const])
        S.dma("sp", s_c0, gqn_s, gqn[:, :], writes=[b_const])
        b_da = Buf("da")
        for jj in range(2):
            li = 3 * jj
            lam_init = 0.8 - 0.6 * math.exp(-0.3 * li)
            for pair in range(2):
                o0 = jj * 256 + pair * 128
                S.op("dve", lambda e, o0=o0: e.tensor_tensor(da_tmp, dal_s[:, o0:o0 + 64], dal_s[:, o0 + 64:o0 + 128], ALU.mult),
                     reads=[b_const], writes=[b_da])
                S.op("dve", lambda e, jj=jj, pair=pair: e.tensor_reduce(
                    da_e[:, jj * 2 + pair:jj * 2 + pair + 1], da_tmp, mybir.AxisListType.X, ALU.add),
                    reads=[b_da], writes=[b_da])
            S.op("act", lambda e, jj=jj: e.activation(out=da_e[:, jj * 2:jj * 2 + 2], in_=da_e[:, jj * 2:jj * 2 + 2], func=AF.Exp),
                 reads=[b_da], writes=[b_da])
            S.op("dve", lambda e, jj=jj, lam_init=lam_init: e.scalar_tensor_tensor(
                da_neglam[:, jj:jj + 1], da_e[:, jj * 2 + 1:jj * 2 + 2], -lam_init, da_e[:, jj * 2:jj * 2 + 1], ALU.add, ALU.subtract),
                reads=[b_da], writes=[b_da])
            S.op("dve", lambda e, jj=jj, lam_init=lam_init: e.tensor_scalar(
                da_subs[:, jj:jj + 1], dasub_s[:, jj:jj + 1], 1.0 - lam_init, None, ALU.mult),
                reads=[b_const, b_da], writes=[b_da])

        S.dma("sp", s_c0, hlb_s, hlbT[:, :], writes=[b_const])
        S.dma("sp", s_c0, hgn_s, hgn[:, :], writes=[b_const])
        b_hg = Buf("hg")
        HL = 1
        S.op("act", lambda e: e.activation(out=hlb_s, in_=hlb_s, func=AF.Exp), reads=[b_const], writes=[b_const])
        hl4 = hlb_s.rearrange("p (d l k) -> p d l k", d=2, l=4)
        for dd in range(2):
            S.op("dve", lambda e, dd=dd: e.tensor_tensor(hg_sum[:, dd * 8:dd * 8 + 8], hl4[:, dd, 0, :], hl4[:, dd, 1, :], ALU.add),
                 reads=[b_const], writes=[b_hg])
            for l in (2, 3):
                S.op("dve", lambda e, dd=dd, l=l: e.tensor_tensor(hg_sum[:, dd * 8:dd * 8 + 8], hg_sum[:, dd * 8:dd * 8 + 8], hl4[:, dd, l, :], ALU.add),
                     reads=[b_const, b_hg], writes=[b_hg])
            S.op("dve", lambda e, dd=dd: e.tensor_copy(hg_lb[:, dd * 8:dd * 8 + 8], hl4[:, dd, 1, :]), reads=[b_const], writes=[b_hg])
            for l in range(2, HL + 1):
                S.op("dve", lambda e, dd=dd, l=l: e.tensor_tensor(hg_lb[:, dd * 8:dd * 8 + 8], hg_lb[:, dd * 8:dd * 8 + 8], hl4[:, dd, l, :], ALU.add),
                     reads=[b_const, b_hg], writes=[b_hg])
        S.op("dve", lambda e: e.reciprocal(hg_sum, hg_sum), reads=[b_hg], writes=[b_hg])
        S.op("dve", lambda e: e.tensor_tensor(hg_lb, hg_lb, hg_sum, ALU.mult), reads=[b_hg], writes=[b_hg])
        S.op("dve", lambda e: e.tensor_scalar(hg_oml, hg_lb, -1.0, 1.0, ALU.mult, ALU.add), reads=[b_hg], writes=[b_hg])

        wm_slots = [AR.f32(8 * 1024), AR.f32(8 * 1024)]
        wm_bufs = [Buf("wm0"), Buf("wm1")]
        wm_sems = [S.new_dma_sem("wm0"), S.new_dma_sem("wm1")]
        blk = 0
        for i in range(DEPTH):
            wv = w_mod[i].rearrange("(k p) n -> p k n", p=128)
            for m in range(9):
                sl = blk % 2
                wt = wm_slots[sl].rearrange("p (k n) -> p k n", k=8)
                S.dma("sp", wm_sems[sl], wt, wv[:, :, m * 1024:(m + 1) * 1024], writes=[wm_bufs[sl]])
                pb = blk % 2
                ps = PS[pb]
                for n in range(8):
                    for k in range(8):
                        S.op("pe", lambda e, ps=ps, wt=wt, n=n, k=k: e.matmul(
                            ps[:, n * 2:n * 2 + 2], wt[:, k, n * 128:(n + 1) * 128], ccs[:, 2 * k:2 * k + 2],
                            start=(k == 0), stop=(k == 7)),
                            reads=[wm_bufs[sl], b_const], writes=[PSB[pb]])
                ps3 = ps[:, 0:16].rearrange("p (n j) -> p n j", j=2)
                c0 = i * 72 + m * 8
                for j in range(2):
                    S.op("dve", lambda e, ps3=ps3, j=j, c0=c0: e.tensor_tensor(
                        MOD[j][:, c0:c0 + 8], ps3[:, :, j], bmod_s[:, c0:c0 + 8], ALU.add),
                        reads=[PSB[pb], b_const], writes=[b_mod])
                blk += 1
        for i in range(DEPTH):
            for s in range(3):
                wgt = 1.0 if s == 1 else 0.5
                c = (i * 3 + s) * 8
                gpre = norm_s[:, i * 48 + (2 * s) * 8: i * 48 + (2 * s) * 8 + 8]
                gpost = norm_s[:, i * 48 + (2 * s + 1) * 8: i * 48 + (2 * s + 1) * 8 + 8]
                for j in range(2):
                    scale = MOD[j][:, i * 72 + (3 * s + 1) * 8: i * 72 + (3 * s + 1) * 8 + 8]
                    gate = MOD[j][:, i * 72 + (3 * s + 2) * 8: i * 72 + (3 * s + 2) * 8 + 8]
                    S.op("dve", lambda e, j=j, c=c, scale=scale, gpre=gpre: e.scalar_tensor_tensor(
                        SA[j][:, c:c + 8], scale, 1.0, gpre, ALU.add, ALU.mult),
                        reads=[b_mod, b_const], writes=[b_mod])
                    S.op("dve", lambda e, j=j, c=c, gate=gate, gpost=gpost, wgt=wgt: e.scalar_tensor_tensor(
                        SG[j][:, c:c + 8], gate, wgt, gpost, ALU.mult, ALU.mult),
                        reads=[b_mod, b_const], writes=[b_mod])

        def scal(i, s, j):
            c = (i * 3 + s) * 8
            shift0 = i * 72 + (3 * s) * 8
            return (lambda k: SA[j][:, c + k:c + k + 1],
                    lambda k: MOD[j][:, shift0 + k:shift0 + k + 1],
                    lambda k: SG[j][:, c + k:c + k + 1])

        S.barrier()
        AR.reset()
        if stop in ("setup", "setup0"):
            S.emit()
            return nc

        xs_v = xs.rearrange("(k p) t -> p k t", p=128)
        xin_v = xT_in.rearrange("(k p) t -> p k t", p=128)
        y_v = yT.rearrange("(k p) t -> p k t", p=128)
        XB = [Buf("xs%d" % t) for t in range(len(TILES))]
        state = {"src_is_input": True}

        def rstd_from_sq(sq3, W, nk, ps, psb, rstd, b_rstd, b_sq, denom):
            for k in range(nk):
                S.op("pe", lambda e, k=k: e.matmul(ps[:, 0:W], ones_bf, sq3[:, k, :], start=(k == 0), stop=(k == nk - 1)),
                     reads=[b_sq, b_c2], writes=[psb])
            S.op("act", lambda e: e.activation(out=rstd[:, 0:W], in_=ps[:, 0:W], func=AF.Sqrt, scale=1.0 / denom, bias=eps_t),
                 reads=[psb, b_c2], writes=[b_rstd])
            S.op("dve", lambda e: e.reciprocal(rstd[:, 0:W], rstd[:, 0:W]), reads=[b_rstd], writes=[b_rstd])

        def ffn_phase(i, f, last=False):
            s = 0 if f == 0 else 2
            need_ctx = not (i == DEPTH - 1 and f == 1)
            tiles = [t for t in range(len(TILES)) if need_ctx or not TILES[t][2]]
            AR.reset()
            xt_s = [AR.f32(8 * 512) for _ in range(2)]
            ysb = AR.f32(8 * 512)
            sq_s = AR.bf16(8 * 512)
            rstd_s = [AR.f32(512) for _ in range(2)]
            t1_s = [AR.f32(512) for _ in range(2)]
            h_s = [AR.bf16(8 * 512) for _ in range(2)]
            act_s = AR.bf16(NJ * 512)
            wo_s = AR.bf16(NJ * 1024)
            wi_s = [AR.bf16(8 * 1024) for _ in range(2)]
            sg_s = [AR.f32(512) for _ in range(2)]
            b_xt = [Buf(), Buf()]
            b_ysb, b_sq, b_act, b_wo = Buf(), Buf(), Buf(), Buf()
            b_rstd = [Buf(), Buf()]
            b_t1 = [Buf(), Buf()]
            b_h = [Buf(), Buf()]
            b_wi = [Buf(), Buf()]
            b_sg = [Buf(), Buf()]
            tag = "%d%d" % (i, f)
            sem_x = [S.new_dma_sem("fx0_" + tag), S.new_dma_sem("fx1_" + tag)]
            sem_wi = [S.new_dma_sem("fwi0_" + tag), S.new_dma_sem("fwi1_" + tag)]
            sem_wo = S.new_dma_sem("fwo_" + tag)
            sem_st = S.new_dma_sem("fst_" + tag)
            pump_until("wi" + tag, "wo" + tag)
            cv_wi = conv_bufs["wi" + tag]
            cv_wo = conv_bufs["wo" + tag]
            wi_v = wi_b[i, f].rearrange("(k p) n -> p k n", p=128)
            wo_v = wo_b[i, f].rearrange("(j p) n -> p j n", p=128)
            src_v = xin_v if state["src_is_input"] else xs_v

            wo3 = wo_s.rearrange("p (j n) -> p j n", j=NJ)
            for a in range(0, NJ, 6):
                b = min(NJ, a + 6)
                S.dma("sp", sem_wo, wo3[:, a:b, :], wo_v[:, a:b, :], reads=cv_wo, writes=[b_wo])

            groups = [(a, min(NJ, a + 4)) for a in range(0, NJ, 4)]
            wi_ctr = [0]

            def load_x(ti):
                t = tiles[ti]
                t0, W, isc = TILES[t]
                sl = ti % 2
                x3 = xt_s[sl].rearrange("p (k t) -> p k t", k=8)
                S.dma("sp", sem_x[sl], x3[:, :, 0:W], src_v[:, :, t0:t0 + W], reads=[XB[t]], writes=[b_xt[sl]])

            def prenorm(ti):
                t = tiles[ti]
                t0, W, isc = TILES[t]
                sl = ti % 2
                A, Bs, G = scal(i, s, 1 if isc else 0)
                x3 = xt_s[sl].rearrange("p (k t) -> p k t", k=8)
                sq3 = sq_s.rearrange("p (k t) -> p k t", k=8)
                h3 = h_s[sl].rearrange("p (k t) -> p k t", k=8)
                S.op("act", lambda e: e.activation(out=sq3[:, :, 0:W], in_=x3[:, :, 0:W], func=AF.Square),
                     reads=[b_xt[sl]], writes=[b_sq])
                rstd_from_sq(sq3[:, :, 0:W], W, 8, PS[4], PSB[4], rstd_s[sl], b_rstd[sl], b_sq, float(D))
                for k in range(8):
                    q = k % 2
                    S.op("dve", lambda e, k=k, q=q: e.scalar_tensor_tensor(
                        t1_s[q][:, 0:W], x3[:, k, 0:W], A(k), rstd_s[sl][:, 0:W], ALU.mult, ALU.mult),
                        reads=[b_xt[sl], b_rstd[sl], b_mod], writes=[b_t1[q]])
                    S.op("act", lambda e, k=k, q=q: e.activation(
                        out=h3[:, k, 0:W], in_=t1_s[q][:, 0:W], func=AF.Identity, bias=Bs(k), scale=1.0),
                        reads=[b_t1[q], b_mod], writes=[b_h[sl]])

            def gateup(ti, gi):
                t = tiles[ti]
                t0, W, isc = TILES[t]
                sl = ti % 2
                h3 = h_s[sl].rearrange("p (k t) -> p k t", k=8)
                a, b = groups[gi]
                ng = b - a
                ws = wi_ctr[0] % 2
                wi_ctr[0] += 1
                w3 = wi_s[ws].rearrange("p (k n) -> p k n", k=8)
                S.dma("sp", sem_wi[ws], w3[:, :, 0:ng * 128], wi_v[:, :, a * 128:b * 128], reads=cv_wi, writes=[b_wi[ws]])
                S.dma("sp", sem_wi[ws], w3[:, :, 512:512 + ng * 128], wi_v[:, :, DFF + a * 128:DFF + b * 128],
                      reads=cv_wi, writes=[b_wi[ws]])
                act3 = act_s.rearrange("p (j t) -> p j t", j=NJ)
                for jj in range(ng):
                    j = a + jj
                    pp = j % 2
                    pg, pu = PS[2 * pp], PS[2 * pp + 1]
                    for k in range(8):
                        S.op("pe", lambda e, k=k, jj=jj, pg=pg: e.matmul(
                            pg[:, 0:W], w3[:, k, jj * 128:(jj + 1) * 128], h3[:, k, 0:W], start=(k == 0), stop=(k == 7)),
                            reads=[b_wi[ws], b_h[sl]], writes=[PSB[2 * pp]])
                    for k in range(8):
                        S.op("pe", lambda e, k=k, jj=jj, pu=pu: e.matmul(
                            pu[:, 0:W], w3[:, k, 512 + jj * 128:512 + (jj + 1) * 128], h3[:, k, 0:W], start=(k == 0), stop=(k == 7)),
                            reads=[b_wi[ws], b_h[sl]], writes=[PSB[2 * pp + 1]])
                    S.op("act", lambda e, pp=pp, pg=pg: e.activation(out=sg_s[pp][:, 0:W], in_=pg[:, 0:W], func=AF.Silu),
                         reads=[PSB[2 * pp]], writes=[b_sg[pp]])
                    S.op("dve", lambda e, pp=pp, pu=pu, j=j: e.tensor_tensor(
                        act3[:, j, 0:W], sg_s[pp][:, 0:W], pu[:, 0:W], ALU.mult),
                        reads=[b_sg[pp], PSB[2 * pp + 1]], writes=[b_act])

            def wout(ti):
                t = tiles[ti]
                t0, W, isc = TILES[t]
                act3 = act_s.rearrange("p (j t) -> p j t", j=NJ)
                y3 = ysb.rearrange("p (k t) -> p k t", k=8)
                sq3 = sq_s.rearrange("p (k t) -> p k t", k=8)
                for dk in range(8):
                    pb = 5 + dk % 2
                    py = PS[pb]
                    for j in range(NJ):
                        S.op("pe", lambda e, j=j, dk=dk, py=py: e.matmul(
                            py[:, 0:W], wo3[:, j, dk * 128:(dk + 1) * 128], act3[:, j, 0:W], start=(j == 0), stop=(j == NJ - 1)),
                            reads=[b_wo, b_act], writes=[PSB[pb]])
                    S.op("act", lambda e, dk=dk, py=py: e.activation(out=sq3[:, dk, 0:W], in_=py[:, 0:W], func=AF.Square),
                         reads=[PSB[pb]], writes=[b_sq])
                    S.op("dve", lambda e, dk=dk, py=py: e.tensor_copy(y3[:, dk, 0:W], py[:, 0:W]),
                         reads=[PSB[pb]], writes=[b_ysb])

            def postnorm(ti):
                t = tiles[ti]
                t0, W, isc = TILES[t]
                sl = ti % 2
                A, Bs, G = scal(i, s, 1 if isc else 0)
                x3 = xt_s[sl].rearrange("p (k t) -> p k t", k=8)
                y3 = ysb.rearrange("p (k t) -> p k t", k=8)
                sq3 = sq_s.rearrange("p (k t) -> p k t", k=8)
                rs = rstd_s[sl]
                rstd_from_sq(sq3[:, :, 0:W], W, 8, PS[4], PSB[4], rs, b_rstd[sl], b_sq, float(D))
                for k in range(8):
                    S.op("dve", lambda e, k=k: e.scalar_tensor_tensor(
                        y3[:, k, 0:W], y3[:, k, 0:W], G(k), rs[:, 0:W], ALU.mult, ALU.mult),
                        reads=[b_rstd[sl], b_mod], writes=[b_ysb])
                S.op("pool", lambda e: e.tensor_tensor(y3[:, :, 0:W], y3[:, :, 0:W], x3[:, :, 0:W], ALU.add),
                     reads=[b_xt[sl], b_ysb], writes=[b_ysb])
                if last and not isc:
                    S.dma("pool", sem_st, y_v[:, :, t0:t0 + W], y3[:, :, 0:W], reads=[b_ysb], writes=[XB[t]])
                else:
                    S.dma("pool", sem_st, xs_v[:, :, t0:t0 + W], y3[:, :, 0:W], reads=[b_ysb], writes=[XB[t]])

            n = len(tiles)
            if DBG:
                load_x(0)
                prenorm(0)
                if DBG >= 2:
                    for gi in range(len(groups)):
                        gateup(0, gi)
                if DBG >= 3:
                    wout(0)
                if DBG >= 4:
                    postnorm(0)
                S.barrier()
                return
            load_x(0)
            prenorm(0)
            for ti in range(n):
                gateup(ti, 0)
                gateup(ti, 1)
                if ti + 1 < n:
                    load_x(ti + 1)
                for gi in range(2, len(groups)):
                    gateup(ti, gi)
                if ti + 1 < n:
                    prenorm(ti + 1)
                wout(ti)
                postnorm(ti)
                pump(5)
            state["src_is_input"] = False
            S.barrier()

        def dump_and_finish():
            AR.reset()
            tb = AR.f32(8 * 512)
            t3 = tb.rearrange("p (k t) -> p k t", k=8)
            bb = Buf()
            s1 = S.new_dma_sem("dbg_l")
            s2 = S.new_dma_sem("dbg_s")
            for t in range(8):
                t0, W, _ = TILES[t]
                S.dma("sp", s1, t3, xs_v[:, :, t0:t0 + W], reads=[XB[t]], writes=[bb])
                S.dma("sp", s2, y_v[:, :, t0:t0 + W], t3, reads=[bb])
            S.barrier()

        def attn_qkv_phase(i, kind, j):
            is_da = kind == 0
            NQ = 8
            NKC = 8 if is_da else 2
            NV = NKC
            ncols = (NQ + 2 * NKC) * 128
            wsrc = (da_qkv_b if is_da else gqa_qkv_b)[j]
            rope = ropeD if is_da else ropeG
            cvb = conv_bufs[("daqkv%d" if is_da else "gqaqkv%d") % j]
            AR.reset()
            xt_s = [AR.f32(8 * 512) for _ in range(2)]
            sq_s = AR.bf16(8 * 512)
            rstd_s = [AR.f32(512) for _ in range(2)]
            t1_s = [AR.f32(512) for _ in range(2)]
            h_s = [AR.bf16(8 * 512) for _ in range(2)]
            w_s = AR.bf16(8 * ncols)
            cs_s = [AR.f32(2 * 512) for _ in range(2)]
            qk_s = [AR.bf16((NQ + NKC) * 512) for _ in range(2)]
            v_s = [AR.bf16(4 * NV * 128) for _ in range(2)]
            qb_s = [AR.bf16(512) for _ in range(2)]
            ta_s = [AR.f32(512) for _ in range(2)]
            tb_s = [AR.f32(512) for _ in range(2)]
            qn_s = [AR.f32(512) for _ in range(2)]
            rs_s = [AR.f32(512) for _ in range(2)]
            b_xt = [Buf(), Buf()]
            b_sq = Buf()
            b_rstd = [Buf(), Buf()]
            b_t1 = [Buf(), Buf()]
            b_h = [Buf(), Buf()]
            b_w = Buf()
            b_cs = [Buf(), Buf()]
            b_qk = [Buf(), Buf()]
            b_v = [Buf(), Buf()]
            b_qb = [Buf(), Buf()]
            b_ta = [Buf(), Buf()]
            b_tb = [Buf(), Buf()]
            b_qn = [Buf(), Buf()]
            b_rs = [Buf(), Buf()]
            tag = "q%d" % i
            sem_x = [S.new_dma_sem("ax0_" + tag), S.new_dma_sem("ax1_" + tag)]
            sem_cs = [S.new_dma_sem("acs0_" + tag), S.new_dma_sem("acs1_" + tag)]
            sem_w = S.new_dma_sem("aw_" + tag)
            sem_st = [S.new_dma_sem("ast0_" + tag), S.new_dma_sem("ast1_" + tag)]
            sem_sv = [S.new_dma_sem("asv0_" + tag), S.new_dma_sem("asv1_" + tag)]
            w3 = w_s.rearrange("p (k n) -> p k n", k=8)
            wv = wsrc.rearrange("(k p) n -> p k n", p=128)
            for a in range(0, ncols, 768):
                S.dma("sp", sem_w, w3[:, :, a:a + 768], wv[:, :, a:a + 768], reads=cvb, writes=[b_w])
            qk_v = qkT.rearrange("(c p) t -> p c t", p=128)
            v_v = vtok.rearrange("(tb p) e -> p tb e", p=128)
            gq = gqn_s[:, 0:1]
            gk = gqn_s[:, 1:2]
            n = len(TILES)
            cnt = [0]

            def load_x(ti):
                t0, W, isc = TILES[ti]
                sl = ti % 2
                x3 = xt_s[sl].rearrange("p (k t) -> p k t", k=8)
                S.dma("sp", sem_x[sl], x3[:, :, 0:W], xs_v[:, :, t0:t0 + W], reads=[XB[ti]], writes=[b_xt[sl]])
                if not isc:
                    c3 = cs_s[sl].rearrange("p (a t) -> p a t", a=2)
                    S.dma("sp", sem_cs[sl], c3[:, :, 0:W], rope.rearrange("a p t -> p a t")[:, :, t0:t0 + W], writes=[b_cs[sl]])

            def prenorm(ti):
                t0, W, isc = TILES[ti]
                sl = ti % 2
                A, Bs, G = scal(i, 1, 1 if isc else 0)
                x3 = xt_s[sl].rearrange("p (k t) -> p k t", k=8)
                sq3 = sq_s.rearrange("p (k t) -> p k t", k=8)
                h3 = h_s[sl].rearrange("p (k t) -> p k t", k=8)
                S.op("act", lambda e: e.activation(out=sq3[:, :, 0:W], in_=x3[:, :, 0:W], func=AF.Square),
                     reads=[b_xt[sl]], writes=[b_sq])
                rstd_from_sq(sq3[:, :, 0:W], W, 8, PS[7], PSB[7], rstd_s[sl], b_rstd[sl], b_sq, float(D))
                for k in range(8):
                    q = k % 2
                    S.op("dve", lambda e, k=k, q=q: e.scalar_tensor_tensor(
                        t1_s[q][:, 0:W], x3[:, k, 0:W], A(k), rstd_s[sl][:, 0:W], ALU.mult, ALU.mult),
                        reads=[b_xt[sl], b_rstd[sl], b_mod], writes=[b_t1[q]])
                    S.op("act", lambda e, k=k, q=q: e.activation(
                        out=h3[:, k, 0:W], in_=t1_s[q][:, 0:W], func=AF.Identity, bias=Bs(k), scale=1.0),
                        reads=[b_t1[q], b_mod], writes=[b_h[sl]])

            def project(ti):
                t0, W, isc = TILES[ti]
                sl = ti % 2
                h3 = h_s[sl].rearrange("p (k t) -> p k t", k=8)
                qk3 = qk_s[sl].rearrange("p (c t) -> p c t", c=NQ + NKC)
                cos_t = cs_s[sl][:, 0:W]
                sin_t = cs_s[sl][:, 512:512 + W]
                for c in range(NQ + NKC):
                    u = cnt[0] % 2
                    cnt[0] += 1
                    pb = u
                    pr = 2 + u
                    ps = PS[pb]
                    for k in range(8):
                        S.op("pe", lambda e, k=k, c=c, ps=ps: e.matmul(
                            ps[:, 0:W], w3[:, k, c * 128:(c + 1) * 128], h3[:, k, 0:W], start=(k == 0), stop=(k == 7)),
                            reads=[b_w, b_h[sl]], writes=[PSB[pb]])
                    if is_da:
                        src = ps[:, 0:W]
                        srcb = [PSB[pb]]
                    else:
                        gain = gq if c < NQ else gk
                        S.op("act", lambda e, u=u, ps=ps: e.activation(out=qb_s[u][:, 0:W], in_=ps[:, 0:W], func=AF.Square),
                             reads=[PSB[pb]], writes=[b_qb[u]])
                        S.op("pe", lambda e, u=u: e.matmul(PS[4 + u][:, 0:W], ones_bf, qb_s[u][:, 0:W], start=True, stop=True),
                             reads=[b_qb[u], b_c2], writes=[PSB[4 + u]])
                        S.op("act", lambda e, u=u: e.activation(out=rs_s[u][:, 0:W], in_=PS[4 + u][:, 0:W], func=AF.Sqrt,
                                                                scale=1.0 / 128.0, bias=eps_t),
                             reads=[PSB[4 + u], b_c2], writes=[b_rs[u]])
                        S.op("dve", lambda e, u=u: e.reciprocal(rs_s[u][:, 0:W], rs_s[u][:, 0:W]), reads=[b_rs[u]], writes=[b_rs[u]])
                        S.op("dve", lambda e, u=u, ps=ps, gain=gain: e.scalar_tensor_tensor(
                            qn_s[u][:, 0:W], ps[:, 0:W], gain, rs_s[u][:, 0:W], ALU.mult, ALU.mult),
                            reads=[PSB[pb], b_rs[u], b_const], writes=[b_qn[u]])
                        src = qn_s[u][:, 0:W]
                        srcb = [b_qn[u]]
                    if isc:
                        S.op("act", lambda e, c=c, src=src: e.activation(out=qk3[:, c, 0:W], in_=src, func=AF.Copy),
                             reads=srcb, writes=[b_qk[sl]])
                    else:
                        S.op("act", lambda e, u=u, src=src: e.activation(out=qb_s[u][:, 0:W], in_=src, func=AF.Copy),
                             reads=srcb, writes=[b_qb[u]])
                        S.op("pe", lambda e, u=u, pr=pr: e.matmul(PS[pr][:, 0:W], rotm_bf, qb_s[u][:, 0:W], start=True, stop=True),
                             reads=[b_qb[u], b_c2], writes=[PSB[pr]])
                        S.op("dve", lambda e, u=u, src=src: e.tensor_tensor(ta_s[u][:, 0:W], src, cos_t, ALU.mult),
                             reads=srcb + [b_cs[sl]], writes=[b_ta[u]])
                        S.op("dve", lambda e, u=u, pr=pr: e.tensor_tensor(tb_s[u][:, 0:W], PS[pr][:, 0:W], sin_t, ALU.mult),
                             reads=[PSB[pr], b_cs[sl]], writes=[b_tb[u]])
                        S.op("pool", lambda e, u=u, c=c: e.tensor_tensor(qk3[:, c, 0:W], ta_s[u][:, 0:W], tb_s[u][:, 0:W], ALU.add),
                             reads=[b_ta[u], b_tb[u]], writes=[b_qk[sl]])
                ntb = W // 128
                vcols = NV * 128
                v3 = v_s[sl].rearrange("p (tb e) -> p tb e", tb=4)
                voff = (NQ + NKC) * 128
                for tb in range(ntb):
                    for c0 in range(0, vcols, 512):
                        cw = min(512, vcols - c0)
                        u = cnt[0] % 2
                        cnt[0] += 1
                        pb = u
                        ps = PS[pb]
                        for k in range(8):
                            S.op("pe", lambda e, k=k, tb=tb, c0=c0, cw=cw, ps=ps: e.matmul(
                                ps[:, 0:cw], h3[:, k, tb * 128:(tb + 1) * 128], w3[:, k, voff + c0:voff + c0 + cw],
                                start=(k == 0), stop=(k == 7)),
                                reads=[b_w, b_h[sl]], writes=[PSB[pb]])
                        if u == 0:
                            S.op("act", lambda e, tb=tb, c0=c0, cw=cw, ps=ps: e.activation(
                                out=v3[:, tb, c0:c0 + cw], in_=ps[:, 0:cw], func=AF.Copy),
                                reads=[PSB[pb]], writes=[b_v[sl]])
                        else:
                            S.op("dve", lambda e, tb=tb, c0=c0, cw=cw, ps=ps: e.tensor_copy(
                                v3[:, tb, c0:c0 + cw], ps[:, 0:cw]),
                                reads=[PSB[pb]], writes=[b_v[sl]])
                S.dma("pool", sem_st[sl], qk_v[:, 0:NQ + NKC, t0:t0 + W], qk3[:, :, 0:W], reads=[b_qk[sl]])
                S.dma("pool", sem_sv[sl], v_v[:, t0 // 128:t0 // 128 + ntb, 0:vcols], v3[:, 0:ntb, 0:vcols],
                      reads=[b_v[sl]])

            load_x(0)
            prenorm(0)
            for ti in range(n):
                if ti + 1 < n:
                    load_x(ti + 1)
                project(ti)
                if ti + 1 < n:
                    prenorm(ti + 1)
            S.barrier()

        def attn_core_phase(i, kind, j):
            is_da = kind == 0
            need_ctx = i < DEPTH - 1
            NQ = 8
            NKC = 8 if is_da else 2
            jj = j
            AR.reset()
            q_s = [AR.bf16(T) for _ in range(2)]
            k_s = [AR.bf16(T) for _ in range(2)]
            k2_s = [AR.bf16(T) for _ in range(2)] if is_da else None
            v_s = [AR.bf16(34 * 128) for _ in range(2)]
            NPT = 6
            pT_s = [AR.bf16(512) for _ in range(NPT)]
            pr_s = [AR.bf16(512) for _ in range(2)]
            b_pr = [Buf(), Buf()]
            r_s = [AR.f32(512) for _ in range(2)]
            dacc_s = [AR.f32(512) for _ in range(2)]
            b_dacc = [Buf(), Buf()]
            o_s = [AR.f32(512) for _ in range(2)]
            osq_s = AR.bf16(512)
            rs_s = AR.f32(512)
            on_s = [AR.bf16(512) for _ in range(2)]
            b_hd = [Buf(), Buf()]
            b_pT = [Buf() for _ in range(NPT)]
            b_r = [Buf(), Buf()]
            b_o = [Buf(), Buf()]
            b_osq, b_rs = Buf(), Buf()
            b_on = [Buf(), Buf()]
            tag = "c%d" % i
            sem_h = [S.new_dma_sem("ch0_" + tag), S.new_dma_sem("ch1_" + tag)]
            sem_st = [S.new_dma_sem("cst0_" + tag), S.new_dma_sem("cst1_" + tag)]
            v_v = vtok.rearrange("(c p) e -> p c e", p=128)
            qtiles = [t for t in range(len(TILES)) if need_ctx or not TILES[t][2]]
            maps = [(0, 64), (64, 64)] if is_da else [(0, 128)]
            nm = len(maps)
            sc = 0.125 if is_da else 128.0 ** -0.5
            neglam = da_neglam[:, jj:jj + 1]
            subs = da_subs[:, jj:jj + 1]
            octr = [0]
            pctr = [0]

            def load_head(hq):
                sl = hq % 2
                kc = NQ + (hq if is_da else hq // 4)
                vc = hq if is_da else hq // 4
                S.dma("sp", sem_h[sl], q_s[sl], qkT[hq * 128:(hq + 1) * 128, :], writes=[b_hd[sl]])
                if is_da:
                    S.dma("sp", sem_h[sl], k_s[sl][0:64, :], qkT[kc * 128:kc * 128 + 64, :], reads=[b_kz], writes=[b_hd[sl]])
                    S.dma("sp", sem_h[sl], k2_s[sl][64:128, :], qkT[kc * 128 + 64:(kc + 1) * 128, :], reads=[b_kz], writes=[b_hd[sl]])
                else:
                    S.dma("sp", sem_h[sl], k_s[sl], qkT[kc * 128:(kc + 1) * 128, :], writes=[b_hd[sl]])
                vd = v_s[sl].rearrange("p (c e) -> p c e", c=34)
                for c0 in range(0, 34, 8):
                    c1 = min(34, c0 + 8)
                    S.dma("sp", sem_h[sl], vd[:, c0:c1, :], v_v[:, c0:c1, vc * 128:(vc + 1) * 128],
                          writes=[b_hd[sl]])

            def head(hq):
                sl = hq % 2
                v3 = v_s[sl].rearrange("p (c e) -> p c e", c=34)
                for t in qtiles:
                    qtile(hq, sl, v3, t)

            def qtile(hq, sl, v3, t):
                if True:
                    t0, W, isc = TILES[t]
                    kcs = [32, 33] if isc else list(range(34))
                    steps = [(kc, m) for kc in kcs for m in range(nm)]

                    def score(n):
                        kc, m = steps[n]
                        r0, rn = maps[m]
                        pb = (0, 1, 7)[pctr[0] % 3]
                        kk_ = k2_s[sl] if (is_da and m == 1) else k_s[sl]
                        S.op("pe", lambda e, kc=kc, pb=pb, kk_=kk_: e.matmul(
                            PS[pb][:, 0:W], kk_[:, kc * 128:(kc + 1) * 128], q_s[sl][:, t0:t0 + W],
                            start=True, stop=True),
                            reads=[b_hd[sl]], writes=[PSB[pb]])
                        pctr[0] += 1
                        return pb

                    pend = [score(0)]
                    if len(steps) > 1:
                        pend.append(score(1))
                    for n in range(len(steps)):
                        kc, m = steps[n]
                        pb = pend.pop(0)
                        if n + 2 < len(steps):
                            pend.append(score(n + 2))
                        u = n % NPT
                        S.op("act", lambda e, pb=pb, u=u: e.activation(out=pT_s[u][:, 0:W], in_=PS[pb][:, 0:W], func=AF.Exp, scale=sc),
                             reads=[PSB[pb]], writes=[b_pT[u]])
                        first = (n < nm)
                        lastk = (n >= len(steps) - nm)
                        S.op("pe", lambda e, kc=kc, m=m, u=u, first=first, lastk=lastk: e.matmul(
                            PS[2 + m][:, 0:W], v3[:, kc, :], pT_s[u][:, 0:W], start=first, stop=lastk),
                            reads=[b_hd[sl], b_pT[u]], writes=[PSB[2 + m]])
                        ik = n // nm
                        if ik % 2 == 1:
                            up = (n - nm) % NPT
                            if ik == 1:
                                S.op("dve", lambda e, m=m, u=u, up=up: e.tensor_tensor(
                                    dacc_s[m][:, 0:W], pT_s[up][:, 0:W], pT_s[u][:, 0:W], ALU.add),
                                    reads=[b_pT[up], b_pT[u]], writes=[b_dacc[m]])
                            else:
                                S.op("dve", lambda e, m=m, u=u, up=up: e.tensor_tensor(
                                    pr_s[m][:, 0:W], pT_s[up][:, 0:W], pT_s[u][:, 0:W], ALU.add),
                                    reads=[b_pT[up], b_pT[u]], writes=[b_pr[m]])
                                S.op("dve", lambda e, m=m: e.tensor_tensor(
                                    dacc_s[m][:, 0:W], dacc_s[m][:, 0:W], pr_s[m][:, 0:W], ALU.add),
                                    reads=[b_pr[m]], writes=[b_dacc[m]])
                        if lastk:
                            S.op("pe", lambda e, m=m: e.matmul(PS[4 + m][:, 0:W], ones_f, dacc_s[m][:, 0:W], start=True, stop=True),
                                 reads=[b_c2, b_dacc[m]], writes=[PSB[4 + m]])
                    oc = octr[0] % 2
                    octr[0] += 1
                    for m in range(nm):
                        S.op("dve", lambda e, m=m: e.reciprocal(r_s[m][:, 0:W], PS[4 + m][:, 0:W]),
                             reads=[PSB[4 + m]], writes=[b_r[m]])
                    if is_da:
                        S.op("dve", lambda e: e.tensor_tensor(o_s[0][:, 0:W], PS[2][:, 0:W], r_s[0][:, 0:W], ALU.mult),
                             reads=[PSB[2], b_r[0]], writes=[b_o[0]])
                        S.op("dve", lambda e: e.scalar_tensor_tensor(o_s[1][:, 0:W], PS[3][:, 0:W], neglam, r_s[1][:, 0:W], ALU.mult, ALU.mult),
                             reads=[PSB[3], b_r[1], b_da], writes=[b_o[1]])
                        S.op("pool", lambda e: e.tensor_tensor(o_s[0][:, 0:W], o_s[0][:, 0:W], o_s[1][:, 0:W], ALU.add),
                             reads=[b_o[1]], writes=[b_o[0]])
                        S.op("act", lambda e: e.activation(out=osq_s[:, 0:W], in_=o_s[0][:, 0:W], func=AF.Square),
                             reads=[b_o[0]], writes=[b_osq])
                        S.op("pe", lambda e: e.matmul(PS[6][:, 0:W], ones_bf, osq_s[:, 0:W], start=True, stop=True),
                             reads=[b_osq, b_c2], writes=[PSB[6]])
                        S.op("act", lambda e: e.activation(out=rs_s[:, 0:W], in_=PS[6][:, 0:W], func=AF.Sqrt, scale=1.0 / 128.0, bias=eps_t),
                             reads=[PSB[6], b_c2], writes=[b_rs])
                        S.op("dve", lambda e: e.reciprocal(rs_s[:, 0:W], rs_s[:, 0:W]), reads=[b_rs], writes=[b_rs])
                        S.op("dve", lambda e, oc=oc: e.scalar_tensor_tensor(on_s[oc][:, 0:W], o_s[0][:, 0:W], subs, rs_s[:, 0:W], ALU.mult, ALU.mult),
                             reads=[b_o[0], b_rs, b_da], writes=[b_on[oc]])
                    else:
                        S.op("dve", lambda e, oc=oc: e.tensor_tensor(on_s[oc][:, 0:W], PS[2][:, 0:W], r_s[0][:, 0:W], ALU.mult),
                             reads=[PSB[2], b_r[0]], writes=[b_on[oc]])
                    S.dma("pool", sem_st[oc], aT[hq * 128:(hq + 1) * 128, t0:t0 + W], on_s[oc][:, 0:W], reads=[b_on[oc]])

            b_kz = Buf()
            if is_da:
                for sl_ in range(2):
                    S.op("pool", lambda e, sl_=sl_: e.memset(k_s[sl_][64:128, :], 0.0), writes=[b_kz])
                    S.op("pool", lambda e, sl_=sl_: e.memset(k2_s[sl_][0:64, :], 0.0), writes=[b_kz])
            load_head(0)
            for hq in range(NQ):
                if hq + 1 < NQ:
                    load_head(hq + 1)
                head(hq)
            S.barrier()

        def mixer_out_phase(i, wsrc, cvb):
            need_ctx = i < DEPTH - 1
            tiles = [t for t in range(len(TILES)) if need_ctx or not TILES[t][2]]
            AR.reset()
            xt_s = [AR.f32(8 * 512) for _ in range(2)]
            a_s = [AR.bf16(8 * 512) for _ in range(2)]
            ysb = AR.f32(8 * 512)
            sq_s = AR.bf16(8 * 512)
            rstd_s = AR.f32(512)
            w_s = AR.bf16(8 * 1024)
            b_xt = [Buf(), Buf()]
            b_a = [Buf(), Buf()]
            b_ysb, b_sq, b_rstd, b_w = Buf(), Buf(), Buf(), Buf()
            tag = "o%d" % i
            sem_x = [S.new_dma_sem("ox0_" + tag), S.new_dma_sem("ox1_" + tag)]
            sem_a = [S.new_dma_sem("oa0_" + tag), S.new_dma_sem("oa1_" + tag)]
            sem_w = S.new_dma_sem("ow_" + tag)
            sem_st = S.new_dma_sem("ost_" + tag)
            w3 = w_s.rearrange("p (k n) -> p k n", k=8)
            S.dma("sp", sem_w, w3, wsrc.rearrange("(k p) n -> p k n", p=128), reads=cvb, writes=[b_w])
            a_v = aT.rearrange("(k p) t -> p k t", p=128)

            def load(ti):
                t = tiles[ti]
                t0, W, isc = TILES[t]
                sl = ti % 2
                x3 = xt_s[sl].rearrange("p (k t) -> p k t", k=8)
                a3 = a_s[sl].rearrange("p (k t) -> p k t", k=8)
                S.dma("sp", sem_x[sl], x3[:, :, 0:W], xs_v[:, :, t0:t0 + W], reads=[XB[t]], writes=[b_xt[sl]])
                S.dma("sp", sem_a[sl], a3[:, :, 0:W], a_v[:, :, t0:t0 + W], writes=[b_a[sl]])

            def compute(ti):
                t = tiles[ti]
                t0, W, isc = TILES[t]
                sl = ti % 2
                A, Bs, G = scal(i, 1, 1 if isc else 0)
                x3 = xt_s[sl].rearrange("p (k t) -> p k t", k=8)
                a3 = a_s[sl].rearrange("p (k t) -> p k t", k=8)
                y3 = ysb.rearrange("p (k t) -> p k t", k=8)
                sq3 = sq_s.rearrange("p (k t) -> p k t", k=8)
                for dk in range(8):
                    pb = dk % 2
                    py = PS[pb]
                    for k in range(8):
                        S.op("pe", lambda e, k=k, dk=dk, py=py: e.matmul(
                            py[:, 0:W], w3[:, k, dk * 128:(dk + 1) * 128], a3[:, k, 0:W], start=(k == 0), stop=(k == 7)),
                            reads=[b_w, b_a[sl]], writes=[PSB[pb]])
                    S.op("act", lambda e, dk=dk, py=py: e.activation(out=sq3[:, dk, 0:W], in_=py[:, 0:W], func=AF.Square),
                         reads=[PSB[pb]], writes=[b_sq])
                    S.op("dve", lambda e, dk=dk, py=py: e.tensor_copy(y3[:, dk, 0:W], py[:, 0:W]),
                         reads=[PSB[pb]], writes=[b_ysb])
                rstd_from_sq(sq3[:, :, 0:W], W, 8, PS[4], PSB[4], rstd_s, b_rstd, b_sq, float(D))
                for k in range(8):
                    S.op("dve", lambda e, k=k: e.scalar_tensor_tensor(
                        y3[:, k, 0:W], y3[:, k, 0:W], G(k), rstd_s[:, 0:W], ALU.mult, ALU.mult),
                        reads=[b_rstd, b_mod], writes=[b_ysb])
                S.op("pool", lambda e: e.tensor_tensor(y3[:, :, 0:W], y3[:, :, 0:W], x3[:, :, 0:W], ALU.add),
                     reads=[b_xt[sl], b_ysb], writes=[b_ysb])
                S.dma("pool", sem_st, xs_v[:, :, t0:t0 + W], y3[:, :, 0:W], reads=[b_ysb], writes=[XB[t]])

            n = len(tiles)
            load(0)
            for ti in range(n):
                if ti + 1 < n:
                    load(ti + 1)
                compute(ti)
            S.barrier()

        NCH = T // 64
        NBL = T // 128

        def hgrn_proj_phase(i, j):
            cvb = conv_bufs["hgin%d" % j]
            AR.reset()
            xt_s = AR.f32(8 * 512)
            sq_s = AR.bf16(8 * 512)
            rstd_s = AR.f32(512)
            t1_s = [AR.f32(512) for _ in range(2)]
            h_s = [AR.bf16(8 * 512) for _ in range(2)]
            w_s = AR.bf16(8 * 5120)
            stg = [AR.f32(8 * 512) for _ in range(2)]
            v_s = AR.bf16(4 * 1024)
            sg_s = [AR.f32(512) for _ in range(2)]
            b_xt, b_sq, b_rstd, b_w, b_v = Buf(), Buf(), Buf(), Buf(), Buf()
            b_t1 = [Buf(), Buf()]
            b_h = [Buf(), Buf()]
            b_stg = [Buf(), Buf()]
            b_sg = [Buf(), Buf()]
            tag = "hp%d" % i
            sem_x = S.new_dma_sem("hx_" + tag)
            sem_w = S.new_dma_sem("hw_" + tag)
            sem_st = [S.new_dma_sem("hst0_" + tag), S.new_dma_sem("hst1_" + tag)]
            sem_sv = S.new_dma_sem("hsv_" + tag)
            w3 = w_s.rearrange("p (k n) -> p k n", k=8)
            wv = hg_in_b[j].rearrange("(k p) n -> p k n", p=128)
            for a in range(0, 5120, 1024):
                S.dma("sp", sem_w, w3[:, :, a:a + 1024], wv[:, :, a:a + 1024], reads=cvb, writes=[b_w])
            v_v = vtok.rearrange("(tb p) e -> p tb e", p=128)
            dst_f32 = [hq_T, hlf_T[0], hkk_T[0], hlf_T[1], hkk_T[1]]
            n = len(TILES)
            cnt = [0]
            scnt = [0]

            def load_x(ti):
                t0, W, isc = TILES[ti]
                x3 = xt_s.rearrange("p (k t) -> p k t", k=8)
                S.dma("sp", sem_x, x3[:, :, 0:W], xs_v[:, :, t0:t0 + W], reads=[XB[ti]], writes=[b_xt])

            def prenorm(ti):
                t0, W, isc = TILES[ti]
                sl = ti % 2
                A, Bs, G = scal(i, 1, 1 if isc else 0)
                x3 = xt_s.rearrange("p (k t) -> p k t", k=8)
                sq3 = sq_s.rearrange("p (k t) -> p k t", k=8)
                h3 = h_s[sl].rearrange("p (k t) -> p k t", k=8)
                S.op("act", lambda e: e.activation(out=sq3[:, :, 0:W], in_=x3[:, :, 0:W], func=AF.Square),
                     reads=[b_xt], writes=[b_sq])
                rstd_from_sq(sq3[:, :, 0:W], W, 8, PS[7], PSB[7], rstd_s, b_rstd, b_sq, float(D))
                for k in range(8):
                    q = k % 2
                    S.op("dve", lambda e, k=k, q=q: e.scalar_tensor_tensor(
                        t1_s[q][:, 0:W], x3[:, k, 0:W], A(k), rstd_s[:, 0:W], ALU.mult, ALU.mult),
                        reads=[b_xt, b_rstd, b_mod], writes=[b_t1[q]])
                    S.op("act", lambda e, k=k, q=q: e.activation(
                        out=h3[:, k, 0:W], in_=t1_s[q][:, 0:W], func=AF.Identity, bias=Bs(k), scale=1.0),
                        reads=[b_t1[q], b_mod], writes=[b_h[sl]])

            def proj_chunk(ti, col0):
                t0, W, isc = TILES[ti]
                sl = ti % 2
                h3 = h_s[sl].rearrange("p (k t) -> p k t", k=8)
                u = cnt[0] % 4
                cnt[0] += 1
                ps = PS[u]
                for k in range(8):
                    S.op("pe", lambda e, k=k, ps=ps: e.matmul(
                        ps[:, 0:W], w3[:, k, col0:col0 + 128], h3[:, k, 0:W], start=(k == 0), stop=(k == 7)),
                        reads=[b_w, b_h[sl]], writes=[PSB[u]])
                return u

            def project(ti):
                t0, W, isc = TILES[ti]
                sl = ti % 2
                h3 = h_s[sl].rearrange("p (k t) -> p k t", k=8)
                g = scnt[0] % 2
                scnt[0] += 1
                s3 = stg[g].rearrange("p (k t) -> p k t", k=8)
                for hh in range(8):
                    u = proj_chunk(ti, hh * 128)
                    S.op("act", lambda e, hh=hh, u=u, s3=s3: e.activation(out=s3[:, hh, 0:W], in_=PS[u][:, 0:W], func=AF.Silu),
                         reads=[PSB[u]], writes=[b_stg[g]])
                S.dma("pool", sem_st[g], hq_T.rearrange("(k p) t -> p k t", p=128)[:, :, t0:t0 + W], s3[:, :, 0:W], reads=[b_stg[g]])
                for d in range(2):
                    g1 = scnt[0] % 2
                    g2 = (scnt[0] + 1) % 2
                    scnt[0] += 2
                    l3 = stg[g1].rearrange("p (k t) -> p k t", k=8)
                    k3 = stg[g2].rearrange("p (k t) -> p k t", k=8)
                    for hh in range(8):
                        u = proj_chunk(ti, (1 + d) * 1024 + hh * 128)
                        q = hh % 2
                        S.op("act", lambda e, u=u, q=q: e.activation(out=sg_s[q][:, 0:W], in_=PS[u][:, 0:W], func=AF.Sigmoid),
                             reads=[PSB[u]], writes=[b_sg[q]])
                        S.op("dve", lambda e, q=q, d=d, hh=hh: e.tensor_scalar(
                            sg_s[q][:, 0:W], sg_s[q][:, 0:W], hg_oml[:, d * 8 + hh:d * 8 + hh + 1], hg_lb[:, d * 8 + hh:d * 8 + hh + 1],
                            ALU.mult, ALU.add),
                            reads=[b_hg], writes=[b_sg[q]])
                        S.op("act", lambda e, q=q, hh=hh, l3=l3: e.activation(out=l3[:, hh, 0:W], in_=sg_s[q][:, 0:W], func=AF.Ln),
                             reads=[b_sg[q]], writes=[b_stg[g1]])
                        S.op("dve", lambda e, q=q, hh=hh, k3=k3: e.tensor_scalar(
                            k3[:, hh, 0:W], sg_s[q][:, 0:W], -1.0, 1.0, ALU.mult, ALU.add),
                            reads=[b_sg[q]], writes=[b_stg[g2]])
                    S.dma("pool", sem_st[g1], hlf_T[d].rearrange("(k p) t -> p k t", p=128)[:, :, t0:t0 + W], l3[:, :, 0:W], reads=[b_stg[g1]])
                    S.dma("pool", sem_st[g2], hkk_T[d].rearrange("(k p) t -> p k t", p=128)[:, :, t0:t0 + W], k3[:, :, 0:W], reads=[b_stg[g2]])
                g = scnt[0] % 2
                scnt[0] += 1
                gb3 = stg[g].bitcast(BF16)[:, 0:8 * 512].rearrange("p (k t) -> p k t", k=8)
                for hh in range(8):
                    u = proj_chunk(ti, 4096 + hh * 128)
                    S.op("act", lambda e, hh=hh, u=u, gb3=gb3: e.activation(out=gb3[:, hh, 0:W], in_=PS[u][:, 0:W], func=AF.Silu),
                         reads=[PSB[u]], writes=[b_stg[g]])
                S.dma("pool", sem_st[g], qkT.rearrange("(c p) t -> p c t", p=128)[:, 0:8, t0:t0 + W], gb3[:, :, 0:W], reads=[b_stg[g]])
                ntb = W // 128
                v3 = v_s.rearrange("p (tb e) -> p tb e", tb=4)
                for tb in range(ntb):
                    for c0 in range(0, 1024, 512):
                        u = cnt[0] % 4
                        cnt[0] += 1
                        ps = PS[u]
                        for k in range(8):
                            S.op("pe", lambda e, k=k, tb=tb, c0=c0, ps=ps: e.matmul(
                                ps[:, 0:512], h3[:, k, tb * 128:(tb + 1) * 128], w3[:, k, 3072 + c0:3072 + c0 + 512],
                                start=(k == 0), stop=(k == 7)),
                                reads=[b_w, b_h[sl]], writes=[PSB[u]])
                        S.op("dve", lambda e, tb=tb, c0=c0, ps=ps: e.tensor_copy(v3[:, tb, c0:c0 + 512], ps[:, 0:512]),
                             reads=[PSB[u]], writes=[b_v])
                S.dma("pool", sem_sv, v_v[:, t0 // 128:t0 // 128 + ntb, :], v3[:, 0:ntb, :], reads=[b_v])

            load_x(0)
            prenorm(0)
            for ti in range(n):
                if ti + 1 < n:
                    load_x(ti + 1)
                project(ti)
                if ti + 1 < n:
                    prenorm(ti + 1)
            S.barrier()

        def hgrn_scan_phase(i, j):
            AR.reset()
            bufA = AR.f32(T)
            bufB = AR.f32(T)
            bufC = AR.f32(T)
            bufQ = AR.f32(T)
            v_s = AR.bf16(NBL * 128)
            gs_s = AR.bf16(T)
            qt_s = AR.bf16(T)
            qh_s = AR.bf16(T)
            kt_s = AR.bf16(T)
            kh_s = AR.bf16(T)
            gm_s = AR.f32(NCH)
            ktok_s = AR.bf16(NBL * 128)
            sbef_s = AR.bf16(NCH * 128)
            st_s = [AR.f32(128) for _ in range(2)]
            att_s = [AR.bf16(512) for _ in range(2)]
            attf_s = AR.f32(512)
            b_attf = Buf()
            oacc = AR.f32(T)
            gl_s = AR.f32(NCH)
            dec_s = AR.f32(NCH)
            mask_s = AR.f32(512)
            msk_s = [AR.bf16(512), AR.bf16(512)]
            osq_s = AR.bf16(512)
            rs_s = AR.f32(512)
            on_s = [AR.bf16(512) for _ in range(2)]
            bA, bB, bC, bQ, bV, bG = Buf(), Buf(), Buf(), Buf(), Buf(), Buf()
            b_qt, b_qh, b_kt, b_ktok, b_sbef = Buf(), Buf(), Buf(), Buf(), Buf()
            b_kh, b_gm = Buf(), Buf()
            b_st = [Buf(), Buf()]
            b_att = [Buf(), Buf()]
            b_oacc, b_gl, b_dec, b_mask = Buf(), Buf(), Buf(), Buf()
            b_osq, b_rs = Buf(), Buf()
            b_on = [Buf(), Buf()]
            tag = "hs%d" % i
            semA, semB, semQ, semV, semG = (S.new_dma_sem(n_ + tag) for n_ in ("hA", "hB", "hQ", "hV", "hG"))
            sem_o = [S.new_dma_sem("ho0" + tag), S.new_dma_sem("ho1" + tag)]
            semM = S.new_dma_sem("hM" + tag)
            v_v = vtok.rearrange("(c p) e -> p c e", p=128)
            S.op("dve", lambda e: e.memset(mask_s, 1.0), writes=[b_mask])
            S.op("dve", lambda e: e.memset(mask_s.rearrange("p (c j) -> p c j", j=64)[:, :, 0:1], 0.0), writes=[b_mask])
            mtmp = AR.f32(256)
            b_mt = Buf()
            S.dma("sp", semM, mtmp, hmask[:, :], writes=[b_mt])
            for dd in range(2):
                for r in range(4):
                    S.op("dve", lambda e, dd=dd, r=r: e.tensor_copy(msk_s[dd][:, r * 128:(r + 1) * 128], mtmp[:, dd * 128:(dd + 1) * 128]),
                         reads=[b_mt], writes=[b_mask])
            ord_f = [64, 65, 66, 67] + list(range(64))
            ord_b = [67, 66, 65, 64] + list(range(63, -1, -1))
            A3 = bufA.rearrange("p (c j) -> p c j", j=64)
            C3 = bufC.rearrange("p (c j) -> p c j", j=64)
            glb = gl_s.unsqueeze(2).to_broadcast([128, NCH, 64])
            pctr = [0]
            octr = [0]

            def direction(hh, d):
                S.dma("sp", semA, bufA, hlf_T[d][hh * 128:(hh + 1) * 128, :], writes=[bA])
                S.dma("sp", semB, bufB, hkk_T[d][hh * 128:(hh + 1) * 128, :], writes=[bB])
                for (t0, W, isc) in TILES:
                    S.op("dve", lambda e, t0=t0, W=W: e.tensor_tensor_scan(
                        bufC[:, t0:t0 + W], mask_s[:, 0:W], bufA[:, t0:t0 + W], 0.0, ALU.mult, ALU.add),
                        reads=[bA, b_mask], writes=[bC])
                S.op("dve", lambda e: e.tensor_copy(gl_s.unsqueeze(2), C3[:, :, 63:64]), reads=[bC], writes=[b_gl])
                S.op("act", lambda e: e.activation(out=dec_s, in_=gl_s, func=AF.Exp), reads=[b_gl], writes=[b_dec])
                if d == 0:
                    S.op("dve", lambda e: e.tensor_tensor(A3, C3, glb, ALU.subtract), reads=[bC, b_gl], writes=[bA])
                else:
                    S.op("dve", lambda e: e.tensor_tensor(bufA, bufA, bufC, ALU.subtract), reads=[bC], writes=[bA])
                    S.op("dve", lambda e: e.tensor_tensor(C3, A3, glb, ALU.add), reads=[bA, b_gl], writes=[bC])
                S.op("act", lambda e: e.activation(out=bufC, in_=bufC, func=AF.Exp), reads=[], writes=[bC])
                S.op("dve", lambda e: e.tensor_tensor(qt_s, bufQ, bufC, ALU.mult), reads=[bQ, bC], writes=[b_qt])
                S.op("act", lambda e: e.activation(out=bufC, in_=bufA, func=AF.Exp, scale=-1.0), reads=[bA], writes=[bC])
                S.op("dve", lambda e: e.tensor_tensor(kt_s, bufB, bufC, ALU.mult), reads=[bB, bC], writes=[b_kt])
                S.op("dve", lambda e: e.tensor_copy(gm_s.unsqueeze(2), A3[:, :, 32:33]), reads=[bA], writes=[b_gm])
                S.op("dve", lambda e: e.tensor_tensor(A3, A3, gm_s.unsqueeze(2).to_broadcast([128, NCH, 64]), ALU.subtract),
                     reads=[b_gm], writes=[bA])
                S.op("dve", lambda e: e.tensor_scalar(bufA, bufA, -80.0, 80.0, ALU.max, ALU.min), reads=[], writes=[bA])
                S.op("act", lambda e: e.activation(out=bufC, in_=bufA, func=AF.Exp), reads=[bA], writes=[bC])
                S.op("dve", lambda e: e.tensor_tensor(qh_s, bufQ, bufC, ALU.mult), reads=[bQ, bC], writes=[b_qh])
                S.op("act", lambda e: e.activation(out=bufC, in_=bufA, func=AF.Exp, scale=-1.0), reads=[bA], writes=[bC])
                S.op("dve", lambda e: e.tensor_tensor(kh_s, bufB, bufC, ALU.mult), reads=[bB, bC], writes=[b_kh])
                kt3 = ktok_s.rearrange("p (b k) -> p b k", b=NBL)
                for b0 in range(0, NBL, 8):
                    b1 = min(NBL, b0 + 8)
                    pb = 6 + (pctr[0] % 2)
                    pctr[0] += 1
                    pbf = PS[pb][:, :].bitcast(BF16)
                    for b in range(b0, b1):
                        S.op("pe", lambda e, b=b, b0=b0, pbf=pbf: e.transpose(
                            pbf[:, (b - b0) * 128:(b - b0 + 1) * 128], kt_s[:, b * 128:(b + 1) * 128], ident_bf),
                            reads=[b_kt, b_c2], writes=[PSB[pb]])
                    S.op("act", lambda e, b0=b0, b1=b1, pbf=pbf: e.activation(
                        out=ktok_s[:, b0 * 128:b1 * 128], in_=pbf[:, 0:(b1 - b0) * 128], func=AF.Copy),
                        reads=[PSB[pb]], writes=[b_ktok])
                order = ord_f if d == 0 else ord_b
                v3 = v_s.rearrange("p (b e) -> p b e", b=NBL)
                sb3 = sbef_s.rearrange("p (c v) -> p c v", c=NCH)
                cur = None
                for g0 in range(0, NCH, 8):
                    grp = order[g0:g0 + 8]
                    slot = {}
                    used = [0, 0]
                    for c in grp:
                        half = c % 2
                        slot[c] = (4 + half, used[half])
                        used[half] += 1
                    for c in grp:
                        blk, half = c // 2, c % 2
                        pb, gi = slot[c]
                        S.op("pe", lambda e, gi=gi, blk=blk, half=half, pb=pb: e.matmul(
                            PS[pb][:, gi * 128:(gi + 1) * 128], kt3[half * 64:(half + 1) * 64, blk, :], v3[half * 64:(half + 1) * 64, blk, :],
                            start=True, stop=True),
                            reads=[b_ktok, bV], writes=[PSB[pb]])
                    for gj, c in enumerate(grp):
                        pos = g0 + gj
                        nxt = pos % 2
                        pb, gi = slot[c]
                        if pos == 0:
                            S.op("dve", lambda e, gi=gi, pb=pb, nxt=nxt: e.tensor_copy(st_s[nxt], PS[pb][:, gi * 128:(gi + 1) * 128]),
                                 reads=[PSB[pb]], writes=[b_st[nxt]])
                        else:
                            prv = 1 - nxt
                            S.op("act", lambda e, c=c, prv=prv: e.activation(out=sb3[:, c, :], in_=st_s[prv], func=AF.Copy),
                                 reads=[b_st[prv]], writes=[b_sbef])
                            if pos < NCH - 1:
                                S.op("dve", lambda e, gi=gi, pb=pb, c=c, nxt=nxt, prv=prv: e.scalar_tensor_tensor(
                                    st_s[nxt], st_s[prv], dec_s[:, c:c + 1], PS[pb][:, gi * 128:(gi + 1) * 128], ALU.mult, ALU.add),
                                    reads=[b_st[prv], PSB[pb], b_dec], writes=[b_st[nxt]])
                first_c = order[0]
                for q0 in range(0, NBL, 4):
                    q1 = min(NBL, q0 + 4)
                    nb = q1 - q0
                    pa = pctr[0] % 2
                    po = 2 + (pctr[0] % 2)
                    pctr[0] += 1
                    au = pa
                    for b in range(q0, q1):
                        S.op("pe", lambda e, b=b, q0=q0, pa=pa: e.matmul(
                            PS[pa][:, (b - q0) * 128:(b - q0 + 1) * 128], kh_s[:, b * 128:(b + 1) * 128], qh_s[:, b * 128:(b + 1) * 128],
                            start=True, stop=True),
                            reads=[b_kh, b_qh], writes=[PSB[pa]])
                    S.op("dve", lambda e, nb=nb, pa=pa: e.tensor_scalar(
                        attf_s[:, 0:nb * 128], PS[pa][:, 0:nb * 128], -3.0e38, 3.0e38, ALU.max, ALU.min),
                        reads=[PSB[pa]], writes=[b_attf])
                    S.op("dve", lambda e, nb=nb, au=au, d=d: e.tensor_tensor(
                        att_s[au][:, 0:nb * 128], attf_s[:, 0:nb * 128], msk_s[d][:, 0:nb * 128], ALU.mult),
                        reads=[b_attf, b_mask], writes=[b_att[au]])
                    for b in range(q0, q1):
                        o0 = (b - q0) * 128
                        halves = [hf for hf in range(2) if 2 * b + hf != first_c]
                        S.op("pe", lambda e, b=b, o0=o0, po=po, au=au, halves=halves: e.matmul(
                            PS[po][:, o0:o0 + 128], v3[:, b, :], att_s[au][:, o0:o0 + 128], start=True, stop=(len(halves) == 0)),
                            reads=[bV, b_att[au]], writes=[PSB[po]])
                        for half in halves:
                            c = 2 * b + half
                            S.op("pe", lambda e, c=c, o0=o0, half=half, po=po, lasth=(half == halves[-1]): e.matmul(
                                PS[po][:, o0 + half * 64:o0 + half * 64 + 64], sb3[:, c, :], qt_s[:, c * 64:(c + 1) * 64],
                                start=False, stop=lasth),
                                reads=[b_sbef, b_qt], writes=[PSB[po]])
                    tt0 = q0 * 128
                    if d == 0:
                        S.op("act", lambda e, tt0=tt0, nb=nb, po=po: e.activation(
                            out=oacc[:, tt0:tt0 + nb * 128], in_=PS[po][:, 0:nb * 128], func=AF.Copy),
                            reads=[PSB[po]], writes=[b_oacc])
                    else:
                        S.op("dve", lambda e, tt0=tt0, nb=nb, po=po: e.tensor_tensor(
                            oacc[:, tt0:tt0 + nb * 128], oacc[:, tt0:tt0 + nb * 128], PS[po][:, 0:nb * 128], ALU.add),
                            reads=[PSB[po]], writes=[b_oacc])

            def finish_head(hh):
                gnorm = hgn_s[:, 0:1]
                for (t0, W, isc) in TILES:
                    oc = octr[0] % 2
                    octr[0] += 1
                    S.op("act", lambda e, t0=t0, W=W: e.activation(out=osq_s[:, 0:W], in_=oacc[:, t0:t0 + W], func=AF.Square),
                         reads=[b_oacc], writes=[b_osq])
                    S.op("pe", lambda e, W=W: e.matmul(PS[6][:, 0:W], ones_bf, osq_s[:, 0:W], start=True, stop=True),
                         reads=[b_osq, b_c2], writes=[PSB[6]])
                    S.op("act", lambda e, W=W: e.activation(out=rs_s[:, 0:W], in_=PS[6][:, 0:W], func=AF.Sqrt, scale=1.0 / 128.0, bias=eps_t),
                         reads=[PSB[6], b_c2], writes=[b_rs])
                    S.op("dve", lambda e, W=W: e.reciprocal(rs_s[:, 0:W], rs_s[:, 0:W]), reads=[b_rs], writes=[b_rs])
                    S.op("dve", lambda e, t0=t0, W=W: e.scalar_tensor_tensor(
                        rs_s[:, 0:W], oacc[:, t0:t0 + W], gnorm, rs_s[:, 0:W], ALU.mult, ALU.mult),
                        reads=[b_oacc, b_hg], writes=[b_rs])
                    S.op("dve", lambda e, t0=t0, W=W, oc=oc: e.tensor_tensor(on_s[oc][:, 0:W], rs_s[:, 0:W], gs_s[:, t0:t0 + W], ALU.mult),
                         reads=[b_rs, bG], writes=[b_on[oc]])
                    S.dma("pool", sem_o[oc], aT[hh * 128:(hh + 1) * 128, t0:t0 + W], on_s[oc][:, 0:W], reads=[b_on[oc]])

            for hh in range(8):
                S.dma("sp", semQ, bufQ, hq_T[hh * 128:(hh + 1) * 128, :], writes=[bQ])
                vd = v_s.rearrange("p (c e) -> p c e", c=NBL)
                for c0 in range(0, NBL, 8):
                    c1 = min(NBL, c0 + 8)
                    S.dma("sp", semV, vd[:, c0:c1, :], v_v[:, c0:c1, hh * 128:(hh + 1) * 128], writes=[bV])
                S.dma("sp", semG, gs_s, qkT[hh * 128:(hh + 1) * 128, :], writes=[bG])
                direction(hh, 0)
                direction(hh, 1)
                finish_head(hh)
            S.barrier()

        b_qkT, b_vtok, b_aT = Buf("qkT"), Buf("vtok"), Buf("aT")

        def mixer(i):
            kind, j = i % 3, i // 3
            pump_until(*mixer_names(i))
            if kind == 0:
                attn_qkv_phase(i, 0, j)
                attn_core_phase(i, 0, j)
                mixer_out_phase(i, da_o_b[j], conv_bufs["dao%d" % j])
            elif kind == 2:
                attn_qkv_phase(i, 2, j)
                attn_core_phase(i, 2, j)
                mixer_out_phase(i, gqa_o_b[j], conv_bufs["gqao%d" % j])
            else:
                hgrn_proj_phase(i, j)
                if stop == "hp":
                    return
                hgrn_scan_phase(i, j)
                if stop == "hs":
                    return
                mixer_out_phase(i, hg_o_b[j], conv_bufs["hgo%d" % j])

        done = False
        start_layer_mixer = 0
        for i in range(start_layer, DEPTH):
            ffn_phase(i, 0)
            if stop == "f%d0" % i:
                dump_and_finish()
                done = True
                break
            if i >= start_layer_mixer:
                mixer(i)
            if stop == "m%d" % i or stop in ("hp", "hs"):
                dump_and_finish()
                done = True
                break
            ffn_phase(i, 1, last=(i == DEPTH - 1))
            if stop == "f%d1" % i and i < DEPTH - 1:
                dump_and_finish()
                done = True
                break
        S.barrier()
        S.emit()
    return nc


def _feat_major(v):
    v = np.asarray(v, dtype=np.float32)
    lead = v.shape[:-1]
    r = v.reshape(lead + (8, 128))
    r = np.moveaxis(r, -1, 0)
    return np.ascontiguousarray(r)


def _consts():
    ident = np.eye(128, dtype=np.float32)
    rot = np.zeros((128, 128), np.float32)
    for j in range(64):
        rot[2 * j + 1, 2 * j] = -1.0
        rot[2 * j, 2 * j + 1] = 1.0
    tril = np.zeros((128, 128), np.float32)
    return np.ascontiguousarray(np.concatenate([ident, rot, tril], axis=1))


def _rope_table(head_dim):
    pairs = head_dim // 4
    inv_freq = np.power(np.float32(10000.0), -np.arange(pairs, dtype=np.float32) / np.float32(pairs)).astype(np.float32)
    rows = TL // GRID_W
    r = np.repeat(np.arange(rows, dtype=np.float32), GRID_W)
    col = np.tile(np.arange(GRID_W, dtype=np.float32), rows)
    ang = np.concatenate([r[:, None] * inv_freq, col[:, None] * inv_freq], axis=-1).astype(np.float32)
    p = np.arange(128)
    a = (p % head_dim) // 2
    tab = ang[:, a].T
    return np.ascontiguousarray(np.stack([np.cos(tab), np.sin(tab)], axis=0).astype(np.float32))


def _hmask():
    m = np.zeros((2, 128, 128), np.float32)
    for blk in range(2):
        for s_ in range(64):
            for t_ in range(64):
                if s_ <= t_:
                    m[0, blk * 64 + s_, blk * 64 + t_] = 1.0
                if s_ >= t_:
                    m[1, blk * 64 + s_, blk * 64 + t_] = 1.0
    return np.ascontiguousarray(np.concatenate([m[0], m[1]], axis=1))


_PROG = {}


def prepare_inputs(inputs):
    x = np.asarray(inputs["x"], np.float32)
    ctx = np.asarray(inputs["ctx"], np.float32)
    c = np.asarray(inputs["c"], np.float32)
    c_ctx = np.asarray(inputs["c_ctx"], np.float32)
    B = x.shape[0]
    shared = {
        "w_mod": np.ascontiguousarray(inputs["w_mod"], dtype=np.float32),
        "bmodT": _feat_major(np.asarray(inputs["b_mod"]).reshape(DEPTH, 9, D)).reshape(128, DEPTH * 72),
        "normT": _feat_major(np.asarray(inputs["norm_g"])).reshape(128, DEPTH * 48),
        "ffn_w_in": np.ascontiguousarray(inputs["ffn_w_in"], dtype=np.float32),
        "ffn_w_out": np.ascontiguousarray(inputs["ffn_w_out"], dtype=np.float32),
        "consts": _consts(),
        "da_w_qkv": np.ascontiguousarray(inputs["da_w_qkv"], dtype=np.float32),
        "da_w_o": np.ascontiguousarray(inputs["da_w_o"], dtype=np.float32),
        "dal": np.ascontiguousarray(np.broadcast_to(np.asarray(inputs["da_lambda"], np.float32).reshape(1, 512), (128, 512))),
        "dasub": np.ascontiguousarray(np.asarray(inputs["da_subln"], np.float32).T),
        "ropeD": _rope_table(64),
        "gqa_w_qkv": np.ascontiguousarray(inputs["gqa_w_qkv"], dtype=np.float32),
        "gqa_w_o": np.ascontiguousarray(inputs["gqa_w_o"], dtype=np.float32),
        "gqn": np.ascontiguousarray(np.stack([np.asarray(inputs["gqa_q_norm"], np.float32)[0],
                                              np.asarray(inputs["gqa_k_norm"], np.float32)[0]], axis=1)),
        "ropeG": _rope_table(128),
        "hg_w_in": np.ascontiguousarray(inputs["hg_w_in"], dtype=np.float32),
        "hg_w_o": np.ascontiguousarray(inputs["hg_w_o"], dtype=np.float32),
        "hlbT": _feat_major(np.asarray(inputs["hg_lower_bound"], np.float32)).reshape(128, 64),
        "hgn": np.ascontiguousarray(np.asarray(inputs["hg_norm"], np.float32).reshape(128, 1)),
        "hmask": _hmask(),
    }
    in_maps = []
    for b in range(B):
        xT = np.ascontiguousarray(np.concatenate([x[b].T, ctx[b].T], axis=1))
        cc = np.stack([c[b], c_ctx], axis=0)
        ccT = _feat_major(cc)
        ccT = np.ascontiguousarray(np.transpose(ccT, (0, 2, 1))).reshape(128, 16)
        m = dict(shared)
        m["xT"] = xT
        m["ccT"] = ccT
        in_maps.append(m)
    return in_maps


def kernel(**inputs):
    stop = inputs.pop("_stop", None)
    cores = inputs.pop("_cores", None)
    start = inputs.pop("_start", 0)
    bsel = inputs.pop("_batch", None)
    in_maps = prepare_inputs(inputs)
    if bsel is not None:
        in_maps = [in_maps[bsel]]
    if cores is not None:
        in_maps = in_maps[:cores]
    key = (stop, start)
    if key not in _PROG:
        _PROG[key] = build_program(stop, start)
    nc = _PROG[key]
    res = run_bass_kernel_spmd(nc, in_maps, core_ids=list(range(len(in_maps))))
    out = np.stack([np.ascontiguousarray(np.asarray(r["yT"]).T) for r in res.results], axis=0)
    return out.astype(np.float32)
```
